# Optimizing a Trainium2 kernel written in Bass

```python
import math
import jax
import jax.numpy as jnp
from jax import lax
import numpy as np

D_MODEL = 1024
BATCH = 8
SEQ = 2048
DEPTH = 4

GRID_W = 64
CTX_LEN = 256
A_HEADS = 4
A_DQK = 64
A_DV = 128
B_HEADS = 8
B_KV_HEADS = 2
B_DH = 64
C_HEADS = 4
C_DK = 128
C_DV = 128
C_CONV = 3
BRANCH_W = A_HEADS * A_DV
D_FF = 2816
FFN_CONV = 3
CHUNK = 64
Q_BLOCK = 128
ROPE_BASE = 10000.0
EPS = 1e-6
M_INIT = -1e30
IN_SPLIT = (A_HEADS * A_DQK, A_HEADS * A_DQK, A_HEADS * A_DV, A_HEADS * A_DV, 4 * A_HEADS,
            B_HEADS * B_DH, B_KV_HEADS * B_DH, B_KV_HEADS * B_DH,
            C_HEADS * C_DK, C_HEADS * C_DK, C_HEADS * C_DV, C_HEADS * C_DV, 2 * C_HEADS, 2 * C_HEADS,
            3 * D_MODEL)
IN_COLS = sum(IN_SPLIT)

kernel_name = 'hybrid_mlstm_gqa_gdn_dit'


def rmsnorm(x, w):
    xf = x.astype(jnp.float32)
    y = xf * lax.rsqrt(jnp.mean(xf * xf, axis=-1, keepdims=True) + EPS)
    return (y * w.astype(jnp.float32)).astype(x.dtype)


def l2norm(x):
    return x * lax.rsqrt(jnp.sum(x * x, axis=-1, keepdims=True) + EPS)


def modulate(x, shift, scale):
    return x * (1 + scale) + shift


def to_heads(u, n):
    b, t, _ = u.shape
    return u.reshape(b, t, n, -1).transpose(0, 2, 1, 3)


def from_heads(u):
    b, n, t, d = u.shape
    return u.transpose(0, 2, 1, 3).reshape(b, t, n * d)


def flip_t(u):
    return jnp.flip(u, axis=2)


def split_in(u):
    idx = np.cumsum(IN_SPLIT)[:-1].tolist()
    return jnp.split(u, idx, axis=-1)


def dwconv(u, w):
    k = w.shape[0]
    p = k // 2
    t = u.shape[1]
    up = jnp.pad(u, ((0, 0), (p, p), (0, 0)))
    out = up[:, 0:t] * w[0]
    for j in range(1, k):
        out = out + up[:, j:j + t] * w[j]
    return out


def axial_rope_tables(n_tok):
    rows = n_tok // GRID_W
    row = jnp.repeat(jnp.arange(rows, dtype=jnp.float32), GRID_W)
    col = jnp.tile(jnp.arange(GRID_W, dtype=jnp.float32), rows)
    n_freq = B_DH // 4
    inv = ROPE_BASE ** (-jnp.arange(n_freq, dtype=jnp.float32) / n_freq)
    ang = jnp.stack([row[:, None] * inv, col[:, None] * inv], axis=1)
    return jnp.cos(ang), jnp.sin(ang)


def apply_rope(x, cos, sin):
    b, h, t, d = x.shape
    xr = x.astype(jnp.float32).reshape(b, h, t, 2, 2, d // 4)
    x1, x2 = xr[..., 0, :], xr[..., 1, :]
    out = jnp.stack([x1 * cos - x2 * sin, x2 * cos + x1 * sin], axis=-2)
    return out.reshape(b, h, t, d).astype(x.dtype)


def blocked_attention(q, k, v):
    b, hq, t, d = q.shape
    hkv = k.shape[1]
    g = hq // hkv
    nb = t // Q_BLOCK
    qb = q.reshape(b, hkv, g, nb, Q_BLOCK, d).transpose(3, 0, 1, 2, 4, 5).astype(jnp.float32) * (d ** -0.5)
    kf = k.astype(jnp.float32)
    vf = v.astype(jnp.float32)

    def one_block(qi):
        s = jnp.einsum('bkgqd,bktd->bkgqt', qi, kf)
        p = jax.nn.softmax(s, axis=-1)
        return jnp.einsum('bkgqt,bktd->bkgqd', p, vf)

    o = lax.map(one_block, qb)
    return o.transpose(1, 2, 3, 0, 4, 5).reshape(b, hq, t, d).astype(q.dtype)


def mlstm_scan(q, k, v, logi, logf, state):
    b, h, t, dk = q.shape
    dv = v.shape[-1]
    nc = t // CHUNK
    tri = jnp.tril(jnp.ones((CHUNK, CHUNK), bool))

    def chunks(u):
        return jnp.moveaxis(u.reshape(b, h, nc, CHUNK, *u.shape[3:]), 2, 0)

    def step(carry, inp):
        C, n, m = carry
        qc, kc, vc, ic, fc = inp
        bcum = jnp.cumsum(fc, axis=-1)
        dlog = jnp.where(tri, bcum[..., :, None] - bcum[..., None, :] + ic[..., None, :], -jnp.inf)
        inter = bcum + m[..., None]
        mt = jnp.maximum(inter, jnp.max(dlog, axis=-1))
        s = jnp.einsum('bhtd,bhsd->bhts', qc, kc) * jnp.exp(dlog - mt[..., None])
        e_inter = jnp.exp(inter - mt)
        num = jnp.einsum('bhts,bhse->bhte', s, vc) + e_inter[..., None] * jnp.einsum('bhtd,bhde->bhte', qc, C)
        den = jnp.sum(s, axis=-1) + e_inter * jnp.einsum('bhtd,bhd->bht', qc, n)
        hc = num / jnp.maximum(jnp.abs(den), jnp.exp(-mt))[..., None]
        btot = bcum[..., -1]
        glog = btot[..., None] - bcum + ic
        m_new = jnp.maximum(btot + m, jnp.max(glog, axis=-1))
        wk = jnp.exp(glog - m_new[..., None])
        decay = jnp.exp(btot + m - m_new)
        C_new = decay[..., None, None] * C + jnp.einsum('bhs,bhsd,bhse->bhde', wk, kc, vc)
        n_new = decay[..., None] * n + jnp.einsum('bhs,bhsd->bhd', wk, kc)
        return (C_new, n_new, m_new), hc

    carry, hs = lax.scan(step, state, (chunks(q), chunks(k), chunks(v), chunks(logi), chunks(logf)))
    return jnp.moveaxis(hs, 0, 2).reshape(b, h, t, dv), carry


def gdn_scan(q, k, v, g, beta, S0):
    b, h, t, dk = q.shape
    dv = v.shape[-1]
    nc = t // CHUNK

    def ch(u):
        return u.reshape(b, h, nc, CHUNK, *u.shape[3:])

    q, k, v, g, beta = ch(q), ch(k), ch(v), ch(g), ch(beta)
    tri = jnp.tril(jnp.ones((CHUNK, CHUNK), bool))
    stri = jnp.tril(jnp.ones((CHUNK, CHUNK), bool), -1)
    G = jnp.cumsum(g, axis=-1)
    decay = jnp.exp(jnp.where(tri, G[..., :, None] - G[..., None, :], -jnp.inf))
    kb = k * beta[..., None]
    A = jnp.where(stri, jnp.einsum('bhntd,bhnsd->bhnts', kb, k) * decay, 0.0)
    eye = jnp.eye(CHUNK, dtype=A.dtype)
    T = lax.linalg.triangular_solve(A + eye, jnp.broadcast_to(eye, A.shape),
                                    left_side=True, lower=True, unit_diagonal=True)
    u = T @ (v * beta[..., None])
    w = T @ (kb * jnp.exp(G)[..., None])
    qk = jnp.where(tri, jnp.einsum('bhntd,bhnsd->bhnts', q, k) * decay, 0.0)
    qg = q * jnp.exp(G)[..., None]
    g_last = G[..., -1]
    kg = k * jnp.exp(g_last[..., None] - G)[..., None]

    def step(S, inp):
        u_c, w_c, qk_c, qg_c, kg_c, gl_c = inp
        v_new = u_c - w_c @ S
        o = qg_c @ S + qk_c @ v_new
        S = S * jnp.exp(gl_c)[..., None, None] + jnp.einsum('bhsd,bhse->bhde', kg_c, v_new)
        return S, o

    xs = (jnp.moveaxis(u, 2, 0), jnp.moveaxis(w, 2, 0), jnp.moveaxis(qk, 2, 0),
          jnp.moveaxis(qg, 2, 0), jnp.moveaxis(kg, 2, 0), jnp.moveaxis(g_last, 2, 0))
    S, o = lax.scan(step, S0, xs)
    return jnp.moveaxis(o, 0, 2).reshape(b, h, t, dv), S


def mlstm_branch(pc, pl, gate_b, norm_w, need_ctx):
    def prep(p):
        q, k, v, o, g = p
        b, t, _ = q.shape
        q = to_heads(q, A_HEADS).astype(jnp.float32) * (A_DQK ** -0.5)
        k = to_heads(k, A_HEADS).astype(jnp.float32)
        v = to_heads(v, A_HEADS).astype(jnp.float32)
        g = (g.astype(jnp.float32) + gate_b.reshape(-1).astype(jnp.float32)).reshape(b, t, 4, A_HEADS).transpose(2, 0, 3, 1)
        return q, k, v, g, o

    qc, kc, vc, gc, oc = prep(pc)
    ql, kl, vl, gl, ol = prep(pl)
    b = ql.shape[0]
    init = (jnp.zeros((b, A_HEADS, A_DQK, A_DV), jnp.float32),
            jnp.zeros((b, A_HEADS, A_DQK), jnp.float32),
            jnp.full((b, A_HEADS), M_INIT, jnp.float32))
    lsig = jax.nn.log_sigmoid
    hcf, st = mlstm_scan(qc, kc, vc, gc[0], lsig(gc[1]), init)
    hlf, _ = mlstm_scan(ql, kl, vl, gl[0], lsig(gl[1]), st)
    hcb, st = mlstm_scan(flip_t(qc), flip_t(kc), flip_t(vc), flip_t(gc[2]), lsig(flip_t(gc[3])), init)
    hlb, _ = mlstm_scan(flip_t(ql), flip_t(kl), flip_t(vl), flip_t(gl[2]), lsig(flip_t(gl[3])), st)

    def post(hsum, o):
        hn = hsum * lax.rsqrt(jnp.mean(hsum * hsum, axis=-1, keepdims=True) + EPS)
        return (from_heads(hn) * norm_w.astype(jnp.float32) * jax.nn.sigmoid(o.astype(jnp.float32))).astype(o.dtype)

    y_lat = post(hlf + flip_t(hlb), ol)
    y_ctx = post(hcf + flip_t(hcb), oc) if need_ctx else None
    return y_ctx, y_lat


def gqa_branch(pc, pl, qn_w, kn_w, cos, sin, need_ctx):
    def prep(p):
        q, k, v = p
        return (rmsnorm(to_heads(q, B_HEADS), qn_w), rmsnorm(to_heads(k, B_KV_HEADS), kn_w),
                to_heads(v, B_KV_HEADS))

    qc, kc, vc = prep(pc)
    ql, kl, vl = prep(pl)
    ql = apply_rope(ql, cos, sin)
    kl = apply_rope(kl, cos, sin)
    k_all = jnp.concatenate([kc, kl], axis=2)
    v_all = jnp.concatenate([vc, vl], axis=2)
    y_lat = from_heads(blocked_attention(ql, k_all, v_all))
    y_ctx = from_heads(blocked_attention(qc, kc, vc)) if need_ctx else None
    return y_ctx, y_lat


def gdn_branch(pc, pl, conv_w, a_log, dt_bias, norm_w, need_ctx):
    def prep(p):
        q, k, v, z, a, beta = p
        b, t, _ = q.shape
        qkv = jax.nn.silu(dwconv(jnp.concatenate([q, k, v], axis=-1), conv_w))
        q, k, v = jnp.split(qkv, [C_HEADS * C_DK, 2 * C_HEADS * C_DK], axis=-1)
        q = l2norm(to_heads(q, C_HEADS).astype(jnp.float32)) * (C_DK ** -0.5)
        k = l2norm(to_heads(k, C_HEADS).astype(jnp.float32))
        v = to_heads(v, C_HEADS).astype(jnp.float32)
        a = a.astype(jnp.float32).reshape(b, t, 2, C_HEADS) + dt_bias.astype(jnp.float32)
        g = (-jnp.exp(a_log.astype(jnp.float32)) * jax.nn.softplus(a)).transpose(2, 0, 3, 1)
        beta = jax.nn.sigmoid(beta.astype(jnp.float32).reshape(b, t, 2, C_HEADS)).transpose(2, 0, 3, 1)
        return q, k, v, g, beta, z

    qc, kc, vc, gc, bc, zc = prep(pc)
    ql, kl, vl, gl, bl, zl = prep(pl)
    b = ql.shape[0]
    s0 = jnp.zeros((b, C_HEADS, C_DK, C_DV), jnp.float32)
    ocf, st = gdn_scan(qc, kc, vc, gc[0], bc[0], s0)
    olf, _ = gdn_scan(ql, kl, vl, gl[0], bl[0], st)
    ocb, st = gdn_scan(flip_t(qc), flip_t(kc), flip_t(vc), flip_t(gc[1]), flip_t(bc[1]), s0)
    olb, _ = gdn_scan(flip_t(ql), flip_t(kl), flip_t(vl), flip_t(gl[1]), flip_t(bl[1]), st)

    def post(osum, z):
        on = osum * lax.rsqrt(jnp.mean(osum * osum, axis=-1, keepdims=True) + EPS) * norm_w.astype(jnp.float32)
        return (from_heads(on) * jax.nn.silu(z.astype(jnp.float32))).astype(z.dtype)

    y_lat = post(olf + flip_t(olb), zl)
    y_ctx = post(ocf + flip_t(ocb), zc) if need_ctx else None
    return y_ctx, y_lat


def merge_branches(ya, yb, yc, gate_pre, wb, wo):
    ga, gb, gc = jnp.split(jax.nn.sigmoid(gate_pre), 3, axis=-1)
    y = ga * (ya @ wb[0]) + gb * (yb @ wb[1]) + gc * (yc @ wb[2])
    return y @ wo


def conv_ffn(xn, w_up, conv_w, w_down):
    u = dwconv(xn @ w_up, conv_w)
    a, gt = jnp.split(u, 2, axis=-1)
    return (a * jax.nn.silu(gt)) @ w_down


def setup_inputs(seed: int = 0) -> dict:
    key = jax.random.key(seed)
    ks = jax.random.split(key, 24)

    def nrm(k, shape, s):
        return jax.random.normal(k, shape, jnp.float32) * s

    dt = jnp.exp(jax.random.uniform(ks[15], (DEPTH, 2, C_HEADS), jnp.float32,
                                    minval=math.log(1e-3), maxval=math.log(1e-1)))
    return {
        'x': nrm(ks[0], (BATCH, SEQ, D_MODEL), 1.0),
        'c': nrm(ks[1], (BATCH, D_MODEL), 1.0),
        'ctx': nrm(ks[2], (BATCH, CTX_LEN, D_MODEL), 1.0),
        'c_ctx': nrm(ks[3], (D_MODEL,), 1.0),
        'norm1_w': 1.0 + nrm(ks[4], (DEPTH, D_MODEL), 0.05),
        'norm2_w': 1.0 + nrm(ks[5], (DEPTH, D_MODEL), 0.05),
        'ada_w': nrm(ks[6], (DEPTH, D_MODEL, 6 * D_MODEL), 0.5 * D_MODEL ** -0.5),
        'ada_b': nrm(ks[7], (DEPTH, 6 * D_MODEL), 0.02),
        'w_in': nrm(ks[8], (DEPTH, D_MODEL, IN_COLS), D_MODEL ** -0.5),
        'a_gate_b': jnp.array([0.0, 3.0, 0.0, 3.0], jnp.float32)[None, :, None] + nrm(ks[9], (DEPTH, 4, A_HEADS), 0.1),
        'a_norm_w': 1.0 + nrm(ks[10], (DEPTH, A_HEADS * A_DV), 0.05),
        'b_qnorm_w': 1.0 + nrm(ks[11], (DEPTH, B_DH), 0.05),
        'b_knorm_w': 1.0 + nrm(ks[12], (DEPTH, B_DH), 0.05),
        'c_conv_w': nrm(ks[13], (DEPTH, C_CONV, 2 * C_HEADS * C_DK + C_HEADS * C_DV), C_CONV ** -0.5),
        'c_a_log': jnp.log(jax.random.uniform(ks[14], (DEPTH, 2, C_HEADS), jnp.float32, minval=1.0, maxval=16.0)),
        'c_dt_bias': dt + jnp.log(-jnp.expm1(-dt)),
        'c_norm_w': 1.0 + nrm(ks[16], (DEPTH, C_DV), 0.05),
        'w_branch': nrm(ks[17], (DEPTH, 3, BRANCH_W, D_MODEL), BRANCH_W ** -0.5),
        'w_out': nrm(ks[18], (DEPTH, D_MODEL, D_MODEL), D_MODEL ** -0.5),
        'w_up': nrm(ks[19], (DEPTH, D_MODEL, 2 * D_FF), D_MODEL ** -0.5),
        'ffn_conv_w': nrm(ks[20], (DEPTH, FFN_CONV, 2 * D_FF), FFN_CONV ** -0.5),
        'w_down': nrm(ks[21], (DEPTH, D_FF, D_MODEL), D_FF ** -0.5),
    }


def reference(x, c, ctx, c_ctx, norm1_w, norm2_w, ada_w, ada_b, w_in, a_gate_b, a_norm_w,
              b_qnorm_w, b_knorm_w, c_conv_w, c_a_log, c_dt_bias, c_norm_w, w_branch, w_out,
              w_up, ffn_conv_w, w_down):
    cos, sin = axial_rope_tables(x.shape[1])
    silu_c = jax.nn.silu(c)
    silu_cc = jax.nn.silu(c_ctx)
    h_lat, h_ctx = x, ctx
    for l in range(DEPTH):
        need_ctx = l < DEPTH - 1
        m_lat = jnp.split((silu_c @ ada_w[l] + ada_b[l])[:, None, :], 6, axis=-1)
        m_ctx = jnp.split((silu_cc @ ada_w[l] + ada_b[l])[None, None, :], 6, axis=-1)
        xn_lat = modulate(rmsnorm(h_lat, norm1_w[l]), m_lat[0], m_lat[1])
        xn_ctx = modulate(rmsnorm(h_ctx, norm1_w[l]), m_ctx[0], m_ctx[1])
        p_lat = split_in(xn_lat @ w_in[l])
        p_ctx = split_in(xn_ctx @ w_in[l])
        ya_ctx, ya_lat = mlstm_branch(p_ctx[0:5], p_lat[0:5], a_gate_b[l], a_norm_w[l], need_ctx)
        yb_ctx, yb_lat = gqa_branch(p_ctx[5:8], p_lat[5:8], b_qnorm_w[l], b_knorm_w[l], cos, sin, need_ctx)
        yc_ctx, yc_lat = gdn_branch(p_ctx[8:14], p_lat[8:14], c_conv_w[l], c_a_log[l], c_dt_bias[l],
                                    c_norm_w[l], need_ctx)
        h_lat = h_lat + m_lat[2] * merge_branches(ya_lat, yb_lat, yc_lat, p_lat[14], w_branch[l], w_out[l])
        h_lat = h_lat + m_lat[5] * conv_ffn(modulate(rmsnorm(h_lat, norm2_w[l]), m_lat[3], m_lat[4]),
                                            w_up[l], ffn_conv_w[l], w_down[l])
        if need_ctx:
            h_ctx = h_ctx + m_ctx[2] * merge_branches(ya_ctx, yb_ctx, yc_ctx, p_ctx[14], w_branch[l], w_out[l])
            h_ctx = h_ctx + m_ctx[5] * conv_ffn(modulate(rmsnorm(h_ctx, norm2_w[l]), m_ctx[3], m_ctx[4]),
                                                w_up[l], ffn_conv_w[l], w_down[l])
    return h_lat
```

```python
import math
from contextlib import ExitStack

import numpy as np
import concourse.bass as bass
import concourse.mybir as mybir
from concourse.bass_utils import run_bass_kernel_spmd

F32 = mybir.dt.float32
BF16 = mybir.dt.bfloat16
AF = mybir.ActivationFunctionType
ALU = mybir.AluOpType
AX = mybir.AxisListType

D = 1024
T = 2304
NT = 18
NCH = 36
TCTX = 256
DEPTH = 4
IN_COLS = 7456
DFF = 2816
EPS = 1e-6
KC = 8

ENGS = ["pe", "act", "dve", "pool", "sp"]
EPOCH = 30000
NDS = 12


class Sched:
    def __init__(self, nc, es):
        self.nc = nc
        self.es = es
        self.ops = {e: [] for e in ENGS}
        self.count = {e: 0 for e in ENGS}
        self.esems = {e: [] for e in ENGS}
        self.waited = {e: {} for e in ENGS}
        self.last_w = {}
        self.readers = {}
        self.dsems = {}
        self.dval = {}
        self.dnext = {}
        self.sem_objs = {}
        self.nsem = 0
        self.latest = {}
        self.total = {e: 0 for e in ENGS}
        for q in ("sp", "pool", "act"):
            self.dsems[q] = [self._newsem(f"d{q}{i}") for i in range(NDS)]
            self.dval[q] = [0] * NDS
            self.dnext[q] = 0

    def _newsem(self, name):
        s = self.es.enter_context(self.nc.semaphore(name))
        self.nsem += 1
        self.sem_objs[self.nsem] = s
        return self.nsem

    def _deps(self, engine, reads, writes):
        deps = {}

        def add(ev):
            if ev is None:
                return
            s, v = ev
            if deps.get(s, 0) < v:
                deps[s] = v
        for k in reads:
            add(self.last_w.get(k))
        for k in writes:
            add(self.last_w.get(k))
            for ev in self.readers.get(k, ()):
                add(ev)
        waits = []
        own = set(self.esems[engine]) if engine == "pe" else ()
        for s, v in deps.items():
            if s in own:
                continue
            if self.waited[engine].get(s, 0) >= v:
                continue
            self.waited[engine][s] = v
            waits.append((s, v))
        return waits

    def _commit(self, ev, reads, writes):
        for k in writes:
            self.last_w[k] = ev
            self.readers[k] = []
        for k in reads:
            self.readers.setdefault(k, []).append(ev)
        self.latest[ev[0]] = ev[1]

    def op(self, engine, fn, reads=(), writes=()):
        psr = [x for x in reads if x.startswith("ps")]
        if psr:
            writes = list(writes) + psr
        waits = self._deps(engine, reads, writes)
        self.count[engine] += 1
        c = self.count[engine]
        ep = (c - 1) // EPOCH
        while len(self.esems[engine]) <= ep:
            self.esems[engine].append(self._newsem(f"e{engine}{len(self.esems[engine])}"))
        sem = self.esems[engine][ep]
        ev = (sem, c - ep * EPOCH)
        self.ops[engine].append((fn, waits, (sem, 1)))
        self._commit(ev, reads, writes)
        return ev

    def dma(self, fn, reads=(), writes=(), q="sp"):
        i = self.dnext[q]
        self.dnext[q] = (i + 1) % NDS
        sem = self.dsems[q][i]
        prev = self.dval[q][i]
        waits = self._deps(q, reads, writes)
        if prev > 0 and self.waited[q].get(sem, 0) < prev:
            self.waited[q][sem] = prev
            waits.append((sem, prev))
        self.dval[q][i] = prev + 16
        ev = (sem, prev + 16)
        self.ops[q].append((fn, waits, (sem, 16)))
        self._commit(ev, reads, writes)
        return ev

    def barrier(self):
        for e in ENGS:
            waits = []
            for s, v in self.latest.items():
                if self.waited[e].get(s, 0) >= v:
                    continue
                self.waited[e][s] = v
                waits.append((s, v))
            if waits:
                self.ops[e].append((None, waits, None))
        self.last_w = {}
        self.readers = {}

    def emit(self):
        nc = self.nc
        so = self.sem_objs
        ops = self.ops
        with nc.Block() as block:
            def run(e, lst):
                for fn, waits, inc in lst:
                    for s, v in waits:
                        e.wait_ge(so[s], v)
                    if fn is None:
                        continue
                    ins = fn(e)
                    if inc is not None:
                        ins.then_inc(so[inc[0]], inc[1])

            @block.tensor
            def _(e):
                run(e, ops["pe"])

            @block.scalar
            def _(e):
                run(e, ops["act"])

            @block.vector
            def _(e):
                run(e, ops["dve"])

            @block.gpsimd
            def _(e):
                run(e, ops["pool"])

            @block.sync
            def _(e):
                run(e, ops["sp"])
        for e in ENGS:
            self.total[e] += len(ops[e])
        self.ops = {e: [] for e in ENGS}


class Arena:
    def __init__(self, big, words):
        self.big = big
        self.words = words
        self.top = 0
        self.base = 0

    def reset(self):
        self.top = self.base

    def freeze(self):
        self.base = self.top

    def mark(self):
        return self.top

    def release(self, m):
        self.top = m

    def alloc(self, shape, dtype, parts=128):
        shape = list(shape)
        n = int(np.prod(shape))
        w = (n + 1) // 2 if dtype == BF16 else n
        a = self.top
        self.top += w
        self.peak = max(getattr(self, "peak", 0), self.top)
        assert self.top <= self.words, f"SBUF arena overflow {self.top} > {self.words}"
        ap = self.big[0:parts, a:a + w]
        if dtype == BF16:
            ap = ap.bitcast(BF16)[:, 0:n]
        if len(shape) == 2:
            ap = ap.rearrange("p (a b) -> p a b", a=shape[0])
        elif len(shape) == 3:
            ap = ap.rearrange("p (a b c) -> p a b c", a=shape[0], b=shape[1])
        return ap


TBLK = [(0, 256), (256, 512), (768, 512), (1280, 512), (1792, 512)]


class K:
    pass


def build_program(dbg=(), nlayers=DEPTH, stop_after=None):
    nc = bass.Bass("TRN2", target_bir_lowering=False)
    k = K()
    k.nc = nc
    inp = {}

    def din(name, shape):
        inp[name] = nc.dram_tensor(name, list(shape), F32, kind="ExternalInput").ap()
        return inp[name]

    din("x", [2048, D]); din("ctx", [TCTX, D]); din("c", [1, D]); din("c_ctx", [1, D])
    din("norm1_w", [DEPTH, D]); din("norm2_w", [DEPTH, D])
    din("ada_w", [DEPTH, D, 6 * D]); din("ada_b", [DEPTH, 6 * D])
    din("w_in", [DEPTH, D, IN_COLS])
    din("a_gate_b", [DEPTH, 16]); din("a_norm_w", [DEPTH, 512])
    din("b_qnorm_w", [DEPTH, 64]); din("b_knorm_w", [DEPTH, 64])
    din("c_conv_w", [DEPTH, 3, 1536]); din("c_a_log", [DEPTH, 8]); din("c_dt_bias", [DEPTH, 8])
    din("c_norm_w", [DEPTH, 128])
    din("w_branch", [DEPTH, 3, 512, D]); din("w_out", [DEPTH, D, D])
    din("w_up", [DEPTH, D, 2 * DFF]); din("ffn_conv_w", [DEPTH, 3, 2 * DFF]); din("w_down", [DEPTH, DFF, D])
    din("k_ident", [128, 128]); din("k_cos", [128, 16 * 32]); din("k_sin", [128, 16 * 32])
    din("k_masks", [128, 6 * 128])
    out = nc.dram_tensor("out", [2048, D], F32, kind="ExternalOutput").ap()

    scr = {}

    def dscr(name, shape, dtype):
        kind = "ExternalOutput" if name in dbg else "Internal"
        scr[name] = nc.dram_tensor("s_" + name, list(shape), dtype, kind=kind).ap()
        return scr[name]

    dscr("hres", [T, D], F32)
    dscr("modrow", [DEPTH, 2, 6 * D], F32)
    dscr("fmAq", [256, T], BF16); dscr("fmAk", [256, T], BF16); dscr("fmAo", [512, T], BF16)
    dscr("gA", [16, T], F32)
    dscr("tmA", [T, 768], BF16); dscr("tmB", [T, 768], BF16)
    dscr("fmC", [1536, T], BF16); dscr("fmCz", [512, T], BF16); dscr("gC", [16, T], F32)
    dscr("fmG", [3072, T], BF16)
    dscr("fmY", [3, 512, T], BF16)
    dscr("hmid", [DFF, T], BF16)
    k.inp, k.scr, k.out = inp, scr, out

    with ExitStack() as es:
        WORDS = 48 * 1024
        big = es.enter_context(nc.sbuf_tensor("big", [128, WORDS], F32))
        k.ps = [es.enter_context(nc.psum_tensor(f"ps{i}", [128, 512], F32)) for i in range(8)]
        S = Sched(nc, es)
        A = Arena(big, WORDS)
        k.S, k.A = S, A
        k.uid = 0
        setup_consts(k)
        phase_init(k)
        S.barrier(); S.emit(); A.reset()
        phase_adaln(k)
        S.barrier(); S.emit(); A.reset()
        done = False
        for l in range(nlayers):
            for ph in (phase_norm_win, phase_attn, phase_mlstm, phase_gdn, phase_merge, phase_ffn):
                ph(k, l)
                S.barrier(); S.emit(); A.reset()
                if stop_after == (l, ph.__name__):
                    done = True
                    break
            if done:
                break
        phase_final(k)
        S.barrier(); S.emit()
        k.totals = dict(S.total)
    return nc, k


def uid(k, s):
    k.uid += 1
    return f"{s}#{k.uid}"


def setup_consts(k):
    S, A, inp = k.S, k.A, k.inp
    k.ident_f = A.alloc([128], F32)
    k.ident_b = A.alloc([128], BF16)
    k.ones_b = A.alloc([128], BF16)
    k.ones_f = A.alloc([128], F32)
    k.eps_t = A.alloc([1], F32)
    k.masks = A.alloc([6, 128], F32)
    S.dma(lambda e: e.dma_start(out=k.ident_f, in_=inp["k_ident"][:, :]), writes=["ident_f"])
    S.dma(lambda e: e.dma_start(out=k.masks, in_=inp["k_masks"].rearrange("p (a b) -> p a b", a=6)), writes=["masks"])
    S.op("dve", lambda e: e.tensor_copy(out=k.ident_b, in_=k.ident_f), reads=["ident_f"], writes=["ident_b"])
    S.op("pool", lambda e: e.memset(k.ones_b, 1.0), writes=["ones_b"])
    S.op("pool", lambda e: e.memset(k.ones_f, 1.0), writes=["ones_f"])
    S.op("pool", lambda e: e.memset(k.eps_t, EPS), writes=["eps_t"])
    A.freeze()


def phase_init(k):
    S, inp, scr = k.S, k.inp, k.scr
    S.dma(lambda e: e.dma_start(out=scr["hres"][0:TCTX, :], in_=inp["ctx"][:, :]), writes=[uid(k, "d")])
    for i in range(4):
        S.dma(lambda e, i=i: e.dma_start(out=scr["hres"][TCTX + i * 512:TCTX + (i + 1) * 512, :],
                                         in_=inp["x"][i * 512:(i + 1) * 512, :]), writes=[uid(k, "d")])


def phase_final(k):
    S, scr = k.S, k.scr
    for i in range(4):
        S.dma(lambda e, i=i: e.dma_start(out=k.out[i * 512:(i + 1) * 512, :],
                                         in_=scr["hres"][TCTX + i * 512:TCTX + (i + 1) * 512, :]), writes=[uid(k, "d")])


def phase_adaln(k):
    S, A, inp, scr, ps = k.S, k.A, k.inp, k.scr, k.ps
    craw = A.alloc([KC, 2], F32)
    sT = A.alloc([KC, 2], F32)
    S.dma(lambda e: e.dma_start(out=craw[:, :, 0], in_=inp["c"].rearrange("o (k p) -> p (o k)", p=128),
                                allow_slow_non_contiguous=True), writes=["craw0"])
    S.dma(lambda e: e.dma_start(out=craw[:, :, 1], in_=inp["c_ctx"].rearrange("o (k p) -> p (o k)", p=128),
                                allow_slow_non_contiguous=True), writes=["craw1"])
    S.op("act", lambda e: e.activation(out=sT, in_=craw, func=AF.Silu), reads=["craw0", "craw1"], writes=["sT"])
    wst = [A.alloc([KC, 512], F32) for _ in range(2)]
    brow = [A.alloc([6 * D], F32, parts=2) for _ in range(2)]
    mrow = [A.alloc([6 * D], F32, parts=2) for _ in range(2)]
    nrow = [A.alloc([2, D], F32, parts=2) for _ in range(2)]
    it = 0
    for l in range(DEPTH):
        b = l % 2
        S.dma(lambda e, l=l, b=b: e.dma_start(out=brow[b], in_=inp["ada_b"][l:l + 1, :].partition_broadcast(2)),
              writes=[f"brow{b}"])
        S.dma(lambda e, l=l, b=b: e.dma_start(out=nrow[b][:, 0, :], in_=inp["norm1_w"][l:l + 1, :].partition_broadcast(2)),
              writes=[f"nrow{b}a"])
        S.dma(lambda e, l=l, b=b: e.dma_start(out=nrow[b][:, 1, :], in_=inp["norm2_w"][l:l + 1, :].partition_broadcast(2)),
              writes=[f"nrow{b}b"])
        for cb in range(12):
            wb = it % 2
            pb = it % 4
            it += 1
            S.dma(lambda e, l=l, cb=cb, wb=wb: e.dma_start(
                out=wst[wb], in_=inp["ada_w"][l].rearrange("(k p) n -> p k n", p=128)[:, :, cb * 512:(cb + 1) * 512]),
                writes=[f"wst{wb}"])
            for kc in range(KC):
                S.op("pe", lambda e, kc=kc, wb=wb, pb=pb: e.matmul(ps[pb][0:2, :], sT[:, kc, :], wst[wb][:, kc, :],
                                                                  start=(kc == 0), stop=(kc == KC - 1)),
                     reads=["sT", f"wst{wb}"], writes=[f"ps{pb}"])
            S.op("dve", lambda e, b=b, cb=cb, pb=pb: e.tensor_tensor(
                out=mrow[b][:, cb * 512:(cb + 1) * 512], in0=ps[pb][0:2, :], in1=brow[b][:, cb * 512:(cb + 1) * 512],
                op=ALU.add), reads=[f"ps{pb}", f"brow{b}"], writes=[f"mrow{b}"])
        for which, off in ((0, 1 * D), (1, 4 * D)):
            S.op("dve", lambda e, b=b, which=which, off=off: e.scalar_tensor_tensor(
                out=mrow[b][:, off:off + D], in0=mrow[b][:, off:off + D], scalar=1.0, in1=nrow[b][:, which, :],
                op0=ALU.add, op1=ALU.mult), reads=[f"mrow{b}", f"nrow{b}a", f"nrow{b}b"], writes=[f"mrow{b}"])
        S.dma(lambda e, l=l, b=b: e.dma_start(out=scr["modrow"][l], in_=mrow[b]), reads=[f"mrow{b}"],
              writes=[uid(k, "d")])


def load_bc(k, dst, src_row, key, q="sp"):
    k.S.dma(lambda e: e.dma_start(out=dst, in_=src_row.partition_broadcast(128)), writes=[key], q=q)


def norm_to_xnT(k, l, shift_off, g_off, xnT):
    S, A, scr, ps = k.S, k.A, k.scr, k.ps
    gbc = [A.alloc([D], F32) for _ in range(2)]
    sbc = [A.alloc([D], F32) for _ in range(2)]
    for r in range(2):
        load_bc(k, gbc[r], scr["modrow"][l, r:r + 1, g_off:g_off + D], f"gbc{r}")
        load_bc(k, sbc[r], scr["modrow"][l, r:r + 1, shift_off:shift_off + D], f"sbc{r}")
    ht = [A.alloc([D], F32) for _ in range(2)]
    junk = A.alloc([D], BF16)
    tmp = [A.alloc([D], F32) for _ in range(2)]
    xs = [A.alloc([D], BF16) for _ in range(2)]
    ss = A.alloc([NT], F32)
    rr = A.alloc([NT], F32)
    rstd = A.alloc([NT], F32)
    S.op("pool", lambda e: e.memset(ss, 0.0), writes=["ss"])
    for j in range(NT):
        b = j % 2
        r = 1 if j < 2 else 0
        pb = j % 2
        S.dma(lambda e, j=j, b=b: e.dma_start(out=ht[b], in_=scr["hres"][j * 128:(j + 1) * 128, :]),
              reads=[f"hres{j}"], writes=[f"ht{b}"])
        S.op("act", lambda e, j=j, b=b: e.activation(out=junk, in_=ht[b], func=AF.Square, accum_out=ss[:, j:j + 1]),
             reads=[f"ht{b}", "ss"], writes=["junk", f"ss{j}"])
        S.op("act", lambda e, j=j: e.activation(out=rr[:, j:j + 1], in_=ss[:, j:j + 1], func=AF.Sqrt,
                                                bias=k.eps_t, scale=1.0 / D),
             reads=[f"ss{j}", "eps_t"], writes=[f"rr{j}"])
        S.op("dve", lambda e, j=j: e.reciprocal(out=rstd[:, j:j + 1], in_=rr[:, j:j + 1]),
             reads=[f"rr{j}"], writes=[f"rstd{j}"])
        S.op("dve", lambda e, j=j, b=b, r=r: e.scalar_tensor_tensor(
            out=tmp[b], in0=ht[b], scalar=rstd[:, j:j + 1], in1=gbc[r], op0=ALU.mult, op1=ALU.mult),
            reads=[f"ht{b}", f"rstd{j}", f"gbc{r}"], writes=[f"tmp{b}"])
        S.op("pool", lambda e, b=b, r=r: e.tensor_tensor(out=xs[b], in0=tmp[b], in1=sbc[r], op=ALU.add),
             reads=[f"tmp{b}", f"sbc{r}"], writes=[f"xs{b}"])
        pst = ps[pb][:, :].bitcast(BF16).rearrange("p (a b) -> p a b", a=KC)
        for kc in range(KC):
            S.op("pe", lambda e, kc=kc, b=b, pst=pst: e.transpose(pst[:, kc, :], xs[b][:, kc * 128:(kc + 1) * 128], k.ident_b),
                 reads=[f"xs{b}", "ident_b"], writes=[f"ps{pb}"])
        S.op("act", lambda e, j=j, pst=pst: e.copy(out=xnT[:, :, j * 128:(j + 1) * 128], in_=pst),
             reads=[f"ps{pb}"], writes=[f"xnT{j}"])


def xn_keys(t0, n):
    return [f"xnT{j}" for j in range(t0 // 128, (t0 + n + 127) // 128)]


def proj_fm(k, xnT, w2d, c0, n, dst, func, scale, odt, wtag):
    S, A, ps = k.S, k.A, k.ps
    st = k.pj
    wb_i = st["it"] % 2
    st["it"] += 1
    wst, wbf = st["wst"][wb_i], st["wbf"][wb_i]
    S.dma(lambda e: e.dma_start(out=wst[:, :, 0:n], in_=w2d.rearrange("(k p) n -> p k n", p=128)[:, :, c0:c0 + n]),
          writes=[f"pwst{wb_i}"])
    eng = "dve" if wb_i == 0 else "pool"
    S.op(eng, lambda e: e.tensor_copy(out=wbf[:, :, 0:n], in_=wst[:, :, 0:n]), reads=[f"pwst{wb_i}"],
         writes=[f"pwbf{wb_i}"])
    for sub in range(0, n, 128):
        m = min(128, n - sub)
        ob_i = st["ob"] % 2
        st["ob"] += 1
        ot = st["otf"][ob_i] if odt == F32 else st["otb"][ob_i]
        okey = f"pot{'f' if odt == F32 else 'b'}{ob_i}"
        for (t0, nt) in TBLK:
            pb = st["pb"] % 4
            st["pb"] += 1
            for kc in range(KC):
                S.op("pe", lambda e, kc=kc, pb=pb, sub=sub, m=m, t0=t0, nt=nt: e.matmul(
                    ps[pb][0:m, 0:nt], wbf[:, kc, sub:sub + m], xnT[:, kc, t0:t0 + nt],
                    start=(kc == 0), stop=(kc == KC - 1)),
                    reads=[f"pwbf{wb_i}"] + xn_keys(t0, nt), writes=[f"ps{pb}"])
            S.op("act", lambda e, pb=pb, m=m, t0=t0, nt=nt, ot=ot: e.activation(
                out=ot[0:m, t0:t0 + nt], in_=ps[pb][0:m, 0:nt], func=func, scale=scale),
                reads=[f"ps{pb}"], writes=[okey])
        S.dma(lambda e, m=m, sub=sub, ot=ot: e.dma_start(out=dst[sub:sub + m, :], in_=ot[0:m, :]),
              reads=[okey], writes=[uid(k, "d")], q="pool")


def proj_tm(k, xnT, w2d, c0, n, dst, dcol):
    S, A, ps = k.S, k.A, k.ps
    st = k.pj
    wb_i = st["it"] % 2
    st["it"] += 1
    wst, wbf = st["wst"][wb_i], st["wbf"][wb_i]
    S.dma(lambda e: e.dma_start(out=wst[:, :, 0:n], in_=w2d.rearrange("(k p) n -> p k n", p=128)[:, :, c0:c0 + n]),
          writes=[f"pwst{wb_i}"])
    eng = "dve" if wb_i == 0 else "pool"
    S.op(eng, lambda e: e.tensor_copy(out=wbf[:, :, 0:n], in_=wst[:, :, 0:n]), reads=[f"pwst{wb_i}"],
         writes=[f"pwbf{wb_i}"])
    for j in range(NT):
        pb = st["pb"] % 4
        st["pb"] += 1
        ob_i = st["ob"] % 2
        st["ob"] += 1
        ot = st["ott"][ob_i]
        for kc in range(KC):
            S.op("pe", lambda e, kc=kc, pb=pb, j=j: e.matmul(
                ps[pb][:, 0:n], xnT[:, kc, j * 128:(j + 1) * 128], wbf[:, kc, 0:n],
                start=(kc == 0), stop=(kc == KC - 1)),
                reads=[f"pwbf{wb_i}", f"xnT{j}"], writes=[f"ps{pb}"])
        S.op("act", lambda e, pb=pb, ot=ot: e.copy(out=ot[:, 0:n], in_=ps[pb][:, 0:n]),
             reads=[f"ps{pb}"], writes=[f"pott{ob_i}"])
        S.dma(lambda e, j=j, ot=ot: e.dma_start(out=dst[j * 128:(j + 1) * 128, dcol:dcol + n], in_=ot[:, 0:n]),
              reads=[f"pott{ob_i}"], writes=[uid(k, "d")], q="pool")


def proj_setup(k):
    A = k.A
    k.pj = dict(it=0, ob=0, pb=4 * 0,
                wst=[A.alloc([KC, 512], F32) for _ in range(2)],
                wbf=[A.alloc([KC, 512], BF16) for _ in range(2)],
                otb=[A.alloc([T], BF16) for _ in range(2)],
                otf=[A.alloc([T], F32) for _ in range(2)],
                ott=[A.alloc([512], BF16) for _ in range(2)])


def phase_norm_win(k, l):
    A, inp, scr = k.A, k.inp, k.scr
    xnT = A.alloc([KC, T], BF16)
    norm_to_xnT(k, l, 0, 1 * D, xnT)
    proj_setup(k)
    w = inp["w_in"][l]
    proj_fm(k, xnT, w, 0, 256, scr["fmAq"], AF.Identity, 0.125, BF16, "Aq")
    proj_fm(k, xnT, w, 256, 256, scr["fmAk"], AF.Identity, 1.0, BF16, "Ak")
    proj_fm(k, xnT, w, 1024, 512, scr["fmAo"], AF.Sigmoid, 1.0, BF16, "Ao")
    proj_fm(k, xnT, w, 1536, 16, scr["gA"], AF.Identity, 1.0, F32, "gA")
    proj_tm(k, xnT, w, 256, 256, scr["tmA"], 0)
    proj_tm(k, xnT, w, 512, 512, scr["tmA"], 256)
    proj_tm(k, xnT, w, 1552, 512, scr["tmB"], 0)
    proj_tm(k, xnT, w, 2064, 256, scr["tmB"], 512)
    for i in range(3):
        proj_fm(k, xnT, w, 2320 + i * 512, 512, scr["fmC"][i * 512:(i + 1) * 512, :], AF.Identity, 1.0, BF16, "C")
    proj_fm(k, xnT, w, 3856, 512, scr["fmCz"], AF.Silu, 1.0, BF16, "Cz")
    proj_fm(k, xnT, w, 4368, 16, scr["gC"], AF.Identity, 1.0, F32, "gC")
    for i in range(6):
        proj_fm(k, xnT, w, 4384 + i * 512, 512, scr["fmG"][i * 512:(i + 1) * 512, :], AF.Sigmoid, 1.0, BF16, "G")


def phase_attn(k, l):
    S, A, inp, scr, ps = k.S, k.A, k.inp, k.scr, k.ps
    wbc = A.alloc([12, 64], F32)
    wkeys = []
    for s_ in range(12):
        src = inp["b_qnorm_w"] if s_ < 8 else inp["b_knorm_w"]
        load_bc(k, wbc[:, s_, :], src[l:l + 1, :], f"wbc{s_}")
        wkeys.append(f"wbc{s_}")
    cos_t = A.alloc([16, 32], F32)
    sin_t = A.alloc([16, 32], F32)
    S.dma(lambda e: e.dma_start(out=cos_t, in_=inp["k_cos"].rearrange("p (a b) -> p a b", a=16)), writes=["cos_t"])
    S.dma(lambda e: e.dma_start(out=sin_t, in_=inp["k_sin"].rearrange("p (a b) -> p a b", a=16)), writes=["sin_t"])
    qkT = A.alloc([6, T], BF16)
    vtm = A.alloc([NT, 2, 66], BF16)
    S.op("pool", lambda e: e.memset(vtm[:, :, :, 64:66], 0.0), writes=["vtm1"])
    S.op("pool", lambda e: e.memset(vtm[:, :, :, 64:65], 1.0), writes=["vtm1"])
    for g_ in range(2):
        S.dma(lambda e, g_=g_: e.dma_start(out=vtm[:, :, g_, 0:64],
                                           in_=scr["tmB"].rearrange("(j p) c -> p j c", p=128)[:, :, 640 + g_ * 64:704 + g_ * 64]),
              writes=[f"vtm0{g_}"])
    dcp = [A.alloc([512], F32) for _ in range(2)]
    raw = [A.alloc([768], BF16) for _ in range(2)]
    xr = [A.alloc([12, 64], F32) for _ in range(2)]
    sq = A.alloc([12, 64], F32)
    ssq = A.alloc([12], F32)
    rr = A.alloc([12], F32)
    rs = A.alloc([12], F32)
    xn = A.alloc([12, 64], F32)
    xw = A.alloc([12, 64], F32)
    t1 = A.alloc([12, 2, 16], F32)
    t2 = A.alloc([12, 2, 16], F32)
    t3 = A.alloc([12, 2, 16], F32)
    t4 = A.alloc([12, 2, 16], F32)
    xb = [A.alloc([12, 64], BF16) for _ in range(2)]
    for j in range(NT):
        b = j % 2
        S.dma(lambda e, j=j, b=b: e.dma_start(out=raw[b], in_=scr["tmB"][j * 128:(j + 1) * 128, :]), writes=[f"raw{b}"])
        rq = raw[b][:, 0:512].rearrange("p (h d) -> p h d", h=8)
        S.op("dve", lambda e, b=b, rq=rq: e.tensor_copy(out=xr[b][:, 0:8, :], in_=rq), reads=[f"raw{b}"], writes=[f"xr{b}"])
        for g in range(2):
            rk = raw[b][:, 512 + g * 64:576 + g * 64].unsqueeze(1).to_broadcast([128, 2, 64])
            S.op("pool", lambda e, b=b, g=g, rk=rk: e.tensor_copy(out=xr[b][:, 8 + 2 * g:10 + 2 * g, :], in_=rk),
                 reads=[f"raw{b}"], writes=[f"xr{b}"])
        S.op("pool", lambda e, b=b: e.tensor_tensor(out=sq, in0=xr[b], in1=xr[b], op=ALU.mult), reads=[f"xr{b}"], writes=["sq"])
        S.op("dve", lambda e: e.tensor_reduce(out=ssq, in_=sq, axis=AX.X, op=ALU.add), reads=["sq"], writes=["ssq"])
        S.op("act", lambda e: e.activation(out=rr, in_=ssq, func=AF.Sqrt, bias=k.eps_t, scale=1.0 / 64),
             reads=["ssq", "eps_t"], writes=["rr"])
        S.op("dve", lambda e: e.reciprocal(out=rs, in_=rr), reads=["rr"], writes=["rs"])
        S.op("dve", lambda e, b=b: e.tensor_tensor(out=xn, in0=xr[b], in1=rs.unsqueeze(2).to_broadcast([128, 12, 64]),
                                                   op=ALU.mult), reads=[f"xr{b}", "rs"], writes=["xn"])
        if j < 2:
            S.op("pool", lambda e, b=b: e.tensor_tensor(out=xb[b], in0=xn, in1=wbc, op=ALU.mult),
                 reads=["xn"] + wkeys, writes=[f"xb{b}"])
        else:
            jt = j - 2
            S.op("pool", lambda e: e.tensor_tensor(out=xw, in0=xn, in1=wbc, op=ALU.mult), reads=["xn"] + wkeys, writes=["xw"])
            xw5 = xw.rearrange("p h (a b f) -> p h a b f", a=2, b=2)
            xb5 = xb[b].rearrange("p h (a b f) -> p h a b f", a=2, b=2)
            x1, x2 = xw5[:, :, :, 0, :], xw5[:, :, :, 1, :]
            cb = cos_t[:, jt, :].rearrange("p (a f) -> p a f", a=2).unsqueeze(1).to_broadcast([128, 12, 2, 16])
            sb = sin_t[:, jt, :].rearrange("p (a f) -> p a f", a=2).unsqueeze(1).to_broadcast([128, 12, 2, 16])
            S.op("dve", lambda e, x1=x1, cb=cb: e.tensor_tensor(out=t1, in0=x1, in1=cb, op=ALU.mult), reads=["xw", "cos_t"], writes=["t1"])
            S.op("pool", lambda e, x2=x2, sb=sb: e.tensor_tensor(out=t2, in0=x2, in1=sb, op=ALU.mult), reads=["xw", "sin_t"], writes=["t2"])
            S.op("dve", lambda e, xb5=xb5: e.tensor_tensor(out=xb5[:, :, :, 0, :], in0=t1, in1=t2, op=ALU.subtract),
                 reads=["t1", "t2"], writes=[f"xb{b}"])
            S.op("pool", lambda e, x2=x2, cb=cb: e.tensor_tensor(out=t3, in0=x2, in1=cb, op=ALU.mult), reads=["xw", "cos_t"], writes=["t3"])
            S.op("dve", lambda e, x1=x1, sb=sb: e.tensor_tensor(out=t4, in0=x1, in1=sb, op=ALU.mult), reads=["xw", "sin_t"], writes=["t4"])
            S.op("pool", lambda e, xb5=xb5: e.tensor_tensor(out=xb5[:, :, :, 1, :], in0=t3, in1=t4, op=ALU.add),
                 reads=["t3", "t4"], writes=[f"xb{b}"])
        pb = j % 2
        pst = ps[pb][:, :].bitcast(BF16)[:, 0:768].rearrange("p (a b) -> p a b", a=6)
        xbf = xb[b].rearrange("p h d -> p (h d)")
        for blk in range(6):
            S.op("pe", lambda e, blk=blk, pst=pst, xbf=xbf: e.transpose(pst[:, blk, :], xbf[:, blk * 128:(blk + 1) * 128], k.ident_b),
                 reads=[f"xb{b}", "ident_b"], writes=[f"ps{pb}"])
        S.op("act", lambda e, j=j, pst=pst: e.copy(out=qkT[:, :, j * 128:(j + 1) * 128], in_=pst),
             reads=[f"ps{pb}"], writes=[f"qkT{j}"])
    pT = [A.alloc([512], BF16) for _ in range(4)]
    rec = [A.alloc([512], F32) for _ in range(2)]
    yo = [A.alloc([512], BF16) for _ in range(2)]
    blocks = [(0, 256, [0, 1])] + [(256 + i * 512, 512, list(range(NT))) for i in range(4)]
    cnt = 0
    sc = 0
    for g in range(2):
        for (t0, nq, ktiles) in blocks:
            qkeys = [f"qkT{j}" for j in range(t0 // 128, (t0 + nq) // 128)]
            for hh in range(4):
                h = g * 4 + hh
                base = (h % 2) * 64
                blk = h // 2
                kblk = 4 + g
                ab = cnt % 2
                cnt += 1
                OT, DEN = ps[4 + ab * 2], ps[5 + ab * 2]
                kOT, kDEN = f"ps{4 + ab * 2}", f"ps{5 + ab * 2}"
                pend = []

                def flush_one():
                    sbi, kt, first, last = pend.pop(0)
                    S.op("pe", lambda e, sbi=sbi, kt=kt, g=g, nq=nq, OT=OT, first=first, last=last: e.matmul(
                        OT[0:65, 0:nq], vtm[:, kt, g, 0:65], pT[sbi][:, 0:nq], start=first, stop=last),
                        reads=[f"vtm0{g}", "vtm1", f"pT{sbi}"], writes=[kOT])

                for ii, kt in enumerate(ktiles):
                    sbi = sc % 4
                    sc += 1
                    first, last = ii == 0, ii == len(ktiles) - 1
                    S.op("pe", lambda e, sbi=sbi, base=base, kblk=kblk, kt=kt, blk=blk, t0=t0, nq=nq: e.matmul(
                        ps[sbi][:, 0:nq], qkT[base:base + 64, kblk, kt * 128:(kt + 1) * 128],
                        qkT[base:base + 64, blk, t0:t0 + nq], start=True, stop=True),
                        reads=[f"qkT{kt}"] + qkeys, writes=[f"ps{sbi}"])
                    S.op("act", lambda e, sbi=sbi, nq=nq: e.activation(out=pT[sbi][:, 0:nq], in_=ps[sbi][:, 0:nq],
                                                                       func=AF.Exp, scale=0.125),
                         reads=[f"ps{sbi}"], writes=[f"pT{sbi}"])
                    pend.append((sbi, kt, first, last))
                    if len(pend) > 2:
                        flush_one()
                while pend:
                    flush_one()
                S.op("act", lambda e, ab=ab, nq=nq, OT=OT: e.copy(out=dcp[ab][64:65, 0:nq], in_=OT[64:65, 0:nq]), reads=[kOT], writes=[f"dcp{ab}"])
                S.op("pe", lambda e, ab=ab, nq=nq, DEN=DEN: e.matmul(DEN[0:64, 0:nq], k.ones_f[64:65, 0:64], dcp[ab][64:65, 0:nq], start=True, stop=True),
                     reads=[f"dcp{ab}", "ones_f"], writes=[kDEN])
                S.op("dve", lambda e, ab=ab, nq=nq, DEN=DEN: e.reciprocal(out=rec[ab][0:64, 0:nq], in_=DEN[0:64, 0:nq]),
                     reads=[kDEN], writes=[f"rec{ab}"])
                S.op("dve", lambda e, ab=ab, nq=nq, OT=OT: e.tensor_tensor(out=yo[ab][0:64, 0:nq], in0=OT[0:64, 0:nq],
                                                                          in1=rec[ab][0:64, 0:nq], op=ALU.mult),
                     reads=[kOT, f"rec{ab}"], writes=[f"yo{ab}"])
                S.dma(lambda e, ab=ab, h=h, t0=t0, nq=nq: e.dma_start(out=scr["fmY"][1, h * 64:(h + 1) * 64, t0:t0 + nq],
                                                                     in_=yo[ab][0:64, 0:nq]),
                      reads=[f"yo{ab}"], writes=[uid(k, "d")], q="pool")


FWD_TILES = list(range(NT))
BWD_TILES = [1, 0] + list(range(NT - 1, 1, -1))
FWD_CH = list(range(NCH))
BWD_CH = [3, 2, 1, 0] + list(range(NCH - 1, 3, -1))


def chunk_mask_tile(k, parts):
    S, A = k.S, k.A
    cm = A.alloc([T], F32)
    S.op("pool", lambda e: e.memset(cm[0:parts, :], 1.0), writes=["cmask"])
    S.op("pool", lambda e: e.memset(cm[0:parts, :].rearrange("p (c l) -> p c l", l=64)[:, :, 0:1], 0.0), writes=["cmask"])
    return cm


def phase_mlstm(k, l):
    S, A, inp, scr, ps = k.S, k.A, k.inp, k.scr, k.ps
    hsum = A.alloc([NT, 512], F32)
    m_hs = A.mark()
    sc_tm = A.alloc([NT, 16], F32)
    dec_bc = A.alloc([2, 4, NCH], F32)
    m0 = A.mark()
    cm = chunk_mask_tile(k, 4)
    ipre = A.alloc([T], F32); fpre = A.alloc([T], F32); e1 = A.alloc([T], F32); l1 = A.alloc([T], F32)
    P = A.alloc([T], F32); nb = A.alloc([T], F32); av = A.alloc([T], F32); arg = A.alloc([T], F32)
    ea = A.alloc([T], F32); fl = A.alloc([T], F32)
    bias = A.alloc([4], F32); nbf = A.alloc([1], F32)
    amax = A.alloc([NCH], F32); Mv = A.alloc([NCH], F32); darg = A.alloc([NCH], F32); mnx = A.alloc([NCH], F32)
    dec = A.alloc([NCH], F32); X = A.alloc([4, NCH], F32); minit = A.alloc([1], F32)
    S.op("pool", lambda e: e.memset(minit[0:4, :], -1.0e4), writes=["minit"])
    S.dma(lambda e: e.dma_start(out=bias[0:4, :], in_=inp["a_gate_b"][l:l + 1, :].rearrange("o (w h) -> h (o w)", h=4),
                                allow_slow_non_contiguous=True), writes=["bias"])
    v3 = lambda ap: ap[0:4, :].rearrange("p (c l) -> p c l", l=64)
    for d in range(2):
        eng = "dve"
        S.dma(lambda e, d=d: e.dma_start(out=ipre[0:4, :], in_=scr["gA"][d * 8:d * 8 + 4, :]), writes=["ipre"])
        S.dma(lambda e, d=d: e.dma_start(out=fpre[0:4, :], in_=scr["gA"][d * 8 + 4:d * 8 + 8, :]), writes=["fpre"])
        S.op("dve", lambda e, d=d: e.tensor_scalar(out=nbf[0:4, :], in0=bias[0:4, 2 * d + 1:2 * d + 2], scalar1=-1.0, scalar2=None,
                                                   op0=ALU.mult), reads=["bias"], writes=["nbf"])
        S.op("act", lambda e: e.activation(out=e1[0:4, :], in_=fpre[0:4, :], func=AF.Exp, bias=nbf[0:4, :], scale=-1.0),
             reads=["fpre", "nbf"], writes=["e1"])
        S.op("act", lambda e: e.activation(out=l1[0:4, :], in_=e1[0:4, :], func=AF.Ln, bias=1.0), reads=["e1"], writes=["l1"])
        S.op("dve", lambda e: e.tensor_tensor_scan(out=P[0:4, :], data0=cm[0:4, :], data1=l1[0:4, :], initial=0.0,
                                                   op0=ALU.mult, op1=ALU.add), reads=["cmask", "l1"], writes=["P"])
        Ptot = v3(P)[:, :, 63:64]
        if d == 0:
            nbv = P
            nbkey = "P"
        else:
            S.op("dve", lambda e, Ptot=Ptot: e.tensor_tensor(out=v3(nb), in0=Ptot.to_broadcast([4, NCH, 64]), in1=v3(P),
                                                            op=ALU.subtract), reads=["P"], writes=["nb"])
            S.op("dve", lambda e: e.tensor_tensor(out=nb[0:4, :], in0=nb[0:4, :], in1=l1[0:4, :], op=ALU.add),
                 reads=["nb", "l1"], writes=["nb"])
            nbv = nb
            nbkey = "nb"
        S.op("dve", lambda e, d=d, nbv=nbv: e.scalar_tensor_tensor(out=av[0:4, :], in0=ipre[0:4, :], scalar=bias[0:4, 2 * d:2 * d + 1],
                                                                  in1=nbv[0:4, :], op0=ALU.add, op1=ALU.add),
             reads=["ipre", "bias", nbkey], writes=["av"])
        S.op("dve", lambda e: e.tensor_reduce(out=amax[0:4, :], in_=v3(av), axis=AX.X, op=ALU.max), reads=["av"], writes=["amax"])
        order = FWD_CH if d == 0 else BWD_CH
        mcur = minit[0:4, 0:1]
        mkey = "minit"
        for c in order:
            S.op(eng, lambda e, c=c, mcur=mcur: e.tensor_tensor(out=Mv[0:4, c:c + 1], in0=mcur, in1=amax[0:4, c:c + 1], op=ALU.max),
                 reads=[mkey, "amax"], writes=["Mv"])
            S.op(eng, lambda e, c=c, mcur=mcur: e.tensor_tensor(out=darg[0:4, c:c + 1], in0=mcur, in1=Mv[0:4, c:c + 1], op=ALU.subtract),
                 reads=[mkey, "Mv"], writes=["darg"])
            S.op(eng, lambda e, c=c: e.tensor_tensor(out=mnx[0:4, c:c + 1], in0=Mv[0:4, c:c + 1], in1=v3(P)[:, c, 63:64], op=ALU.subtract),
                 reads=["Mv", "P"], writes=["mnx"])
            mcur = mnx[0:4, c:c + 1]
            mkey = "mnx"
        Mb = Mv[0:4, :].unsqueeze(2).to_broadcast([4, NCH, 64])
        S.op("dve", lambda e, Mb=Mb: e.tensor_tensor(out=v3(arg), in0=v3(av), in1=Mb, op=ALU.subtract), reads=["av", "Mv"], writes=["arg"])
        S.op("act", lambda e: e.activation(out=ea[0:4, :], in_=arg[0:4, :], func=AF.Exp), reads=["arg"], writes=["ea"])
        S.op("dve", lambda e, Mb=Mb, nbv=nbv: e.tensor_tensor(out=v3(arg), in0=v3(nbv), in1=Mb, op=ALU.subtract),
             reads=[nbkey, "Mv", "ea"], writes=["arg"])
        S.op("act", lambda e: e.activation(out=fl[0:4, :], in_=arg[0:4, :], func=AF.Exp), reads=["arg"], writes=["fl"])
        S.op("act", lambda e: e.activation(out=dec[0:4, :], in_=darg[0:4, :], func=AF.Exp), reads=["darg"], writes=["dec"])
        pst = ps[d][:, 0:NT * 8].rearrange("p (j w) -> p j w", w=8)
        for j in range(NT):
            for w_, src, skey in ((0, ea, "ea"), (1, fl, "fl")):
                S.op("pe", lambda e, j=j, w_=w_, src=src, pst=pst: e.transpose(
                    pst[:, j, w_ * 4:(w_ + 1) * 4], src[0:4, j * 128:(j + 1) * 128], k.ident_f[0:4, 0:4]),
                    reads=[skey, "ident_f"], writes=[f"ps{d}"])
        S.op("dve", lambda e, d=d, pst=pst: e.tensor_copy(out=sc_tm[:, :, d * 8:(d + 1) * 8], in_=pst), reads=[f"ps{d}"], writes=["sc_tm"])
        S.op("dve", lambda e: e.tensor_tensor(out=X[0:4], in0=dec[0:4, :].unsqueeze(1).to_broadcast([4, 4, NCH]),
                                              in1=k.ident_f[0:4, 0:4].unsqueeze(2).to_broadcast([4, 4, NCH]), op=ALU.mult),
             reads=["dec", "ident_f"], writes=["X"])
        S.op("pe", lambda e, d=d: e.matmul(ps[2 + d][:, 0:4 * NCH], k.ones_f[0:4, :], X[0:4].rearrange("p h c -> p (h c)"),
                                           start=True, stop=True), reads=["X", "ones_f"], writes=[f"ps{2 + d}"])
        S.op("act", lambda e, d=d: e.copy(out=dec_bc[:, d].rearrange("p h c -> p (h c)"), in_=ps[2 + d][:, 0:4 * NCH]),
             reads=[f"ps{2 + d}"], writes=["dec_bc"])
    S.barrier(); S.emit(); A.release(m0)
    qTz = [A.alloc([NT, 2, 128], BF16) for _ in range(4)]
    kT = [A.alloc([T], BF16) for _ in range(4)]
    qst = A.alloc([T], BF16)
    ktm = A.alloc([NT, 256], BF16)
    vext = A.alloc([NT, 4, 130], BF16)
    Cst = A.alloc([8, 130], F32)
    S.op("pool", lambda e: e.memset(hsum, 0.0), writes=["hsum"])
    S.op("pool", lambda e: e.memset(Cst[0:64], 0.0), writes=[f"C{i}" for i in range(8)])
    S.op("pool", lambda e: e.memset(vext[:, :, :, 128:130], 0.0), writes=["vext1"])
    S.op("pool", lambda e: e.memset(vext[:, :, :, 128:129], 1.0), writes=["vext1"])
    for h in range(4):
        S.dma(lambda e, h=h: e.dma_start(out=vext[:, :, h, 0:128],
                                         in_=scr["tmA"].rearrange("(j p) c -> p j c", p=128)[:, :, 256 + h * 128:256 + (h + 1) * 128]),
              writes=[f"vext0{h}"])
    S.dma(lambda e: e.dma_start(out=ktm, in_=scr["tmA"].rearrange("(j p) c -> p j c", p=128)[:, :, 0:256]), writes=["ktm"])
    for h in range(4):
        S.dma(lambda e, h=h: e.dma_start(out=kT[h][0:64, :], in_=scr["fmAk"][h * 64:(h + 1) * 64, :]), writes=[f"kT{h}"])
        S.dma(lambda e, h=h: e.dma_start(out=qst[0:64, :], in_=scr["fmAq"][h * 64:(h + 1) * 64, :]), writes=["qst"])
        S.op("pool", lambda e, h=h: e.memset(qTz[h][0:64], 0.0), writes=[f"qTz{h}"])
        q3 = qst[0:64, :].rearrange("p (j x) -> p j x", x=128)
        S.op("dve", lambda e, h=h, q3=q3: e.tensor_copy(out=qTz[h][0:64, :, 0, 0:64], in_=q3[:, :, 0:64]), reads=["qst"], writes=[f"qTz{h}"])
        S.op("pool", lambda e, h=h, q3=q3: e.tensor_copy(out=qTz[h][0:64, :, 1, 64:128], in_=q3[:, :, 64:128]), reads=["qst"], writes=[f"qTz{h}"])
    ve = [A.alloc([130], BF16) for _ in range(4)]
    sTm = [A.alloc([128], BF16) for _ in range(4)]
    Cdb = [A.alloc([130], BF16) for _ in range(4)]
    dn = [A.alloc([1], F32) for _ in range(4)]
    rc = [A.alloc([1], F32) for _ in range(4)]
    def mpipe(d, h, j, pis):
        r = h
        ci = d * 4 + h
        sb = ps[h][:, 0:128]
        Ub = ps[h][0:64, 256:386]
        Pb = ps[4 + h]
        kS = kU = f"ps{h}"
        kP = f"ps{4 + h}"
        eacol = sc_tm[:, j, d * 8 + h:d * 8 + h + 1]
        flcol = sc_tm[:, j, d * 8 + 4 + h:d * 8 + 4 + h + 1]
        S.op("act", lambda e: e.activation(out=ve[r], in_=vext[:, j, h, :], func=AF.Copy, scale=eacol),
             reads=[f"vext0{h}", "vext1", "sc_tm"], writes=[f"ve{r}"])
        for pi in range(2):
            S.op("pe", lambda e, pi=pi: e.matmul(sb, kT[h][0:64, j * 128:(j + 1) * 128], qTz[h][0:64, j, pi, :], start=(pi == 0), stop=(pi == 1)),
                 reads=[f"kT{h}", f"qTz{h}"], writes=[kS])
        yield
        S.op("dve", lambda e: e.tensor_tensor(out=sTm[r], in0=sb, in1=k.masks[:, d, :], op=ALU.mult), reads=[kS, "masks"], writes=[f"sTm{r}"])
        yield
        S.op("pe", lambda e: e.matmul(Pb[:, 0:130], sTm[r], ve[r], start=True, stop=False), reads=[f"sTm{r}", f"ve{r}"], writes=[kP])
        for n_, pi in enumerate(pis):
            c = 2 * j + pi
            dcol = dec_bc[0:64, d, h, c:c + 1]
            S.op("dve", lambda e, dcol=dcol: e.tensor_scalar(out=Cdb[r][0:64, :], in0=Cst[0:64, ci, :], scalar1=dcol, scalar2=None, op0=ALU.mult),
                 reads=[f"C{ci}", "dec_bc"], writes=[f"Cdb{r}"])
            yield
            S.op("pe", lambda e, pi=pi, n_=n_: e.matmul(Pb[:, 0:130], qTz[h][0:64, j, pi, :], Cdb[r][0:64, :], start=False, stop=(n_ == 1)),
                 reads=[f"qTz{h}", f"Cdb{r}"], writes=[kP])
            S.op("pe", lambda e, pi=pi: e.matmul(Ub, ktm[pi * 64:(pi + 1) * 64, j, h * 64:(h + 1) * 64], ve[r][pi * 64:(pi + 1) * 64, :],
                                                 start=True, stop=True), reads=["ktm", f"ve{r}"], writes=[kU])
            yield
            S.op("dve", lambda e, dcol=dcol: e.scalar_tensor_tensor(out=Cst[0:64, ci, :], in0=Cst[0:64, ci, :], scalar=dcol, in1=Ub,
                                                                  op0=ALU.mult, op1=ALU.add),
                 reads=[f"C{ci}", "dec_bc", kU], writes=[f"C{ci}"])
            yield
        S.op("act", lambda e: e.activation(out=dn[r], in_=Pb[:, 128:129], func=AF.Abs), reads=[kP], writes=[f"dn{r}"])
        yield
        S.op("dve", lambda e: e.tensor_tensor(out=dn[r], in0=dn[r], in1=flcol, op=ALU.max), reads=[f"dn{r}", "sc_tm"], writes=[f"dn{r}"])
        S.op("dve", lambda e: e.reciprocal(out=rc[r], in_=dn[r]), reads=[f"dn{r}"], writes=[f"rc{r}"])
        S.op("dve", lambda e: e.scalar_tensor_tensor(out=hsum[:, j, h * 128:(h + 1) * 128], in0=Pb[:, 0:128], scalar=rc[r],
                                                     in1=hsum[:, j, h * 128:(h + 1) * 128], op0=ALU.mult, op1=ALU.add),
             reads=[kP, f"rc{r}", f"hsum{j}", "hsum"], writes=[f"hsum{j}"])

    for step in range(NT):
        for d in range(2):
            j = (FWD_TILES if d == 0 else BWD_TILES)[step]
            pis = (0, 1) if d == 0 else (1, 0)
            alive = [mpipe(d, h, j, pis) for h in range(4)]
            while alive:
                nxt = []
                for g_ in alive:
                    try:
                        next(g_)
                        nxt.append(g_)
                    except StopIteration:
                        pass
                alive = nxt
    S.barrier(); S.emit(); A.release(m_hs)
    head_norm_out(k, hsum, "hsum", scr["fmAo"], inp["a_norm_w"][l:l + 1, :].rearrange("o (h p) -> p (o h)", p=128), 0, False)


def head_norm_out(k, hsum, hkey, gate_fm, nw_src, yidx, nw_shared):
    S, A, scr, ps = k.S, k.A, k.scr, k.ps
    oT = A.alloc([4, T], BF16)
    yaT = A.alloc([4, T], BF16)
    nw = A.alloc([4], F32)
    S.dma(lambda e: e.dma_start(out=oT, in_=gate_fm.rearrange("(h p) t -> p h t", p=128)), writes=["oT"])
    if nw_shared:
        for h in range(4):
            S.dma(lambda e, h=h: e.dma_start(out=nw[:, h:h + 1], in_=nw_src, allow_slow_non_contiguous=True), writes=[f"nw{h}"])
    else:
        S.dma(lambda e: e.dma_start(out=nw, in_=nw_src, allow_slow_non_contiguous=True), writes=["nw0"])
    nwk = [f"nw{h}" for h in range(4)] if nw_shared else ["nw0"]
    sq = [A.alloc([4, 128], F32) for _ in range(2)]
    ssq = A.alloc([NT, 4], F32)
    rr = A.alloc([NT, 4], F32)
    rs = A.alloc([NT, 4], F32)
    hn = [A.alloc([4, 128], BF16) for _ in range(2)]
    for j in range(NT):
        b = j % 2
        h3 = hsum[:, j, :].rearrange("p (h e) -> p h e", h=4)
        S.op("pool", lambda e, b=b, h3=h3: e.tensor_tensor(out=sq[b], in0=h3, in1=h3, op=ALU.mult), reads=[f"{hkey}{j}", hkey], writes=[f"hsq{b}"])
        S.op("dve", lambda e, b=b, j=j: e.tensor_reduce(out=ssq[:, j, :], in_=sq[b], axis=AX.X, op=ALU.add), reads=[f"hsq{b}"], writes=[f"hssq{j}"])
        S.op("act", lambda e, j=j: e.activation(out=rr[:, j, :], in_=ssq[:, j, :], func=AF.Sqrt, bias=k.eps_t, scale=1.0 / 128),
             reads=[f"hssq{j}", "eps_t"], writes=[f"hrr{j}"])
        S.op("dve", lambda e, j=j: e.reciprocal(out=rs[:, j, :], in_=rr[:, j, :]), reads=[f"hrr{j}"], writes=[f"hrs{j}"])
        S.op("dve", lambda e, j=j, b=b, h3=h3: e.tensor_tensor(out=hn[b], in0=h3, in1=rs[:, j, :].unsqueeze(2).to_broadcast([128, 4, 128]),
                                                            op=ALU.mult), reads=[f"{hkey}{j}", hkey, f"hrs{j}"], writes=[f"hn{b}"])
        pb = 4 + j % 2
        pst = ps[pb][:, :].bitcast(BF16)[:, 0:512].rearrange("p (a b) -> p a b", a=4)
        for h in range(4):
            S.op("pe", lambda e, h=h, b=b, pst=pst: e.transpose(pst[:, h, :], hn[b][:, h, :], k.ident_b), reads=[f"hn{b}", "ident_b"],
                 writes=[f"ps{pb}"])
        for h in range(4):
            S.op("dve", lambda e, h=h, j=j, pst=pst: e.scalar_tensor_tensor(
                out=yaT[:, h, j * 128:(j + 1) * 128], in0=pst[:, h, :], scalar=nw[:, h:h + 1], in1=oT[:, h, j * 128:(j + 1) * 128],
                op0=ALU.mult, op1=ALU.mult), reads=[f"ps{pb}", "oT"] + nwk, writes=[f"yaT{h}"])
    for h in range(4):
        S.dma(lambda e, h=h: e.dma_start(out=scr["fmY"][yidx, h * 128:(h + 1) * 128, :], in_=yaT[:, h, :]), reads=[f"yaT{h}"],
              writes=[uid(k, "d")])


MD = F32
GDN_STOP = None
GDN_LIM = 99


def phase_gdn(k, l):
    S, A, inp, scr, ps = k.S, k.A, k.inp, k.scr, k.ps
    osum = A.alloc([NT, 512], F32)
    m_os = A.mark()
    qT = [A.alloc([T], BF16) for _ in range(4)]
    kT = [A.alloc([T], BF16) for _ in range(4)]
    kvtm = A.alloc([NT, 8, 128], BF16)
    sc_tm = A.alloc([2, 7, 72], F32)
    egl_bc = A.alloc([2, 144], F32)
    m0 = A.mark()
    cwraw = A.alloc([3, 128], F32, parts=12)
    cw = A.alloc([3, 12], F32)
    S.dma(lambda e: e.dma_start(out=cwraw, in_=inp["c_conv_w"][l].rearrange("k (c p) -> c k p", p=128)), writes=["cwraw"])
    for kk in range(3):
        S.op("pe", lambda e, kk=kk: e.transpose(ps[7][:, 0:12], cwraw[0:12, kk, :], k.ident_f[0:12, 0:12]),
             reads=["cwraw", "ident_f"], writes=["ps7"])
        S.op("dve", lambda e, kk=kk: e.tensor_copy(out=cw[:, kk, :], in_=ps[7][:, 0:12]), reads=["ps7"], writes=["cw"])
    u = [A.alloc([T], BF16) for _ in range(2)]
    cv = [A.alloc([T], F32) for _ in range(2)]
    sv = cv
    sq = A.alloc([T], F32)
    rn = [A.alloc([512], F32) for _ in range(2)]
    vT = [A.alloc([T], BF16) for _ in range(4)]
    segs_l = [(1, 256), (257, T)]
    segs_r = [(0, 255), (256, T - 1)]
    pc = 0
    for ch in range(12):
        b = ch % 2
        kind, h = ch // 4, ch % 4
        S.dma(lambda e, ch=ch, b=b: e.dma_start(out=u[b], in_=scr["fmC"][ch * 128:(ch + 1) * 128, :]), writes=[f"u{b}"])
        S.op("act", lambda e, ch=ch, b=b: e.activation(out=cv[b], in_=u[b], func=AF.Identity, scale=cw[:, 1, ch:ch + 1]),
             reads=[f"u{b}", "cw"], writes=[f"cv{b}", f"sv{b}"])
        for (a0, a1) in segs_l:
            S.op("dve", lambda e, ch=ch, b=b, a0=a0, a1=a1: e.scalar_tensor_tensor(
                out=cv[b][:, a0:a1], in0=u[b][:, a0 - 1:a1 - 1], scalar=cw[:, 0, ch:ch + 1], in1=cv[b][:, a0:a1],
                op0=ALU.mult, op1=ALU.add), reads=[f"u{b}", "cw", f"cv{b}"], writes=[f"cv{b}"])
        for (a0, a1) in segs_r:
            S.op("dve", lambda e, ch=ch, b=b, a0=a0, a1=a1: e.scalar_tensor_tensor(
                out=cv[b][:, a0:a1], in0=u[b][:, a0 + 1:a1 + 1], scalar=cw[:, 2, ch:ch + 1], in1=cv[b][:, a0:a1],
                op0=ALU.mult, op1=ALU.add), reads=[f"u{b}", "cw", f"cv{b}"], writes=[f"cv{b}"])
        if kind == 2:
            S.op("act", lambda e, b=b, h=h: e.activation(out=vT[h], in_=cv[b], func=AF.Silu), reads=[f"cv{b}"], writes=[f"vT{h}"])
            continue
        S.op("act", lambda e, b=b: e.activation(out=sv[b], in_=cv[b], func=AF.Silu), reads=[f"cv{b}"], writes=[f"cv{b}", f"sv{b}"])
        S.op("act", lambda e, b=b: e.activation(out=sq, in_=sv[b], func=AF.Square), reads=[f"sv{b}"], writes=["sq"])
        for (t0, nt) in TBLK:
            pb = pc % 4
            rb = pc % 2
            pc += 1
            S.op("pe", lambda e, pb=pb, t0=t0, nt=nt: e.matmul(ps[pb][:, 0:nt], k.ones_f, sq[:, t0:t0 + nt], start=True, stop=True),
                 reads=["ones_f", "sq"], writes=[f"ps{pb}"])
            S.op("act", lambda e, pb=pb, rb=rb, nt=nt: e.activation(out=rn[rb][:, 0:nt], in_=ps[pb][:, 0:nt], func=AF.Sqrt, bias=k.eps_t, scale=1.0),
                 reads=[f"ps{pb}", "eps_t"], writes=[f"rn{rb}"])
            S.op("dve", lambda e, rb=rb, nt=nt: e.reciprocal(out=rn[rb][:, 0:nt], in_=rn[rb][:, 0:nt]), reads=[f"rn{rb}"], writes=[f"rn{rb}"])
            if kind == 0:
                S.op("dve", lambda e, b=b, rb=rb, t0=t0, nt=nt, h=h: e.scalar_tensor_tensor(
                    out=qT[h][:, t0:t0 + nt], in0=sv[b][:, t0:t0 + nt], scalar=float(128 ** -0.5), in1=rn[rb][:, 0:nt],
                    op0=ALU.mult, op1=ALU.mult), reads=[f"sv{b}", f"rn{rb}"], writes=[f"qT{h}"])
            else:
                S.op("dve", lambda e, b=b, rb=rb, t0=t0, nt=nt, h=h: e.tensor_tensor(
                    out=kT[h][:, t0:t0 + nt], in0=sv[b][:, t0:t0 + nt], in1=rn[rb][:, 0:nt], op=ALU.mult),
                    reads=[f"sv{b}", f"rn{rb}"], writes=[f"kT{h}"])
    for j in range(NT):
        pb = 4 + j % 2
        pst = ps[pb][:, :].bitcast(BF16).rearrange("p (a b) -> p a b", a=8)
        for h in range(4):
            S.op("pe", lambda e, h=h, j=j, pst=pst: e.transpose(pst[:, h, :], kT[h][:, j * 128:(j + 1) * 128], k.ident_b),
                 reads=[f"kT{h}", "ident_b"], writes=[f"ps{pb}"])
            S.op("pe", lambda e, h=h, j=j, pst=pst: e.transpose(pst[:, 4 + h, :], vT[h][:, j * 128:(j + 1) * 128], k.ident_b),
                 reads=[f"vT{h}", "ident_b"], writes=[f"ps{pb}"])
        S.op("act", lambda e, j=j, pst=pst: e.copy(out=kvtm[:, j], in_=pst), reads=[f"ps{pb}"], writes=["kvtm"])
    S.barrier(); S.emit(); A.release(m0)
    if GDN_STOP == "A":
        return
    cm = A.alloc([128], F32)
    S.op("pool", lambda e: e.memset(cm[0:72, :], 1.0), writes=["cmask"])
    S.op("pool", lambda e: e.memset(cm[0:72, :].rearrange("p (c l) -> p c l", l=64)[:, :, 0:1], 0.0), writes=["cmask"])
    wa = A.alloc([128], F32); wbt = A.alloc([128], F32); Gp = A.alloc([128], F32); Gd = A.alloc([128], F32)
    tmpg = A.alloc([128], F32); eG = A.alloc([128], F32)
    ghb = A.alloc([128], BF16); ghi = A.alloc([128], F32); glo = A.alloc([128], F32); nghi = A.alloc([128], F32); nglo = A.alloc([128], F32)
    dtb = A.alloc([2], F32); alog = A.alloc([2], F32); nega = A.alloc([2], F32)
    egc = A.alloc([2], F32); Z = A.alloc([72, 2], F32)
    for d in range(2):
        for h in range(4):
            S.dma(lambda e, d=d, h=h: e.dma_start(out=dtb[h * 18:(h + 1) * 18, d:d + 1],
                                                  in_=inp["c_dt_bias"][l:l + 1, d * 4 + h:d * 4 + h + 1].partition_broadcast(18)),
                  writes=[f"dtb{d}{h}"])
            S.dma(lambda e, d=d, h=h: e.dma_start(out=alog[h * 18:(h + 1) * 18, d:d + 1],
                                                  in_=inp["c_a_log"][l:l + 1, d * 4 + h:d * 4 + h + 1].partition_broadcast(18)),
                  writes=[f"alog{d}{h}"])
    S.op("act", lambda e: e.activation(out=nega[0:72, :], in_=alog[0:72, :], func=AF.Exp),
         reads=[f"alog{d}{h}" for d in range(2) for h in range(4)], writes=["nega"])
    S.op("dve", lambda e: e.tensor_scalar(out=nega[0:72, :], in0=nega[0:72, :], scalar1=-1.0, scalar2=None, op0=ALU.mult),
         reads=["nega"], writes=["nega"])
    dtbk = [f"dtb{d}{h}" for d in range(2) for h in range(4)]
    v3 = lambda ap: ap[0:72, :].rearrange("p (c l) -> p c l", l=64)
    for d in range(2):
        S.dma(lambda e, d=d: e.dma_start(out=wa[0:72, :], in_=scr["gC"][d * 4:d * 4 + 4, :].rearrange("h (j x) -> (h j) x", x=128)), writes=["wa"])
        S.dma(lambda e, d=d: e.dma_start(out=wbt[0:72, :], in_=scr["gC"][8 + d * 4:8 + d * 4 + 4, :].rearrange("h (j x) -> (h j) x", x=128)),
              writes=["wbt"])
        S.op("act", lambda e, d=d: e.activation(out=wa[0:72, :], in_=wa[0:72, :], func=AF.Exp, bias=dtb[0:72, d:d + 1], scale=1.0),
             reads=["wa"] + dtbk, writes=["wa"])
        S.op("act", lambda e: e.activation(out=wa[0:72, :], in_=wa[0:72, :], func=AF.Ln, bias=1.0), reads=["wa"], writes=["wa"])
        S.op("dve", lambda e, d=d: e.tensor_scalar(out=wa[0:72, :], in0=wa[0:72, :], scalar1=nega[0:72, d:d + 1], scalar2=None, op0=ALU.mult),
             reads=["wa", "nega"], writes=["wa"])
        S.op("dve", lambda e: e.tensor_tensor_scan(out=Gp[0:72, :], data0=cm[0:72, :], data1=wa[0:72, :], initial=0.0,
                                                   op0=ALU.mult, op1=ALU.add), reads=["cmask", "wa"], writes=["Gp"])
        Gtot = v3(Gp)[:, :, 63:64]
        if d == 0:
            S.op("dve", lambda e: e.tensor_copy(out=Gd[0:72, :], in_=Gp[0:72, :]), reads=["Gp"], writes=["Gd"])
        else:
            S.op("dve", lambda e, Gtot=Gtot: e.tensor_tensor(out=v3(tmpg), in0=Gtot.to_broadcast([72, 2, 64]), in1=v3(Gp), op=ALU.subtract),
                 reads=["Gp"], writes=["tmpg"])
            S.op("dve", lambda e: e.tensor_tensor(out=Gd[0:72, :], in0=tmpg[0:72, :], in1=wa[0:72, :], op=ALU.add),
                 reads=["tmpg", "wa"], writes=["Gd"])
        S.op("act", lambda e: e.activation(out=wbt[0:72, :], in_=wbt[0:72, :], func=AF.Sigmoid), reads=["wbt"], writes=["wbt"])
        S.op("act", lambda e: e.activation(out=eG[0:72, :], in_=Gd[0:72, :], func=AF.Exp), reads=["Gd"], writes=["eG"])
        S.op("dve", lambda e, Gtot=Gtot: e.tensor_tensor(out=v3(tmpg), in0=Gtot.to_broadcast([72, 2, 64]), in1=v3(Gd), op=ALU.subtract),
             reads=["Gp", "Gd"], writes=["tmpg"])
        S.op("act", lambda e: e.activation(out=tmpg[0:72, :], in_=tmpg[0:72, :], func=AF.Exp), reads=["tmpg"], writes=["tmpg"])
        S.op("act", lambda e, Gtot=Gtot: e.activation(out=egc[0:72, :].unsqueeze(2), in_=Gtot, func=AF.Exp), reads=["Gp"], writes=["egc"])
        S.op("dve", lambda e: e.tensor_copy(out=ghb[0:72, :], in_=wa[0:72, :]), reads=["wa"], writes=["ghb"])
        S.op("dve", lambda e: e.tensor_copy(out=ghi[0:72, :], in_=ghb[0:72, :]), reads=["ghb"], writes=["ghi"])
        S.op("dve", lambda e: e.tensor_tensor(out=glo[0:72, :], in0=wa[0:72, :], in1=ghi[0:72, :], op=ALU.subtract), reads=["wa", "ghi"], writes=["glo"])
        S.op("dve", lambda e: e.tensor_copy(out=ghb[0:72, :], in_=glo[0:72, :]), reads=["glo", "ghi"], writes=["ghb"])
        S.op("dve", lambda e: e.tensor_copy(out=glo[0:72, :], in_=ghb[0:72, :]), reads=["ghb"], writes=["glo"])
        S.op("dve", lambda e: e.tensor_scalar(out=nghi[0:72, :], in0=ghi[0:72, :], scalar1=-1.0, scalar2=None, op0=ALU.mult), reads=["ghi"], writes=["nghi"])
        S.op("dve", lambda e: e.tensor_scalar(out=nglo[0:72, :], in0=glo[0:72, :], scalar1=-1.0, scalar2=None, op0=ALU.mult), reads=["glo"], writes=["nglo"])
        pst = ps[d][:, 0:7 * 72].rearrange("p (w x) -> p w x", w=7)
        for w_, src, skey in ((0, wbt, "wbt"), (1, eG, "eG"), (2, tmpg, "tmpg"), (3, ghi, "ghi"), (4, glo, "glo"), (5, nghi, "nghi"), (6, nglo, "nglo")):
            S.op("pe", lambda e, w_=w_, src=src, pst=pst: e.transpose(pst[:, w_, :], src[0:72, :], k.ident_f[0:72, 0:72]),
                 reads=[skey, "ident_f"], writes=[f"ps{d}"])
        S.op("dve", lambda e, d=d, pst=pst: e.tensor_copy(out=sc_tm[:, d], in_=pst), reads=[f"ps{d}"], writes=["sc_tm"])
        S.op("dve", lambda e: e.tensor_tensor(out=Z[0:72], in0=egc[0:72, :].unsqueeze(1).to_broadcast([72, 72, 2]),
                                              in1=k.ident_f[0:72, 0:72].unsqueeze(2).to_broadcast([72, 72, 2]), op=ALU.mult),
             reads=["egc", "ident_f"], writes=["Z"])
        S.op("pe", lambda e, d=d: e.matmul(ps[2 + d][:, 0:144], k.ones_f[0:72, :], Z[0:72].rearrange("p x c -> p (x c)"),
                                           start=True, stop=True), reads=["Z", "ones_f"], writes=[f"ps{2 + d}"])
        S.op("act", lambda e, d=d: e.copy(out=egl_bc[:, d], in_=ps[2 + d][:, 0:144]), reads=[f"ps{2 + d}"], writes=["egl_bc"])
    S.barrier(); S.emit(); A.release(m0)
    if GDN_STOP == "A2":
        return
    Sf = A.alloc([8, 128], F32)
    Sb = A.alloc([8, 128], BF16)
    S.op("pool", lambda e: e.memset(osum, 0.0), writes=["osum"])
    S.op("pool", lambda e: e.memset(Sf, 0.0), writes=[f"Sf{i}" for i in range(8)])
    S.op("pool", lambda e: e.memset(Sb, 0.0), writes=[f"Sb{i}" for i in range(8)])
    NB = 4
    GT = [[A.alloc([128], BF16) for _ in range(4)] for _ in range(NB)]
    xd = [A.alloc([128], F32) for _ in range(NB)]
    DT = [A.alloc([128], F32) for _ in range(NB)]
    qkT = [A.alloc([128], BF16) for _ in range(NB)]
    tf = xd
    Pm = [[A.alloc([128], MD) for _ in range(6)] for _ in range(NB)]
    Qm = [[A.alloc([128], MD) for _ in range(5)] for _ in range(NB)]
    Yf = [A.alloc([256], F32) for _ in range(NB)]
    Yb = [A.alloc([256], MD) for _ in range(NB)] if MD == BF16 else Yf
    uw = [A.alloc([128], F32) for _ in range(NB)]
    wb_ = [A.alloc([128], BF16) for _ in range(NB)]
    wTz = [A.alloc([2, 128], BF16) for _ in range(NB)]
    vn = [A.alloc([128], BF16) for _ in range(NB)]
    kgl = [A.alloc([128], BF16) for _ in range(NB)]
    qkv = [A.alloc([128], F32) for _ in range(NB)]
    ot = qkv
    qz = [A.alloc([2, 128], BF16) for _ in range(NB)]
    for r in range(NB):
        S.op("pool", lambda e, r=r: e.memset(wTz[r], 0.0), writes=[f"wTz{r}"])
        S.op("pool", lambda e, r=r: e.memset(qz[r], 0.0), writes=[f"qz{r}"])
    identm = k.ident_b if MD == BF16 else k.ident_f
    it = 0
    ecount = [0]

    def evac(dst, dkey, src, skey):
        eng = "act" if ecount[0] % 2 == 0 else "dve"
        ecount[0] += 1
        if eng == "act":
            S.op("act", lambda e: e.copy(out=dst, in_=src), reads=[skey], writes=[dkey])
        else:
            S.op("dve", lambda e: e.tensor_copy(out=dst, in_=src), reads=[skey], writes=[dkey])

    def reg(bank, i, n=128):
        return bank[:, i * 128:i * 128 + n]

    def pipe(d, h, j, pis):
        r = h
        si = d * 4 + h
        X, Y = ps[h], ps[4 + h]
        kX, kY = f"ps{h}", f"ps{4 + h}"
        kTt = kT[h][:, j * 128:(j + 1) * 128]
        hj = h * 18 + j
        bcol = sc_tm[:, d, 0, hj:hj + 1]
        eGcol = sc_tm[:, d, 1, hj:hj + 1]
        eglcol = sc_tm[:, d, 2, hj:hj + 1]
        S.op("pool", lambda e: e.tensor_copy(out=qz[r][:, 0, 0:64], in_=qT[h][:, j * 128:j * 128 + 64]), reads=[f"qT{h}"], writes=[f"qz{r}"])
        S.op("pool", lambda e: e.tensor_copy(out=qz[r][:, 1, 64:128], in_=qT[h][:, j * 128 + 64:(j + 1) * 128]), reads=[f"qT{h}"], writes=[f"qz{r}"])
        for q_ in range(4):
            gc_ = sc_tm[:, d, 3 + q_, hj:hj + 1]
            if q_ % 2 == 0:
                S.op("act", lambda e, q_=q_, gc_=gc_: e.activation(out=GT[r][q_], in_=k.masks[:, d, :], func=AF.Copy, scale=gc_),
                     reads=["masks", "sc_tm"], writes=[f"GT{r}_{q_}"])
            else:
                S.op("dve", lambda e, q_=q_, gc_=gc_: e.tensor_scalar(out=GT[r][q_], in0=k.masks[:, d, :], scalar1=gc_, scalar2=None, op0=ALU.mult),
                     reads=["masks", "sc_tm"], writes=[f"GT{r}_{q_}"])
        yield
        S.op("pe", lambda e: e.matmul(reg(X, 0), kTt, kTt, start=True, stop=True), reads=[f"kT{h}"], writes=[kX])
        for pi in range(2):
            S.op("pe", lambda e, pi=pi: e.matmul(reg(X, 1), kTt, qz[r][:, pi, :], start=(pi == 0), stop=(pi == 1)),
                 reads=[f"kT{h}", f"qz{r}"], writes=[kX])
        for q_ in range(2):
            S.op("pe", lambda e, q_=q_: e.matmul(reg(X, 2), k.ones_b, GT[r][q_], start=(q_ == 0), stop=False),
                 reads=[f"GT{r}_{q_}", "ones_b"], writes=[kX])
        for q_ in range(2, 4):
            S.op("pe", lambda e, q_=q_: e.matmul(reg(X, 2), GT[r][q_], k.ones_b, start=False, stop=(q_ == 3)),
                 reads=[f"GT{r}_{q_}", "ones_b"], writes=[kX])
        yield
        S.op("dve", lambda e: e.tensor_tensor(out=xd[r], in0=reg(X, 2), in1=k.masks[:, 4 + d, :], op=ALU.add),
             reads=[kX, "masks"], writes=[f"xd{r}", f"tf{r}"])
        S.op("act", lambda e: e.activation(out=DT[r], in_=xd[r], func=AF.Exp), reads=[f"xd{r}"], writes=[f"DT{r}"])
        yield
        S.op("dve", lambda e: e.tensor_tensor(out=qkT[r], in0=reg(X, 1), in1=DT[r], op=ALU.mult), reads=[kX, f"DT{r}"], writes=[f"qkT{r}"])
        S.op("dve", lambda e: e.scalar_tensor_tensor(out=tf[r], in0=reg(X, 0), scalar=bcol, in1=DT[r], op0=ALU.mult, op1=ALU.mult),
             reads=[kX, "sc_tm", f"DT{r}"], writes=[f"tf{r}", f"xd{r}"])
        S.op("pool", lambda e: e.tensor_tensor(out=Pm[r][0], in0=tf[r], in1=k.masks[:, 2 + d, :], op=ALU.mult),
             reads=[f"tf{r}", "masks"], writes=[f"P{r}_0"])
        yield
        q1t = reg(X, 3).bitcast(BF16)[:, 0:128] if MD == BF16 else reg(X, 3)
        S.op("pe", lambda e: e.transpose(q1t, Pm[r][0], identm), reads=[f"P{r}_0", "ident_b", "ident_f"], writes=[kX])
        yield
        evac(Qm[r][0], f"Q{r}_0", q1t, kX)
        yield
        for i in range(1, 6):
            ra = i % 2
            S.op("pe", lambda e, i=i, ra=ra: e.matmul(reg(X, ra), Qm[r][i - 1], Pm[r][i - 1], start=True, stop=True),
                 reads=[f"Q{r}_{i - 1}", f"P{r}_{i - 1}"], writes=[kX])
            if i < 5:
                S.op("pe", lambda e, i=i, ra=ra: e.matmul(reg(Y, ra), Pm[r][i - 1], Qm[r][i - 1], start=True, stop=True),
                     reads=[f"Q{r}_{i - 1}", f"P{r}_{i - 1}"], writes=[kY])
            yield
            evac(Pm[r][i], f"P{r}_{i}", reg(X, ra), kX)
            if i < 5:
                evac(Qm[r][i], f"Q{r}_{i}", reg(Y, ra), kY)
            yield
        S.op("pool", lambda e: e.tensor_copy(out=Yf[r][:, 0:128], in_=kvtm[:, j, 4 + h, :]), reads=["kvtm"], writes=[f"Yf{r}a"])
        S.op("act", lambda e: e.activation(out=Yf[r][:, 128:256], in_=kvtm[:, j, h, :], func=AF.Copy, scale=eGcol),
             reads=["kvtm", "sc_tm"], writes=[f"Yf{r}b"])
        if MD == BF16:
            S.op("pool", lambda e: e.tensor_copy(out=Yb[r], in_=Yf[r]), reads=[f"Yf{r}a", f"Yf{r}b"], writes=[f"Yb{r}"])
        S.op("act", lambda e: e.activation(out=kgl[r], in_=kvtm[:, j, h, :], func=AF.Copy, scale=eglcol), reads=["kvtm", "sc_tm"], writes=[f"kgl{r}"])
        yield
        for n_, i in enumerate(range(5, -1, -1)):
            yr = n_ % 2
            S.op("pe", lambda e, i=i, yr=yr: e.matmul(Y[:, yr * 256:(yr + 1) * 256], Pm[r][i], Yb[r], start=True, stop=True),
                 reads=[f"P{r}_{i}", f"Yb{r}", f"Yf{r}a", f"Yf{r}b"], writes=[kY])
            yield
            S.op("dve", lambda e, yr=yr: e.tensor_tensor(out=Yf[r], in0=Y[:, yr * 256:(yr + 1) * 256], in1=Yf[r], op=ALU.add),
                 reads=[kY, f"Yf{r}a", f"Yf{r}b"], writes=[f"Yf{r}a", f"Yf{r}b"])
            if i > 0 and MD == BF16:
                S.op("act", lambda e: e.copy(out=Yb[r], in_=Yf[r]), reads=[f"Yf{r}a", f"Yf{r}b"], writes=[f"Yb{r}"])
            yield
        S.op("dve", lambda e: e.tensor_scalar(out=uw[r], in0=Yf[r][:, 0:128], scalar1=bcol, scalar2=None, op0=ALU.mult),
             reads=[f"Yf{r}a", f"Yf{r}b", "sc_tm"], writes=[f"uw{r}"])
        S.op("act", lambda e: e.activation(out=wb_[r], in_=Yf[r][:, 128:256], func=AF.Copy, scale=bcol),
             reads=[f"Yf{r}a", f"Yf{r}b", "sc_tm"], writes=[f"wb{r}"])
        yield
        wtt = reg(X, 3).bitcast(BF16)[:, 0:128]
        S.op("pe", lambda e: e.transpose(wtt, wb_[r], k.ident_b), reads=[f"wb{r}", "ident_b"], writes=[kX])
        yield
        S.op("act", lambda e: e.copy(out=wTz[r][:, 0, 0:64], in_=wtt[:, 0:64]), reads=[kX], writes=[f"wTz{r}"])
        S.op("dve", lambda e: e.tensor_copy(out=wTz[r][:, 1, 64:128], in_=wtt[:, 64:128]), reads=[kX], writes=[f"wTz{r}"])
        yield
        for n_, pi in enumerate(pis):
            R = slice(pi * 64, (pi + 1) * 64)
            S.op("pe", lambda e, pi=pi: e.matmul(reg(X, pi), wTz[r][:, pi, :], Sb[:, si, :], start=True, stop=True),
                 reads=[f"wTz{r}", f"Sb{si}"], writes=[kX])
            S.op("pe", lambda e, pi=pi, n_=n_: e.matmul(reg(Y, 0), qz[r][:, pi, :], Sb[:, si, :], start=(n_ == 0), stop=(n_ == 1)),
                 reads=[f"qz{r}", f"Sb{si}"], writes=[kY])
            yield
            S.op("dve", lambda e, pi=pi, R=R: e.tensor_tensor(out=vn[r][R, :], in0=uw[r][R, :], in1=reg(X, pi)[R, :], op=ALU.subtract),
                 reads=[f"uw{r}", kX], writes=[f"vn{r}"])
            yield
            S.op("pe", lambda e, R=R: e.matmul(reg(X, 2), kgl[r][R, :], vn[r][R, :], start=True, stop=True),
                 reads=[f"kgl{r}", f"vn{r}"], writes=[kX])
            yield
            gcol = egl_bc[:, d, hj * 2 + pi:hj * 2 + pi + 1]
            S.op("dve", lambda e, gcol=gcol: e.scalar_tensor_tensor(out=Sf[:, si, :], in0=Sf[:, si, :], scalar=gcol, in1=reg(X, 2),
                                                                  op0=ALU.mult, op1=ALU.add),
                 reads=[f"Sf{si}", "egl_bc", kX], writes=[f"Sf{si}"])
            S.op("act", lambda e: e.copy(out=Sb[:, si, :], in_=Sf[:, si, :]), reads=[f"Sf{si}"], writes=[f"Sb{si}"])
            yield
        S.op("pe", lambda e: e.matmul(reg(X, 3), qkT[r], vn[r], start=True, stop=True), reads=[f"qkT{r}", f"vn{r}"], writes=[kX])
        yield
        S.op("act", lambda e: e.copy(out=qkv[r], in_=reg(X, 3)), reads=[kX], writes=[f"qkv{r}", f"ot{r}"])
        S.op("dve", lambda e: e.scalar_tensor_tensor(out=ot[r], in0=reg(Y, 0), scalar=eGcol, in1=qkv[r], op0=ALU.mult, op1=ALU.add),
             reads=[kY, "sc_tm", f"qkv{r}"], writes=[f"ot{r}", f"qkv{r}"])
        S.op("pool", lambda e: e.tensor_tensor(out=osum[:, j, h * 128:(h + 1) * 128], in0=osum[:, j, h * 128:(h + 1) * 128], in1=ot[r], op=ALU.add),
             reads=[f"ot{r}", "osum", f"osum{j}"], writes=[f"osum{j}"])

    for step in range(NT):
        for d in range(2):
            j = (FWD_TILES if d == 0 else BWD_TILES)[step]
            pis = (0, 1) if d == 0 else (1, 0)
            alive = [pipe(d, h, j, pis) for h in range(4)]
            while alive:
                nxt = []
                for g_ in alive:
                    try:
                        next(g_)
                        nxt.append(g_)
                    except StopIteration:
                        pass
                alive = nxt
    S.barrier(); S.emit(); A.release(m_os)
    head_norm_out(k, osum, "osum", scr["fmCz"], inp["c_norm_w"][l:l + 1, :].rearrange("o p -> p o"), 2, True)


def residual_update(k, j, psrc, pkeys, gbc_r, gkey, ht, tmp, b):
    S, scr = k.S, k.scr
    S.dma(lambda e: e.dma_start(out=ht[b], in_=scr["hres"][j * 128:(j + 1) * 128, :]), reads=[f"hres{j}"], writes=[f"rht{b}"])
    for half in range(2):
        cs = slice(half * 512, (half + 1) * 512)
        S.op("dve", lambda e, half=half, cs=cs: e.tensor_tensor(out=tmp[half], in0=psrc[half][:, :], in1=gbc_r[:, cs], op=ALU.mult),
             reads=[pkeys[half], gkey], writes=[f"rtmp{half}"])
        S.op("pool", lambda e, half=half, cs=cs: e.tensor_tensor(out=ht[b][:, cs], in0=ht[b][:, cs], in1=tmp[half], op=ALU.add),
             reads=[f"rtmp{half}", f"rht{b}"], writes=[f"rht{b}"])
    S.dma(lambda e: e.dma_start(out=scr["hres"][j * 128:(j + 1) * 128, :], in_=ht[b]), reads=[f"rht{b}"],
          writes=[f"hres{j}"], q="pool")


def phase_merge(k, l):
    S, A, inp, scr, ps = k.S, k.A, k.inp, k.scr, k.ps
    wbr = A.alloc([3, 4, D], BF16)
    wo = A.alloc([8, D], BF16)
    stage = A.alloc([4, D], F32)
    for i in range(3):
        S.dma(lambda e, i=i: e.dma_start(out=stage, in_=inp["w_branch"][l, i].rearrange("(k p) n -> p k n", p=128)),
              writes=["stage"])
        S.op("dve" if i % 2 == 0 else "pool", lambda e, i=i: e.tensor_copy(out=wbr[:, i], in_=stage), reads=["stage"], writes=[f"wbr{i}"])
    for hf in range(2):
        S.dma(lambda e, hf=hf: e.dma_start(out=stage, in_=inp["w_out"][l].rearrange("(k p) n -> p k n", p=128)[:, hf * 4:(hf + 1) * 4, :]),
              writes=["stage"])
        S.op("dve" if hf == 0 else "pool", lambda e, hf=hf: e.tensor_copy(out=wo[:, hf * 4:(hf + 1) * 4, :], in_=stage),
             reads=["stage"], writes=[f"wo{hf}"])
    g1bc = [A.alloc([D], F32) for _ in range(2)]
    for r in range(2):
        load_bc(k, g1bc[r], scr["modrow"][l, r:r + 1, 2 * D:3 * D], f"g1bc{r}")
    yin = [[A.alloc([4, 512], BF16) for _ in range(3)] for _ in range(2)]
    gin = [[A.alloc([8, 512], BF16) for _ in range(3)] for _ in range(2)]
    yT = [A.alloc([8, 512], BF16) for _ in range(2)]
    tA = [A.alloc([512], F32) for _ in range(2)]
    tB = [A.alloc([512], F32) for _ in range(2)]
    tC = [A.alloc([512], F32) for _ in range(2)]
    ht = [A.alloc([D], F32) for _ in range(2)]
    tmp = [A.alloc([512], F32) for _ in range(2)]
    tcount = 0
    for bi, (t0, nt) in enumerate(TBLK):
        bb = bi % 2
        r = 1 if t0 < TCTX else 0
        for i in range(3):
            S.dma(lambda e, i=i, bb=bb, t0=t0, nt=nt: e.dma_start(
                out=yin[bb][i][:, :, 0:nt], in_=scr["fmY"][i].rearrange("(k p) t -> p k t", p=128)[:, :, t0:t0 + nt]),
                writes=[f"yin{bb}{i}"])
            S.dma(lambda e, i=i, bb=bb, t0=t0, nt=nt: e.dma_start(
                out=gin[bb][i][:, :, 0:nt],
                in_=scr["fmG"][i * D:(i + 1) * D, :].rearrange("(k p) t -> p k t", p=128)[:, :, t0:t0 + nt]),
                writes=[f"gin{bb}{i}"])
        for dc in range(8):
            pbase = 3 * (dc % 2)
            tb = dc % 2
            for i in range(3):
                for kc in range(4):
                    S.op("pe", lambda e, i=i, kc=kc, dc=dc, bb=bb, nt=nt, pbase=pbase: e.matmul(
                        ps[pbase + i][:, 0:nt], wbr[:, i, kc, dc * 128:(dc + 1) * 128], yin[bb][i][:, kc, 0:nt],
                        start=(kc == 0), stop=(kc == 3)),
                        reads=[f"wbr{i}", f"yin{bb}{i}"], writes=[f"ps{pbase + i}"])
            for i, tt in enumerate((tA, tB, tC)):
                S.op("dve", lambda e, i=i, tt=tt, dc=dc, bb=bb, nt=nt, pbase=pbase, tb=tb: e.tensor_tensor(
                    out=tt[tb][:, 0:nt], in0=ps[pbase + i][:, 0:nt], in1=gin[bb][i][:, dc, 0:nt], op=ALU.mult),
                    reads=[f"ps{pbase + i}", f"gin{bb}{i}"], writes=[f"t{i}{tb}"])
            S.op("pool", lambda e, tb=tb, nt=nt: e.tensor_tensor(out=tA[tb][:, 0:nt], in0=tA[tb][:, 0:nt], in1=tB[tb][:, 0:nt], op=ALU.add),
                 reads=[f"t0{tb}", f"t1{tb}"], writes=[f"t0{tb}"])
            S.op("pool", lambda e, tb=tb, nt=nt, dc=dc, bb=bb: e.tensor_tensor(out=yT[bb][:, dc, 0:nt], in0=tA[tb][:, 0:nt], in1=tC[tb][:, 0:nt], op=ALU.add),
                 reads=[f"t0{tb}", f"t2{tb}"], writes=[f"yT{bb}"])
        for ti in range(nt // 128):
            j = t0 // 128 + ti
            for half in range(2):
                for kc in range(8):
                    S.op("pe", lambda e, kc=kc, ti=ti, half=half, bb=bb: e.matmul(
                        ps[6 + half][:, :], yT[bb][:, kc, ti * 128:(ti + 1) * 128], wo[:, kc, half * 512:(half + 1) * 512],
                        start=(kc == 0), stop=(kc == 7)),
                        reads=[f"yT{bb}", "wo0", "wo1"], writes=[f"ps{6 + half}"])
            residual_update(k, j, [ps[6], ps[7]], ["ps6", "ps7"], g1bc[r], f"g1bc{r}", ht, tmp, tcount % 2)
            tcount += 1


def phase_ffn(k, l):
    S, A, inp, scr, ps = k.S, k.A, k.inp, k.scr, k.ps
    xnT = A.alloc([KC, T], BF16)
    norm_to_xnT(k, l, 3 * D, 4 * D, xnT)
    cwraw = A.alloc([3, 128], F32, parts=44)
    cw = A.alloc([3, 44], F32)
    S.dma(lambda e: e.dma_start(out=cwraw, in_=inp["ffn_conv_w"][l].rearrange("k (c p) -> c k p", p=128)), writes=["cwraw"])
    for kk in range(3):
        S.op("pe", lambda e, kk=kk: e.transpose(ps[7][:, 0:44], cwraw[0:44, kk, :], k.ident_f[0:44, 0:44]),
             reads=["cwraw", "ident_f"], writes=["ps7"])
        S.op("dve", lambda e, kk=kk: e.tensor_copy(out=cw[:, kk, :], in_=ps[7][:, 0:44]), reads=["ps7"], writes=["cw"])
    wst2 = [A.alloc([KC, 256], F32) for _ in range(2)]
    wbf2 = [A.alloc([KC, 256], BF16) for _ in range(2)]
    cbuf = [[A.alloc([T], F32) for _ in range(2)] for _ in range(2)]
    sg = A.alloc([T], F32)
    hmt = [A.alloc([T], BF16) for _ in range(2)]
    blocks = [(0, 256, 0, 256)]
    for i in range(5):
        s0 = 256 + i * 410
        blocks.append((s0, min(s0 + 410, T), 256, T))
    pcount = 0
    for jp in range(22):
        wb = jp % 2
        for w2, c0 in ((0, jp * 128), (1, DFF + jp * 128)):
            S.dma(lambda e, wb=wb, w2=w2, c0=c0: e.dma_start(
                out=wst2[wb][:, :, w2 * 128:(w2 + 1) * 128],
                in_=inp["w_up"][l].rearrange("(k p) n -> p k n", p=128)[:, :, c0:c0 + 128]), writes=[f"fwst{wb}{w2}"])
        S.op("pool", lambda e, wb=wb: e.tensor_copy(out=wbf2[wb], in_=wst2[wb]), reads=[f"fwst{wb}0", f"fwst{wb}1"], writes=[f"fwbf{wb}"])
        for (s, e_, q0, q1) in blocks:
            hl = 1 if s > q0 else 0
            hr = 1 if e_ < q1 else 0
            n = e_ - s
            ncol = n + hl + hr
            for w2 in range(2):
                chunk = jp + 22 * w2
                pb = pcount % 6
                pcount += 1
                cdst = cbuf[wb][w2]
                ckey = f"cbuf{wb}{w2}"
                for kc in range(KC):
                    S.op("pe", lambda e, kc=kc, wb=wb, w2=w2, pb=pb, s=s, hl=hl, ncol=ncol: e.matmul(
                        ps[pb][:, 0:ncol], wbf2[wb][:, kc, w2 * 128:(w2 + 1) * 128], xnT[:, kc, s - hl:s - hl + ncol],
                        start=(kc == 0), stop=(kc == KC - 1)),
                        reads=[f"fwbf{wb}"] + xn_keys(s - hl, ncol), writes=[f"ps{pb}"])
                S.op("act", lambda e, pb=pb, cdst=cdst, s=s, e_=e_, hl=hl, n=n, chunk=chunk: e.activation(
                    out=cdst[:, s:e_], in_=ps[pb][:, hl:hl + n], func=AF.Identity, scale=cw[:, 1, chunk:chunk + 1]),
                    reads=[f"ps{pb}", "cw"], writes=[ckey])
                ts = s + (1 - hl)
                S.op("dve", lambda e, pb=pb, cdst=cdst, s=s, e_=e_, hl=hl, ts=ts, chunk=chunk: e.scalar_tensor_tensor(
                    out=cdst[:, ts:e_], in0=ps[pb][:, ts - s + hl - 1:e_ - s + hl - 1], scalar=cw[:, 0, chunk:chunk + 1],
                    in1=cdst[:, ts:e_], op0=ALU.mult, op1=ALU.add),
                    reads=[f"ps{pb}", "cw", ckey], writes=[ckey])
                te = e_ - (1 - hr)
                S.op("dve", lambda e, pb=pb, cdst=cdst, s=s, hl=hl, te=te, chunk=chunk: e.scalar_tensor_tensor(
                    out=cdst[:, s:te], in0=ps[pb][:, hl + 1:hl + 1 + (te - s)], scalar=cw[:, 2, chunk:chunk + 1],
                    in1=cdst[:, s:te], op0=ALU.mult, op1=ALU.add),
                    reads=[f"ps{pb}", "cw", ckey], writes=[ckey])
        S.op("act", lambda e, wb=wb: e.activation(out=sg, in_=cbuf[wb][1], func=AF.Silu), reads=[f"cbuf{wb}1"], writes=["sg"])
        S.op("pool", lambda e, wb=wb: e.tensor_tensor(out=hmt[wb], in0=cbuf[wb][0], in1=sg, op=ALU.mult),
             reads=[f"cbuf{wb}0", "sg"], writes=[f"hmt{wb}"])
        S.dma(lambda e, wb=wb, jp=jp: e.dma_start(out=scr["hmid"][jp * 128:(jp + 1) * 128, :], in_=hmt[wb]),
              reads=[f"hmt{wb}"], writes=[uid(k, "d")], q="pool")
    S.barrier(); S.emit(); A.reset()
    wd = A.alloc([22, D], BF16)
    stage = A.alloc([4, D], F32)
    for pi in range(6):
        j0 = pi * 4
        nj = min(4, 22 - j0)
        S.dma(lambda e, j0=j0, nj=nj: e.dma_start(
            out=stage[:, 0:nj, :], in_=inp["w_down"][l].rearrange("(j p) n -> p j n", p=128)[:, j0:j0 + nj, :]), writes=["stage"])
        S.op("dve" if pi % 2 == 0 else "pool", lambda e, j0=j0, nj=nj: e.tensor_copy(out=wd[:, j0:j0 + nj, :], in_=stage[:, 0:nj, :]),
             reads=["stage"], writes=[f"wd{pi}"])
    wdkeys = [f"wd{pi}" for pi in range(6)]
    g2bc = [A.alloc([D], F32) for _ in range(2)]
    for r in range(2):
        load_bc(k, g2bc[r], scr["modrow"][l, r:r + 1, 5 * D:6 * D], f"g2bc{r}")
    hin = [A.alloc([22, 512], BF16) for _ in range(2)]
    ht = [A.alloc([D], F32) for _ in range(2)]
    tmp = [A.alloc([512], F32) for _ in range(2)]
    tcount = 0
    for bi, (t0, nt) in enumerate(TBLK):
        bb = bi % 2
        r = 1 if t0 < TCTX else 0
        S.dma(lambda e, bb=bb, t0=t0, nt=nt: e.dma_start(
            out=hin[bb][:, :, 0:nt], in_=scr["hmid"].rearrange("(j p) t -> p j t", p=128)[:, :, t0:t0 + nt]), writes=[f"hin{bb}"])
        for ti in range(nt // 128):
            j = t0 // 128 + ti
            pq = (tcount % 2) * 2
            for half in range(2):
                for jj in range(22):
                    S.op("pe", lambda e, jj=jj, ti=ti, half=half, bb=bb, pq=pq: e.matmul(
                        ps[pq + half][:, :], hin[bb][:, jj, ti * 128:(ti + 1) * 128], wd[:, jj, half * 512:(half + 1) * 512],
                        start=(jj == 0), stop=(jj == 21)),
                        reads=[f"hin{bb}"] + wdkeys, writes=[f"ps{pq + half}"])
            residual_update(k, j, [ps[pq], ps[pq + 1]], [f"ps{pq}", f"ps{pq + 1}"], g2bc[r], f"g2bc{r}", ht, tmp, tcount % 2)
            tcount += 1


def host_consts():
    ident = np.eye(128, dtype=np.float32)
    n_freq = 16
    inv = (10000.0 ** (-np.arange(n_freq, dtype=np.float32) / n_freq)).astype(np.float32)
    tok = np.arange(2048)
    row = (tok // 64).astype(np.float32)
    col = (tok % 64).astype(np.float32)
    ang = np.stack([row[:, None] * inv, col[:, None] * inv], axis=1).astype(np.float32)
    cos = np.cos(ang).astype(np.float32).reshape(16, 128, 32).transpose(1, 0, 2).reshape(128, 16 * 32)
    sin = np.sin(ang).astype(np.float32).reshape(16, 128, 32).transpose(1, 0, 2).reshape(128, 16 * 32)
    s = np.arange(128)[:, None]
    t = np.arange(128)[None, :]
    same = (s // 64) == (t // 64)
    NEG = -1000.0
    m = np.zeros((128, 6, 128), np.float32)
    m[:, 0] = (same & (s <= t))
    m[:, 1] = (same & (s >= t))
    m[:, 2] = -1.0 * (same & (s < t))
    m[:, 3] = -1.0 * (same & (s > t))
    m[:, 4] = np.where(same & (s <= t), 0.0, NEG)
    m[:, 5] = np.where(same & (s >= t), 0.0, NEG)
    return dict(k_ident=ident, k_cos=np.ascontiguousarray(cos), k_sin=np.ascontiguousarray(sin),
                k_masks=np.ascontiguousarray(m.reshape(128, 6 * 128)))


def make_in_maps(inputs):
    f = lambda a: np.ascontiguousarray(np.asarray(a, dtype=np.float32))
    consts = host_consts()
    shared = dict(
        c_ctx=f(inputs["c_ctx"]).reshape(1, D),
        norm1_w=f(inputs["norm1_w"]), norm2_w=f(inputs["norm2_w"]),
        ada_w=f(inputs["ada_w"]), ada_b=f(inputs["ada_b"]), w_in=f(inputs["w_in"]),
        a_gate_b=f(inputs["a_gate_b"]).reshape(DEPTH, 16), a_norm_w=f(inputs["a_norm_w"]),
        b_qnorm_w=f(inputs["b_qnorm_w"]), b_knorm_w=f(inputs["b_knorm_w"]),
        c_conv_w=f(inputs["c_conv_w"]), c_a_log=f(inputs["c_a_log"]).reshape(DEPTH, 8),
        c_dt_bias=f(inputs["c_dt_bias"]).reshape(DEPTH, 8), c_norm_w=f(inputs["c_norm_w"]),
        w_branch=f(inputs["w_branch"]), w_out=f(inputs["w_out"]), w_up=f(inputs["w_up"]),
        ffn_conv_w=f(inputs["ffn_conv_w"]), w_down=f(inputs["w_down"]), **consts)
    x = f(inputs["x"]); c = f(inputs["c"]); ctx = f(inputs["ctx"])
    maps = []
    for b in range(8):
        m = dict(shared)
        m["x"] = x[b]; m["ctx"] = ctx[b]; m["c"] = c[b].reshape(1, D)
        maps.append(m)
    return maps


def kernel(**inputs):
    nc, k = build_program()
    in_maps = make_in_maps(inputs)
    res = run_bass_kernel_spmd(nc, in_maps, core_ids=list(range(8)))
    return np.stack([np.asarray(r["out"], dtype=np.float32) for r in res.results], axis=0)
```

```python
import math
from contextlib import ExitStack

import numpy as np
import concourse.bass as bass
import concourse.mybir as mybir
from concourse.bass_utils import run_bass_kernel_spmd

F32 = mybir.dt.float32
BF16 = mybir.dt.bfloat16
AF = mybir.ActivationFunctionType
ALU = mybir.AluOpType
AX = mybir.AxisListType

D = 1024
T = 2304
NT = 18
NCH = 36
TCTX = 256
DEPTH = 4
IN_COLS = 7456
DFF = 2816
EPS = 1e-6
KC = 8

ENGS = ["pe", "act", "dve", "pool", "sp"]
EPOCH = 30000
NDS = 12


class Sched:
    def __init__(self, nc, es):
        self.nc = nc
        self.es = es
        self.ops = {e: [] for e in ENGS}
        self.count = {e: 0 for e in ENGS}
        self.esems = {e: [] for e in ENGS}
        self.waited = {e: {} for e in ENGS}
        self.last_w = {}
        self.readers = {}
        self.dsems = {}
        self.dval = {}
        self.dnext = {}
        self.sem_objs = {}
        self.nsem = 0
        self.latest = {}
        self.total = {e: 0 for e in ENGS}
        for q in ("sp", "pool", "act"):
            self.dsems[q] = [self._newsem(f"d{q}{i}") for i in range(NDS)]
            self.dval[q] = [0] * NDS
            self.dnext[q] = 0

    def _newsem(self, name):
        s = self.es.enter_context(self.nc.semaphore(name))
        self.nsem += 1
        self.sem_objs[self.nsem] = s
        return self.nsem

    def _deps(self, engine, reads, writes):
        deps = {}

        def add(ev):
            if ev is None:
                return
            s, v = ev
            if deps.get(s, 0) < v:
                deps[s] = v
        for k in reads:
            add(self.last_w.get(k))
        for k in writes:
            add(self.last_w.get(k))
            for ev in self.readers.get(k, ()):
                add(ev)
        waits = []
        own = set(self.esems[engine]) if engine == "pe" else ()
        for s, v in deps.items():
            if s in own:
                continue
            if self.waited[engine].get(s, 0) >= v:
                continue
            self.waited[engine][s] = v
            waits.append((s, v))
        return waits

    def _commit(self, ev, reads, writes):
        for k in writes:
            self.last_w[k] = ev
            self.readers[k] = []
        for k in reads:
            self.readers.setdefault(k, []).append(ev)
        self.latest[ev[0]] = ev[1]

    def op(self, engine, fn, reads=(), writes=()):
        psr = [x for x in reads if x.startswith("ps")]
        if psr:
            writes = list(writes) + psr
        waits = self._deps(engine, reads, writes)
        self.count[engine] += 1
        c = self.count[engine]
        ep = (c - 1) // EPOCH
        while len(self.esems[engine]) <= ep:
            self.esems[engine].append(self._newsem(f"e{engine}{len(self.esems[engine])}"))
        sem = self.esems[engine][ep]
        ev = (sem, c - ep * EPOCH)
        self.ops[engine].append((fn, waits, (sem, 1)))
        self._commit(ev, reads, writes)
        return ev

    def dma(self, fn, reads=(), writes=(), q="sp"):
        i = self.dnext[q]
        self.dnext[q] = (i + 1) % NDS
        sem = self.dsems[q][i]
        prev = self.dval[q][i]
        waits = self._deps(q, reads, writes)
        if prev > 0 and self.waited[q].get(sem, 0) < prev:
            self.waited[q][sem] = prev
            waits.append((sem, prev))
        self.dval[q][i] = prev + 16
        ev = (sem, prev + 16)
        self.ops[q].append((fn, waits, (sem, 16)))
        self._commit(ev, reads, writes)
        return ev

    def barrier(self):
        for e in ENGS:
            waits = []
            for s, v in self.latest.items():
                if self.waited[e].get(s, 0) >= v:
                    continue
                self.waited[e][s] = v
                waits.append((s, v))
            if waits:
                self.ops[e].append((None, waits, None))
        self.last_w = {}
        self.readers = {}

    def emit(self):
        nc = self.nc
        so = self.sem_objs
        ops = self.ops
        with nc.Block() as block:
            def run(e, lst):
                for fn, waits, inc in lst:
                    for s, v in waits:
                        e.wait_ge(so[s], v)
                    if fn is None:
                        continue
                    ins = fn(e)
                    if inc is not None:
                        ins.then_inc(so[inc[0]], inc[1])

            @block.tensor
            def _(e):
                run(e, ops["pe"])

            @block.scalar
            def _(e):
                run(e, ops["act"])

            @block.vector
            def _(e):
                run(e, ops["dve"])

            @block.gpsimd
            def _(e):
                run(e, ops["pool"])

            @block.sync
            def _(e):
                run(e, ops["sp"])
        for e in ENGS:
            self.total[e] += len(ops[e])
        self.ops = {e: [] for e in ENGS}


class Arena:
    def __init__(self, big, words):
        self.big = big
        self.words = words
        self.top = 0
        self.base = 0

    def reset(self):
        self.top = self.base

    def freeze(self):
        self.base = self.top

    def mark(self):
        return self.top

    def release(self, m):
        self.top = m

    def alloc(self, shape, dtype, parts=128):
        shape = list(shape)
        n = int(np.prod(shape))
        w = (n + 1) // 2 if dtype == BF16 else n
        a = self.top
        self.top += w
        self.peak = max(getattr(self, "peak", 0), self.top)
        assert self.top <= self.words, f"SBUF arena overflow {self.top} > {self.words}"
        ap = self.big[0:parts, a:a + w]
        if dtype == BF16:
            ap = ap.bitcast(BF16)[:, 0:n]
        if len(shape) == 2:
            ap = ap.rearrange("p (a b) -> p a b", a=shape[0])
        elif len(shape) == 3:
            ap = ap.rearrange("p (a b c) -> p a b c", a=shape[0], b=shape[1])
        return ap


TBLK = [(0, 256), (256, 512), (768, 512), (1280, 512), (1792, 512)]


class K:
    pass


def build_program(dbg=(), nlayers=DEPTH, stop_after=None):
    nc = bass.Bass("TRN2", target_bir_lowering=False)
    k = K()
    k.nc = nc
    inp = {}

    def din(name, shape):
        inp[name] = nc.dram_tensor(name, list(shape), F32, kind="ExternalInput").ap()
        return inp[name]

    din("x", [2048, D]); din("ctx", [TCTX, D]); din("c", [1, D]); din("c_ctx", [1, D])
    din("norm1_w", [DEPTH, D]); din("norm2_w", [DEPTH, D])
    din("ada_w", [DEPTH, D, 6 * D]); din("ada_b", [DEPTH, 6 * D])
    din("w_in", [DEPTH, D, IN_COLS])
    din("a_gate_b", [DEPTH, 16]); din("a_norm_w", [DEPTH, 512])
    din("b_qnorm_w", [DEPTH, 64]); din("b_knorm_w", [DEPTH, 64])
    din("c_conv_w", [DEPTH, 3, 1536]); din("c_a_log", [DEPTH, 8]); din("c_dt_bias", [DEPTH, 8])
    din("c_norm_w", [DEPTH, 128])
    din("w_branch", [DEPTH, 3, 512, D]); din("w_out", [DEPTH, D, D])
    din("w_up", [DEPTH, D, 2 * DFF]); din("ffn_conv_w", [DEPTH, 3, 2 * DFF]); din("w_down", [DEPTH, DFF, D])
    din("k_ident", [128, 128]); din("k_cos", [128, 16 * 32]); din("k_sin", [128, 16 * 32])
    din("k_masks", [128, 6 * 128])
    out = nc.dram_tensor("out", [2048, D], F32, kind="ExternalOutput").ap()

    scr = {}

    def dscr(name, shape, dtype):
        kind = "ExternalOutput" if name in dbg else "Internal"
        scr[name] = nc.dram_tensor("s_" + name, list(shape), dtype, kind=kind).ap()
        return scr[name]

    dscr("hres", [T, D], F32)
    dscr("modrow", [DEPTH, 2, 6 * D], F32)
    dscr("fmAq", [256, T], BF16); dscr("fmAk", [256, T], BF16); dscr("fmAo", [512, T], BF16)
    dscr("gA", [16, T], F32)
    dscr("tmA", [T, 768], BF16); dscr("tmB", [T, 768], BF16)
    dscr("fmC", [1536, T], BF16); dscr("fmCz", [512, T], BF16); dscr("gC", [16, T], F32)
    dscr("fmG", [3072, T], BF16)
    dscr("fmY", [3, 512, T], BF16)
    dscr("hmid", [DFF, T], BF16)
    k.inp, k.scr, k.out = inp, scr, out

    with ExitStack() as es:
        WORDS = 48 * 1024
        big = es.enter_context(nc.sbuf_tensor("big", [128, WORDS], F32))
        k.ps = [es.enter_context(nc.psum_tensor(f"ps{i}", [128, 512], F32)) for i in range(8)]
        S = Sched(nc, es)
        A = Arena(big, WORDS)
        k.S, k.A = S, A
        k.uid = 0
        setup_consts(k)
        phase_init(k)
        S.barrier(); S.emit(); A.reset()
        phase_adaln(k)
        S.barrier(); S.emit(); A.reset()
        done = False
        for l in range(nlayers):
            for ph in (phase_norm_win, phase_attn, phase_mlstm, phase_gdn, phase_merge, phase_ffn):
                ph(k, l)
                S.barrier(); S.emit(); A.reset()
                if stop_after == (l, ph.__name__):
                    done = True
                    break
            if done:
                break
        phase_final(k)
        S.barrier(); S.emit()
        k.totals = dict(S.total)
    return nc, k


def uid(k, s):
    k.uid += 1
    return f"{s}#{k.uid}"


def setup_consts(k):
    S, A, inp = k.S, k.A, k.inp
    k.ident_f = A.alloc([128], F32)
    k.ident_b = A.alloc([128], BF16)
    k.ones_b = A.alloc([128], BF16)
    k.ones_f = A.alloc([128], F32)
    k.eps_t = A.alloc([1], F32)
    k.masks = A.alloc([6, 128], F32)
    S.dma(lambda e: e.dma_start(out=k.ident_f, in_=inp["k_ident"][:, :]), writes=["ident_f"])
    S.dma(lambda e: e.dma_start(out=k.masks, in_=inp["k_masks"].rearrange("p (a b) -> p a b", a=6)), writes=["masks"])
    S.op("dve", lambda e: e.tensor_copy(out=k.ident_b, in_=k.ident_f), reads=["ident_f"], writes=["ident_b"])
    S.op("pool", lambda e: e.memset(k.ones_b, 1.0), writes=["ones_b"])
    S.op("pool", lambda e: e.memset(k.ones_f, 1.0), writes=["ones_f"])
    S.op("pool", lambda e: e.memset(k.eps_t, EPS), writes=["eps_t"])
    A.freeze()


def phase_init(k):
    S, inp, scr = k.S, k.inp, k.scr
    S.dma(lambda e: e.dma_start(out=scr["hres"][0:TCTX, :], in_=inp["ctx"][:, :]), writes=[uid(k, "d")])
    for i in range(4):
        S.dma(lambda e, i=i: e.dma_start(out=scr["hres"][TCTX + i * 512:TCTX + (i + 1) * 512, :],
                                         in_=inp["x"][i * 512:(i + 1) * 512, :]), writes=[uid(k, "d")])


def phase_final(k):
    S, scr = k.S, k.scr
    for i in range(4):
        S.dma(lambda e, i=i: e.dma_start(out=k.out[i * 512:(i + 1) * 512, :],
                                         in_=scr["hres"][TCTX + i * 512:TCTX + (i + 1) * 512, :]), writes=[uid(k, "d")])


def phase_adaln(k):
    S, A, inp, scr, ps = k.S, k.A, k.inp, k.scr, k.ps
    craw = A.alloc([KC, 2], F32)
    sT = A.alloc([KC, 2], F32)
    S.dma(lambda e: e.dma_start(out=craw[:, :, 0], in_=inp["c"].rearrange("o (k p) -> p (o k)", p=128),
                                allow_slow_non_contiguous=True), writes=["craw0"])
    S.dma(lambda e: e.dma_start(out=craw[:, :, 1], in_=inp["c_ctx"].rearrange("o (k p) -> p (o k)", p=128),
                                allow_slow_non_contiguous=True), writes=["craw1"])
    S.op("act", lambda e: e.activation(out=sT, in_=craw, func=AF.Silu), reads=["craw0", "craw1"], writes=["sT"])
    wst = [A.alloc([KC, 512], F32) for _ in range(2)]
    brow = [A.alloc([6 * D], F32, parts=2) for _ in range(2)]
    mrow = [A.alloc([6 * D], F32, parts=2) for _ in range(2)]
    nrow = [A.alloc([2, D], F32, parts=2) for _ in range(2)]
    it = 0
    for l in range(DEPTH):
        b = l % 2
        S.dma(lambda e, l=l, b=b: e.dma_start(out=brow[b], in_=inp["ada_b"][l:l + 1, :].partition_broadcast(2)),
              writes=[f"brow{b}"])
        S.dma(lambda e, l=l, b=b: e.dma_start(out=nrow[b][:, 0, :], in_=inp["norm1_w"][l:l + 1, :].partition_broadcast(2)),
              writes=[f"nrow{b}a"])
        S.dma(lambda e, l=l, b=b: e.dma_start(out=nrow[b][:, 1, :], in_=inp["norm2_w"][l:l + 1, :].partition_broadcast(2)),
              writes=[f"nrow{b}b"])
        for cb in range(12):
            wb = it % 2
            pb = it % 4
            it += 1
            S.dma(lambda e, l=l, cb=cb, wb=wb: e.dma_start(
                out=wst[wb], in_=inp["ada_w"][l].rearrange("(k p) n -> p k n", p=128)[:, :, cb * 512:(cb + 1) * 512]),
                writes=[f"wst{wb}"])
            for kc in range(KC):
                S.op("pe", lambda e, kc=kc, wb=wb, pb=pb: e.matmul(ps[pb][0:2, :], sT[:, kc, :], wst[wb][:, kc, :],
                                                                  start=(kc == 0), stop=(kc == KC - 1)),
                     reads=["sT", f"wst{wb}"], writes=[f"ps{pb}"])
            S.op("dve", lambda e, b=b, cb=cb, pb=pb: e.tensor_tensor(
                out=mrow[b][:, cb * 512:(cb + 1) * 512], in0=ps[pb][0:2, :], in1=brow[b][:, cb * 512:(cb + 1) * 512],
                op=ALU.add), reads=[f"ps{pb}", f"brow{b}"], writes=[f"mrow{b}"])
        for which, off in ((0, 1 * D), (1, 4 * D)):
            S.op("dve", lambda e, b=b, which=which, off=off: e.scalar_tensor_tensor(
                out=mrow[b][:, off:off + D], in0=mrow[b][:, off:off + D], scalar=1.0, in1=nrow[b][:, which, :],
                op0=ALU.add, op1=ALU.mult), reads=[f"mrow{b}", f"nrow{b}a", f"nrow{b}b"], writes=[f"mrow{b}"])
        S.dma(lambda e, l=l, b=b: e.dma_start(out=scr["modrow"][l], in_=mrow[b]), reads=[f"mrow{b}"],
              writes=[uid(k, "d")])


def load_bc(k, dst, src_row, key, q="sp"):
    k.S.dma(lambda e: e.dma_start(out=dst, in_=src_row.partition_broadcast(128)), writes=[key], q=q)


def norm_to_xnT(k, l, shift_off, g_off, xnT):
    S, A, scr, ps = k.S, k.A, k.scr, k.ps
    gbc = [A.alloc([D], F32) for _ in range(2)]
    sbc = [A.alloc([D], F32) for _ in range(2)]
    for r in range(2):
        load_bc(k, gbc[r], scr["modrow"][l, r:r + 1, g_off:g_off + D], f"gbc{r}")
        load_bc(k, sbc[r], scr["modrow"][l, r:r + 1, shift_off:shift_off + D], f"sbc{r}")
    ht = [A.alloc([D], F32) for _ in range(2)]
    junk = A.alloc([D], BF16)
    tmp = [A.alloc([D], F32) for _ in range(2)]
    xs = [A.alloc([D], BF16) for _ in range(2)]
    ss = A.alloc([NT], F32)
    rr = A.alloc([NT], F32)
    rstd = A.alloc([NT], F32)
    S.op("pool", lambda e: e.memset(ss, 0.0), writes=["ss"])
    for j in range(NT):
        b = j % 2
        r = 1 if j < 2 else 0
        pb = j % 2
        S.dma(lambda e, j=j, b=b: e.dma_start(out=ht[b], in_=scr["hres"][j * 128:(j + 1) * 128, :]),
              reads=[f"hres{j}"], writes=[f"ht{b}"])
        S.op("act", lambda e, j=j, b=b: e.activation(out=junk, in_=ht[b], func=AF.Square, accum_out=ss[:, j:j + 1]),
             reads=[f"ht{b}", "ss"], writes=["junk", f"ss{j}"])
        S.op("act", lambda e, j=j: e.activation(out=rr[:, j:j + 1], in_=ss[:, j:j + 1], func=AF.Sqrt,
                                                bias=k.eps_t, scale=1.0 / D),
             reads=[f"ss{j}", "eps_t"], writes=[f"rr{j}"])
        S.op("dve", lambda e, j=j: e.reciprocal(out=rstd[:, j:j + 1], in_=rr[:, j:j + 1]),
             reads=[f"rr{j}"], writes=[f"rstd{j}"])
        S.op("dve", lambda e, j=j, b=b, r=r: e.scalar_tensor_tensor(
            out=tmp[b], in0=ht[b], scalar=rstd[:, j:j + 1], in1=gbc[r], op0=ALU.mult, op1=ALU.mult),
            reads=[f"ht{b}", f"rstd{j}", f"gbc{r}"], writes=[f"tmp{b}"])
        S.op("dve", lambda e, b=b, r=r: e.tensor_tensor(out=xs[b], in0=tmp[b], in1=sbc[r], op=ALU.add),
             reads=[f"tmp{b}", f"sbc{r}"], writes=[f"xs{b}"])
        pst = ps[pb][:, :].bitcast(BF16).rearrange("p (a b) -> p a b", a=KC)
        for kc in range(KC):
            S.op("pe", lambda e, kc=kc, b=b, pst=pst: e.transpose(pst[:, kc, :], xs[b][:, kc * 128:(kc + 1) * 128], k.ident_b),
                 reads=[f"xs{b}", "ident_b"], writes=[f"ps{pb}"])
        S.op("act", lambda e, j=j, pst=pst: e.copy(out=xnT[:, :, j * 128:(j + 1) * 128], in_=pst),
             reads=[f"ps{pb}"], writes=[f"xnT{j}"])


def xn_keys(t0, n):
    return [f"xnT{j}" for j in range(t0 // 128, (t0 + n + 127) // 128)]


def proj_fm(k, xnT, w2d, c0, n, dst, func, scale, odt, wtag):
    S, A, ps = k.S, k.A, k.ps
    st = k.pj
    wb_i = st["it"] % 2
    st["it"] += 1
    wst, wbf = st["wst"][wb_i], st["wbf"][wb_i]
    S.dma(lambda e: e.dma_start(out=wst[:, :, 0:n], in_=w2d.rearrange("(k p) n -> p k n", p=128)[:, :, c0:c0 + n]),
          writes=[f"pwst{wb_i}"])
    eng = "dve" if wb_i == 0 else "pool"
    S.op(eng, lambda e: e.tensor_copy(out=wbf[:, :, 0:n], in_=wst[:, :, 0:n]), reads=[f"pwst{wb_i}"],
         writes=[f"pwbf{wb_i}"])
    for sub in range(0, n, 128):
        m = min(128, n - sub)
        ob_i = st["ob"] % 2
        st["ob"] += 1
        ot = st["otf"][ob_i] if odt == F32 else st["otb"][ob_i]
        okey = f"pot{'f' if odt == F32 else 'b'}{ob_i}"
        for (t0, nt) in TBLK:
            pb = st["pb"] % 4
            st["pb"] += 1
            for kc in range(KC):
                S.op("pe", lambda e, kc=kc, pb=pb, sub=sub, m=m, t0=t0, nt=nt: e.matmul(
                    ps[pb][0:m, 0:nt], wbf[:, kc, sub:sub + m], xnT[:, kc, t0:t0 + nt],
                    start=(kc == 0), stop=(kc == KC - 1)),
                    reads=[f"pwbf{wb_i}"] + xn_keys(t0, nt), writes=[f"ps{pb}"])
            S.op("act", lambda e, pb=pb, m=m, t0=t0, nt=nt, ot=ot: e.activation(
                out=ot[0:m, t0:t0 + nt], in_=ps[pb][0:m, 0:nt], func=func, scale=scale),
                reads=[f"ps{pb}"], writes=[okey])
        S.dma(lambda e, m=m, sub=sub, ot=ot: e.dma_start(out=dst[sub:sub + m, :], in_=ot[0:m, :]),
              reads=[okey], writes=[uid(k, "d")], q="pool")


def proj_tm(k, xnT, w2d, c0, n, dst, dcol):
    S, A, ps = k.S, k.A, k.ps
    st = k.pj
    wb_i = st["it"] % 2
    st["it"] += 1
    wst, wbf = st["wst"][wb_i], st["wbf"][wb_i]
    S.dma(lambda e: e.dma_start(out=wst[:, :, 0:n], in_=w2d.rearrange("(k p) n -> p k n", p=128)[:, :, c0:c0 + n]),
          writes=[f"pwst{wb_i}"])
    eng = "dve" if wb_i == 0 else "pool"
    S.op(eng, lambda e: e.tensor_copy(out=wbf[:, :, 0:n], in_=wst[:, :, 0:n]), reads=[f"pwst{wb_i}"],
         writes=[f"pwbf{wb_i}"])
    for j in range(NT):
        pb = st["pb"] % 4
        st["pb"] += 1
        ob_i = st["ob"] % 2
        st["ob"] += 1
        ot = st["ott"][ob_i]
        for kc in range(KC):
            S.op("pe", lambda e, kc=kc, pb=pb, j=j: e.matmul(
                ps[pb][:, 0:n], xnT[:, kc, j * 128:(j + 1) * 128], wbf[:, kc, 0:n],
                start=(kc == 0), stop=(kc == KC - 1)),
                reads=[f"pwbf{wb_i}", f"xnT{j}"], writes=[f"ps{pb}"])
        S.op("act", lambda e, pb=pb, ot=ot: e.copy(out=ot[:, 0:n], in_=ps[pb][:, 0:n]),
             reads=[f"ps{pb}"], writes=[f"pott{ob_i}"])
        S.dma(lambda e, j=j, ot=ot: e.dma_start(out=dst[j * 128:(j + 1) * 128, dcol:dcol + n], in_=ot[:, 0:n]),
              reads=[f"pott{ob_i}"], writes=[uid(k, "d")], q="pool")


def proj_setup(k):
    A = k.A
    k.pj = dict(it=0, ob=0, pb=4 * 0,
                wst=[A.alloc([KC, 512], F32) for _ in range(2)],
                wbf=[A.alloc([KC, 512], BF16) for _ in range(2)],
                otb=[A.alloc([T], BF16) for _ in range(2)],
                otf=[A.alloc([T], F32) for _ in range(2)],
                ott=[A.alloc([512], BF16) for _ in range(2)])


def phase_norm_win(k, l):
    A, inp, scr = k.A, k.inp, k.scr
    xnT = A.alloc([KC, T], BF16)
    norm_to_xnT(k, l, 0, 1 * D, xnT)
    proj_setup(k)
    w = inp["w_in"][l]
    proj_fm(k, xnT, w, 0, 256, scr["fmAq"], AF.Identity, 0.125, BF16, "Aq")
    proj_fm(k, xnT, w, 256, 256, scr["fmAk"], AF.Identity, 1.0, BF16, "Ak")
    proj_fm(k, xnT, w, 1024, 512, scr["fmAo"], AF.Sigmoid, 1.0, BF16, "Ao")
    proj_fm(k, xnT, w, 1536, 16, scr["gA"], AF.Identity, 1.0, F32, "gA")
    proj_tm(k, xnT, w, 256, 256, scr["tmA"], 0)
    proj_tm(k, xnT, w, 512, 512, scr["tmA"], 256)
    proj_tm(k, xnT, w, 1552, 512, scr["tmB"], 0)
    proj_tm(k, xnT, w, 2064, 256, scr["tmB"], 512)
    for i in range(3):
        proj_fm(k, xnT, w, 2320 + i * 512, 512, scr["fmC"][i * 512:(i + 1) * 512, :], AF.Identity, 1.0, BF16, "C")
    proj_fm(k, xnT, w, 3856, 512, scr["fmCz"], AF.Silu, 1.0, BF16, "Cz")
    proj_fm(k, xnT, w, 4368, 16, scr["gC"], AF.Identity, 1.0, F32, "gC")
    for i in range(6):
        proj_fm(k, xnT, w, 4384 + i * 512, 512, scr["fmG"][i * 512:(i + 1) * 512, :], AF.Sigmoid, 1.0, BF16, "G")


def phase_attn(k, l):
    S, A, inp, scr, ps = k.S, k.A, k.inp, k.scr, k.ps
    wbc = A.alloc([12, 64], F32)
    wkeys = []
    for s_ in range(12):
        src = inp["b_qnorm_w"] if s_ < 8 else inp["b_knorm_w"]
        load_bc(k, wbc[:, s_, :], src[l:l + 1, :], f"wbc{s_}")
        wkeys.append(f"wbc{s_}")
    cos_t = A.alloc([16, 32], F32)
    sin_t = A.alloc([16, 32], F32)
    S.dma(lambda e: e.dma_start(out=cos_t, in_=inp["k_cos"].rearrange("p (a b) -> p a b", a=16)), writes=["cos_t"])
    S.dma(lambda e: e.dma_start(out=sin_t, in_=inp["k_sin"].rearrange("p (a b) -> p a b", a=16)), writes=["sin_t"])
    qkT = A.alloc([6, T], BF16)
    vtm = A.alloc([NT, 128], BF16)
    S.dma(lambda e: e.dma_start(out=vtm, in_=scr["tmB"].rearrange("(j p) c -> p j c", p=128)[:, :, 640:768]),
          writes=["vtm"])
    raw = [A.alloc([768], BF16) for _ in range(2)]
    xr = [A.alloc([12, 64], F32) for _ in range(2)]
    sq = A.alloc([12, 64], F32)
    ssq = A.alloc([12], F32)
    rr = A.alloc([12], F32)
    rs = A.alloc([12], F32)
    xn = A.alloc([12, 64], F32)
    xw = A.alloc([12, 64], F32)
    t1 = A.alloc([12, 2, 16], F32)
    t2 = A.alloc([12, 2, 16], F32)
    t3 = A.alloc([12, 2, 16], F32)
    t4 = A.alloc([12, 2, 16], F32)
    xb = [A.alloc([12, 64], BF16) for _ in range(2)]
    for j in range(NT):
        b = j % 2
        S.dma(lambda e, j=j, b=b: e.dma_start(out=raw[b], in_=scr["tmB"][j * 128:(j + 1) * 128, :]), writes=[f"raw{b}"])
        rq = raw[b][:, 0:512].rearrange("p (h d) -> p h d", h=8)
        S.op("dve", lambda e, b=b, rq=rq: e.tensor_copy(out=xr[b][:, 0:8, :], in_=rq), reads=[f"raw{b}"], writes=[f"xr{b}"])
        for g in range(2):
            rk = raw[b][:, 512 + g * 64:576 + g * 64].unsqueeze(1).to_broadcast([128, 2, 64])
            S.op("pool", lambda e, b=b, g=g, rk=rk: e.tensor_copy(out=xr[b][:, 8 + 2 * g:10 + 2 * g, :], in_=rk),
                 reads=[f"raw{b}"], writes=[f"xr{b}"])
        S.op("pool", lambda e, b=b: e.tensor_tensor(out=sq, in0=xr[b], in1=xr[b], op=ALU.mult), reads=[f"xr{b}"], writes=["sq"])
        S.op("dve", lambda e: e.tensor_reduce(out=ssq, in_=sq, axis=AX.X, op=ALU.add), reads=["sq"], writes=["ssq"])
        S.op("act", lambda e: e.activation(out=rr, in_=ssq, func=AF.Sqrt, bias=k.eps_t, scale=1.0 / 64),
             reads=["ssq", "eps_t"], writes=["rr"])
        S.op("dve", lambda e: e.reciprocal(out=rs, in_=rr), reads=["rr"], writes=["rs"])
        S.op("dve", lambda e, b=b: e.tensor_tensor(out=xn, in0=xr[b], in1=rs.unsqueeze(2).to_broadcast([128, 12, 64]),
                                                   op=ALU.mult), reads=[f"xr{b}", "rs"], writes=["xn"])
        if j < 2:
            S.op("pool", lambda e, b=b: e.tensor_tensor(out=xb[b], in0=xn, in1=wbc, op=ALU.mult),
                 reads=["xn"] + wkeys, writes=[f"xb{b}"])
        else:
            jt = j - 2
            S.op("pool", lambda e: e.tensor_tensor(out=xw, in0=xn, in1=wbc, op=ALU.mult), reads=["xn"] + wkeys, writes=["xw"])
            xw5 = xw.rearrange("p h (a b f) -> p h a b f", a=2, b=2)
            xb5 = xb[b].rearrange("p h (a b f) -> p h a b f", a=2, b=2)
            x1, x2 = xw5[:, :, :, 0, :], xw5[:, :, :, 1, :]
            cb = cos_t[:, jt, :].rearrange("p (a f) -> p a f", a=2).unsqueeze(1).to_broadcast([128, 12, 2, 16])
            sb = sin_t[:, jt, :].rearrange("p (a f) -> p a f", a=2).unsqueeze(1).to_broadcast([128, 12, 2, 16])
            S.op("dve", lambda e, x1=x1, cb=cb: e.tensor_tensor(out=t1, in0=x1, in1=cb, op=ALU.mult), reads=["xw", "cos_t"], writes=["t1"])
            S.op("pool", lambda e, x2=x2, sb=sb: e.tensor_tensor(out=t2, in0=x2, in1=sb, op=ALU.mult), reads=["xw", "sin_t"], writes=["t2"])
            S.op("dve", lambda e, xb5=xb5: e.tensor_tensor(out=xb5[:, :, :, 0, :], in0=t1, in1=t2, op=ALU.subtract),
                 reads=["t1", "t2"], writes=[f"xb{b}"])
            S.op("pool", lambda e, x2=x2, cb=cb: e.tensor_tensor(out=t3, in0=x2, in1=cb, op=ALU.mult), reads=["xw", "cos_t"], writes=["t3"])
            S.op("dve", lambda e, x1=x1, sb=sb: e.tensor_tensor(out=t4, in0=x1, in1=sb, op=ALU.mult), reads=["xw", "sin_t"], writes=["t4"])
            S.op("pool", lambda e, xb5=xb5: e.tensor_tensor(out=xb5[:, :, :, 1, :], in0=t3, in1=t4, op=ALU.add),
                 reads=["t3", "t4"], writes=[f"xb{b}"])
        pb = j % 2
        pst = ps[pb][:, :].bitcast(BF16)[:, 0:768].rearrange("p (a b) -> p a b", a=6)
        xbf = xb[b].rearrange("p h d -> p (h d)")
        for blk in range(6):
            S.op("pe", lambda e, blk=blk, pst=pst, xbf=xbf: e.transpose(pst[:, blk, :], xbf[:, blk * 128:(blk + 1) * 128], k.ident_b),
                 reads=[f"xb{b}", "ident_b"], writes=[f"ps{pb}"])
        S.op("act", lambda e, j=j, pst=pst: e.copy(out=qkT[:, :, j * 128:(j + 1) * 128], in_=pst),
             reads=[f"ps{pb}"], writes=[f"qkT{j}"])
    pT = [A.alloc([512], BF16) for _ in range(4)]
    rec = [A.alloc([512], F32) for _ in range(2)]
    yo = [A.alloc([512], BF16) for _ in range(2)]
    blocks = [(0, 256, [0, 1])] + [(256 + i * 512, 512, list(range(NT))) for i in range(4)]
    cnt = 0
    sc = 0
    for g in range(2):
        for (t0, nq, ktiles) in blocks:
            qkeys = [f"qkT{j}" for j in range(t0 // 128, (t0 + nq) // 128)]
            for hh in range(4):
                h = g * 4 + hh
                base = (h % 2) * 64
                blk = h // 2
                kblk = 4 + g
                ab = cnt % 2
                cnt += 1
                OT, DEN = ps[4 + ab * 2], ps[5 + ab * 2]
                kOT, kDEN = f"ps{4 + ab * 2}", f"ps{5 + ab * 2}"
                pend = []

                def flush_one():
                    sbi, kt, first, last = pend.pop(0)
                    S.op("pe", lambda e, sbi=sbi, kt=kt, g=g, nq=nq, OT=OT, first=first, last=last: e.matmul(
                        OT[0:64, 0:nq], vtm[:, kt, g * 64:(g + 1) * 64], pT[sbi][:, 0:nq], start=first, stop=last),
                        reads=["vtm", f"pT{sbi}"], writes=[kOT])
                    S.op("pe", lambda e, sbi=sbi, nq=nq, DEN=DEN, first=first, last=last: e.matmul(
                        DEN[0:64, 0:nq], k.ones_b[:, 0:64], pT[sbi][:, 0:nq], start=first, stop=last),
                        reads=["ones_b", f"pT{sbi}"], writes=[kDEN])

                for ii, kt in enumerate(ktiles):
                    sbi = sc % 4
                    sc += 1
                    first, last = ii == 0, ii == len(ktiles) - 1
                    S.op("pe", lambda e, sbi=sbi, base=base, kblk=kblk, kt=kt, blk=blk, t0=t0, nq=nq: e.matmul(
                        ps[sbi][:, 0:nq], qkT[base:base + 64, kblk, kt * 128:(kt + 1) * 128],
                        qkT[base:base + 64, blk, t0:t0 + nq], start=True, stop=True),
                        reads=[f"qkT{kt}"] + qkeys, writes=[f"ps{sbi}"])
                    S.op("act", lambda e, sbi=sbi, nq=nq: e.activation(out=pT[sbi][:, 0:nq], in_=ps[sbi][:, 0:nq],
                                                                       func=AF.Exp, scale=0.125),
                         reads=[f"ps{sbi}"], writes=[f"pT{sbi}"])
                    pend.append((sbi, kt, first, last))
                    if len(pend) > 2:
                        flush_one()
                while pend:
                    flush_one()
                S.op("dve", lambda e, ab=ab, nq=nq, DEN=DEN: e.reciprocal(out=rec[ab][0:64, 0:nq], in_=DEN[0:64, 0:nq]),
                     reads=[kDEN], writes=[f"rec{ab}"])
                S.op("dve", lambda e, ab=ab, nq=nq, OT=OT: e.tensor_tensor(out=yo[ab][0:64, 0:nq], in0=OT[0:64, 0:nq],
                                                                          in1=rec[ab][0:64, 0:nq], op=ALU.mult),
                     reads=[kOT, f"rec{ab}"], writes=[f"yo{ab}"])
                S.dma(lambda e, ab=ab, h=h, t0=t0, nq=nq: e.dma_start(out=scr["fmY"][1, h * 64:(h + 1) * 64, t0:t0 + nq],
                                                                     in_=yo[ab][0:64, 0:nq]),
                      reads=[f"yo{ab}"], writes=[uid(k, "d")], q="pool")


FWD_TILES = list(range(NT))
BWD_TILES = [1, 0] + list(range(NT - 1, 1, -1))
FWD_CH = list(range(NCH))
BWD_CH = [3, 2, 1, 0] + list(range(NCH - 1, 3, -1))


def chunk_mask_tile(k, parts):
    S, A = k.S, k.A
    cm = A.alloc([T], F32)
    S.op("pool", lambda e: e.memset(cm[0:parts, :], 1.0), writes=["cmask"])
    S.op("pool", lambda e: e.memset(cm[0:parts, :].rearrange("p (c l) -> p c l", l=64)[:, :, 0:1], 0.0), writes=["cmask"])
    return cm


def phase_mlstm(k, l):
    S, A, inp, scr, ps = k.S, k.A, k.inp, k.scr, k.ps
    hsum = A.alloc([NT, 512], F32)
    m_hs = A.mark()
    sc_tm = A.alloc([NT, 16], F32)
    dec_bc = A.alloc([2, 4, NCH], F32)
    m0 = A.mark()
    cm = chunk_mask_tile(k, 4)
    ipre = A.alloc([T], F32); fpre = A.alloc([T], F32); e1 = A.alloc([T], F32); l1 = A.alloc([T], F32)
    P = A.alloc([T], F32); nb = A.alloc([T], F32); av = A.alloc([T], F32); arg = A.alloc([T], F32)
    ea = A.alloc([T], F32); fl = A.alloc([T], F32)
    bias = A.alloc([4], F32); nbf = A.alloc([1], F32)
    amax = A.alloc([NCH], F32); Mv = A.alloc([NCH], F32); darg = A.alloc([NCH], F32); mnx = A.alloc([NCH], F32)
    dec = A.alloc([NCH], F32); X = A.alloc([4, NCH], F32); minit = A.alloc([1], F32)
    S.op("pool", lambda e: e.memset(minit[0:4, :], -1.0e4), writes=["minit"])
    S.dma(lambda e: e.dma_start(out=bias[0:4, :], in_=inp["a_gate_b"][l:l + 1, :].rearrange("o (w h) -> h (o w)", h=4),
                                allow_slow_non_contiguous=True), writes=["bias"])
    v3 = lambda ap: ap[0:4, :].rearrange("p (c l) -> p c l", l=64)
    for d in range(2):
        eng = "dve"
        S.dma(lambda e, d=d: e.dma_start(out=ipre[0:4, :], in_=scr["gA"][d * 8:d * 8 + 4, :]), writes=["ipre"])
        S.dma(lambda e, d=d: e.dma_start(out=fpre[0:4, :], in_=scr["gA"][d * 8 + 4:d * 8 + 8, :]), writes=["fpre"])
        S.op("dve", lambda e, d=d: e.tensor_scalar(out=nbf[0:4, :], in0=bias[0:4, 2 * d + 1:2 * d + 2], scalar1=-1.0, scalar2=None,
                                                   op0=ALU.mult), reads=["bias"], writes=["nbf"])
        S.op("act", lambda e: e.activation(out=e1[0:4, :], in_=fpre[0:4, :], func=AF.Exp, bias=nbf[0:4, :], scale=-1.0),
             reads=["fpre", "nbf"], writes=["e1"])
        S.op("act", lambda e: e.activation(out=l1[0:4, :], in_=e1[0:4, :], func=AF.Ln, bias=1.0), reads=["e1"], writes=["l1"])
        S.op("dve", lambda e: e.tensor_tensor_scan(out=P[0:4, :], data0=cm[0:4, :], data1=l1[0:4, :], initial=0.0,
                                                   op0=ALU.mult, op1=ALU.add), reads=["cmask", "l1"], writes=["P"])
        Ptot = v3(P)[:, :, 63:64]
        if d == 0:
            nbv = P
            nbkey = "P"
        else:
            S.op("dve", lambda e, Ptot=Ptot: e.tensor_tensor(out=v3(nb), in0=Ptot.to_broadcast([4, NCH, 64]), in1=v3(P),
                                                            op=ALU.subtract), reads=["P"], writes=["nb"])
            S.op("dve", lambda e: e.tensor_tensor(out=nb[0:4, :], in0=nb[0:4, :], in1=l1[0:4, :], op=ALU.add),
                 reads=["nb", "l1"], writes=["nb"])
            nbv = nb
            nbkey = "nb"
        S.op("dve", lambda e, d=d, nbv=nbv: e.scalar_tensor_tensor(out=av[0:4, :], in0=ipre[0:4, :], scalar=bias[0:4, 2 * d:2 * d + 1],
                                                                  in1=nbv[0:4, :], op0=ALU.add, op1=ALU.add),
             reads=["ipre", "bias", nbkey], writes=["av"])
        S.op("dve", lambda e: e.tensor_reduce(out=amax[0:4, :], in_=v3(av), axis=AX.X, op=ALU.max), reads=["av"], writes=["amax"])
        order = FWD_CH if d == 0 else BWD_CH
        mcur = minit[0:4, 0:1]
        mkey = "minit"
        for c in order:
            S.op(eng, lambda e, c=c, mcur=mcur: e.tensor_tensor(out=Mv[0:4, c:c + 1], in0=mcur, in1=amax[0:4, c:c + 1], op=ALU.max),
                 reads=[mkey, "amax"], writes=["Mv"])
            S.op(eng, lambda e, c=c, mcur=mcur: e.tensor_tensor(out=darg[0:4, c:c + 1], in0=mcur, in1=Mv[0:4, c:c + 1], op=ALU.subtract),
                 reads=[mkey, "Mv"], writes=["darg"])
            S.op(eng, lambda e, c=c: e.tensor_tensor(out=mnx[0:4, c:c + 1], in0=Mv[0:4, c:c + 1], in1=v3(P)[:, c, 63:64], op=ALU.subtract),
                 reads=["Mv", "P"], writes=["mnx"])
            mcur = mnx[0:4, c:c + 1]
            mkey = "mnx"
        Mb = Mv[0:4, :].unsqueeze(2).to_broadcast([4, NCH, 64])
        S.op("dve", lambda e, Mb=Mb: e.tensor_tensor(out=v3(arg), in0=v3(av), in1=Mb, op=ALU.subtract), reads=["av", "Mv"], writes=["arg"])
        S.op("act", lambda e: e.activation(out=ea[0:4, :], in_=arg[0:4, :], func=AF.Exp), reads=["arg"], writes=["ea"])
        S.op("dve", lambda e, Mb=Mb, nbv=nbv: e.tensor_tensor(out=v3(arg), in0=v3(nbv), in1=Mb, op=ALU.subtract),
             reads=[nbkey, "Mv", "ea"], writes=["arg"])
        S.op("act", lambda e: e.activation(out=fl[0:4, :], in_=arg[0:4, :], func=AF.Exp), reads=["arg"], writes=["fl"])
        S.op("act", lambda e: e.activation(out=dec[0:4, :], in_=darg[0:4, :], func=AF.Exp), reads=["darg"], writes=["dec"])
        pst = ps[d][:, 0:NT * 8].rearrange("p (j w) -> p j w", w=8)
        for j in range(NT):
            for w_, src, skey in ((0, ea, "ea"), (1, fl, "fl")):
                S.op("pe", lambda e, j=j, w_=w_, src=src, pst=pst: e.transpose(
                    pst[:, j, w_ * 4:(w_ + 1) * 4], src[0:4, j * 128:(j + 1) * 128], k.ident_f[0:4, 0:4]),
                    reads=[skey, "ident_f"], writes=[f"ps{d}"])
        S.op("dve", lambda e, d=d, pst=pst: e.tensor_copy(out=sc_tm[:, :, d * 8:(d + 1) * 8], in_=pst), reads=[f"ps{d}"], writes=["sc_tm"])
        S.op("dve", lambda e: e.tensor_tensor(out=X[0:4], in0=dec[0:4, :].unsqueeze(1).to_broadcast([4, 4, NCH]),
                                              in1=k.ident_f[0:4, 0:4].unsqueeze(2).to_broadcast([4, 4, NCH]), op=ALU.mult),
             reads=["dec", "ident_f"], writes=["X"])
        S.op("pe", lambda e, d=d: e.matmul(ps[2 + d][:, 0:4 * NCH], k.ones_f[0:4, :], X[0:4].rearrange("p h c -> p (h c)"),
                                           start=True, stop=True), reads=["X", "ones_f"], writes=[f"ps{2 + d}"])
        S.op("act", lambda e, d=d: e.copy(out=dec_bc[:, d].rearrange("p h c -> p (h c)"), in_=ps[2 + d][:, 0:4 * NCH]),
             reads=[f"ps{2 + d}"], writes=["dec_bc"])
    S.barrier(); S.emit(); A.release(m0)
    qTz = [A.alloc([NT, 2, 128], BF16) for _ in range(4)]
    kT = [A.alloc([T], BF16) for _ in range(4)]
    qst = A.alloc([T], BF16)
    ktm = A.alloc([NT, 256], BF16)
    vext = A.alloc([NT, 4, 130], BF16)
    Cst = A.alloc([8, 130], F32)
    S.op("pool", lambda e: e.memset(hsum, 0.0), writes=["hsum"])
    S.op("pool", lambda e: e.memset(Cst[0:64], 0.0), writes=[f"C{i}" for i in range(8)])
    S.op("pool", lambda e: e.memset(vext[:, :, :, 128:130], 0.0), writes=["vext1"])
    S.op("pool", lambda e: e.memset(vext[:, :, :, 128:129], 1.0), writes=["vext1"])
    for h in range(4):
        S.dma(lambda e, h=h: e.dma_start(out=vext[:, :, h, 0:128],
                                         in_=scr["tmA"].rearrange("(j p) c -> p j c", p=128)[:, :, 256 + h * 128:256 + (h + 1) * 128]),
              writes=[f"vext0{h}"])
    S.dma(lambda e: e.dma_start(out=ktm, in_=scr["tmA"].rearrange("(j p) c -> p j c", p=128)[:, :, 0:256]), writes=["ktm"])
    for h in range(4):
        S.dma(lambda e, h=h: e.dma_start(out=kT[h][0:64, :], in_=scr["fmAk"][h * 64:(h + 1) * 64, :]), writes=[f"kT{h}"])
        S.dma(lambda e, h=h: e.dma_start(out=qst[0:64, :], in_=scr["fmAq"][h * 64:(h + 1) * 64, :]), writes=["qst"])
        S.op("pool", lambda e, h=h: e.memset(qTz[h][0:64], 0.0), writes=[f"qTz{h}"])
        q3 = qst[0:64, :].rearrange("p (j x) -> p j x", x=128)
        S.op("dve", lambda e, h=h, q3=q3: e.tensor_copy(out=qTz[h][0:64, :, 0, 0:64], in_=q3[:, :, 0:64]), reads=["qst"], writes=[f"qTz{h}"])
        S.op("pool", lambda e, h=h, q3=q3: e.tensor_copy(out=qTz[h][0:64, :, 1, 64:128], in_=q3[:, :, 64:128]), reads=["qst"], writes=[f"qTz{h}"])
    ve = [A.alloc([130], BF16) for _ in range(4)]
    sTm = [A.alloc([128], BF16) for _ in range(4)]
    Cdb = [A.alloc([130], BF16) for _ in range(4)]
    dn = [A.alloc([1], F32) for _ in range(4)]
    rc = [A.alloc([1], F32) for _ in range(4)]
    def mpipe(d, h, j, pis):
        r = h
        ci = d * 4 + h
        sb = ps[h][:, 0:128]
        Ub = ps[h][0:64, 256:386]
        Pb = ps[4 + h]
        kS = kU = f"ps{h}"
        kP = f"ps{4 + h}"
        eacol = sc_tm[:, j, d * 8 + h:d * 8 + h + 1]
        flcol = sc_tm[:, j, d * 8 + 4 + h:d * 8 + 4 + h + 1]
        S.op("act", lambda e: e.activation(out=ve[r], in_=vext[:, j, h, :], func=AF.Copy, scale=eacol),
             reads=[f"vext0{h}", "vext1", "sc_tm"], writes=[f"ve{r}"])
        for pi in range(2):
            S.op("pe", lambda e, pi=pi: e.matmul(sb, kT[h][0:64, j * 128:(j + 1) * 128], qTz[h][0:64, j, pi, :], start=(pi == 0), stop=(pi == 1)),
                 reads=[f"kT{h}", f"qTz{h}"], writes=[kS])
        yield
        S.op("dve", lambda e: e.tensor_tensor(out=sTm[r], in0=sb, in1=k.masks[:, d, :], op=ALU.mult), reads=[kS, "masks"], writes=[f"sTm{r}"])
        yield
        S.op("pe", lambda e: e.matmul(Pb[:, 0:130], sTm[r], ve[r], start=True, stop=False), reads=[f"sTm{r}", f"ve{r}"], writes=[kP])
        for n_, pi in enumerate(pis):
            c = 2 * j + pi
            dcol = dec_bc[0:64, d, h, c:c + 1]
            S.op("dve", lambda e, dcol=dcol: e.tensor_scalar(out=Cdb[r][0:64, :], in0=Cst[0:64, ci, :], scalar1=dcol, scalar2=None, op0=ALU.mult),
                 reads=[f"C{ci}", "dec_bc"], writes=[f"Cdb{r}"])
            yield
            S.op("pe", lambda e, pi=pi, n_=n_: e.matmul(Pb[:, 0:130], qTz[h][0:64, j, pi, :], Cdb[r][0:64, :], start=False, stop=(n_ == 1)),
                 reads=[f"qTz{h}", f"Cdb{r}"], writes=[kP])
            S.op("pe", lambda e, pi=pi: e.matmul(Ub, ktm[pi * 64:(pi + 1) * 64, j, h * 64:(h + 1) * 64], ve[r][pi * 64:(pi + 1) * 64, :],
                                                 start=True, stop=True), reads=["ktm", f"ve{r}"], writes=[kU])
            yield
            S.op("dve", lambda e, dcol=dcol: e.scalar_tensor_tensor(out=Cst[0:64, ci, :], in0=Cst[0:64, ci, :], scalar=dcol, in1=Ub,
                                                                  op0=ALU.mult, op1=ALU.add),
                 reads=[f"C{ci}", "dec_bc", kU], writes=[f"C{ci}"])
            yield
        S.op("act", lambda e: e.activation(out=dn[r], in_=Pb[:, 128:129], func=AF.Abs), reads=[kP], writes=[f"dn{r}"])
        yield
        S.op("dve", lambda e: e.tensor_tensor(out=dn[r], in0=dn[r], in1=flcol, op=ALU.max), reads=[f"dn{r}", "sc_tm"], writes=[f"dn{r}"])
        S.op("dve", lambda e: e.reciprocal(out=rc[r], in_=dn[r]), reads=[f"dn{r}"], writes=[f"rc{r}"])
        S.op("dve", lambda e: e.scalar_tensor_tensor(out=hsum[:, j, h * 128:(h + 1) * 128], in0=Pb[:, 0:128], scalar=rc[r],
                                                     in1=hsum[:, j, h * 128:(h + 1) * 128], op0=ALU.mult, op1=ALU.add),
             reads=[kP, f"rc{r}", f"hsum{j}", "hsum"], writes=[f"hsum{j}"])

    for step in range(NT):
        for d in range(2):
            j = (FWD_TILES if d == 0 else BWD_TILES)[step]
            pis = (0, 1) if d == 0 else (1, 0)
            alive = [mpipe(d, h, j, pis) for h in range(4)]
            while alive:
                nxt = []
                for g_ in alive:
                    try:
                        next(g_)
                        nxt.append(g_)
                    except StopIteration:
                        pass
                alive = nxt
    S.barrier(); S.emit(); A.release(m_hs)
    head_norm_out(k, hsum, "hsum", scr["fmAo"], inp["a_norm_w"][l:l + 1, :].rearrange("o (h p) -> p (o h)", p=128), 0, False)


def head_norm_out(k, hsum, hkey, gate_fm, nw_src, yidx, nw_shared):
    S, A, scr, ps = k.S, k.A, k.scr, k.ps
    oT = A.alloc([4, T], BF16)
    yaT = A.alloc([4, T], BF16)
    nw = A.alloc([4], F32)
    S.dma(lambda e: e.dma_start(out=oT, in_=gate_fm.rearrange("(h p) t -> p h t", p=128)), writes=["oT"])
    if nw_shared:
        for h in range(4):
            S.dma(lambda e, h=h: e.dma_start(out=nw[:, h:h + 1], in_=nw_src, allow_slow_non_contiguous=True), writes=[f"nw{h}"])
    else:
        S.dma(lambda e: e.dma_start(out=nw, in_=nw_src, allow_slow_non_contiguous=True), writes=["nw0"])
    nwk = [f"nw{h}" for h in range(4)] if nw_shared else ["nw0"]
    sq = [A.alloc([4, 128], F32) for _ in range(2)]
    ssq = A.alloc([NT, 4], F32)
    rr = A.alloc([NT, 4], F32)
    rs = A.alloc([NT, 4], F32)
    hn = [A.alloc([4, 128], BF16) for _ in range(2)]
    for j in range(NT):
        b = j % 2
        h3 = hsum[:, j, :].rearrange("p (h e) -> p h e", h=4)
        S.op("pool", lambda e, b=b, h3=h3: e.tensor_tensor(out=sq[b], in0=h3, in1=h3, op=ALU.mult), reads=[f"{hkey}{j}", hkey], writes=[f"hsq{b}"])
        S.op("dve", lambda e, b=b, j=j: e.tensor_reduce(out=ssq[:, j, :], in_=sq[b], axis=AX.X, op=ALU.add), reads=[f"hsq{b}"], writes=[f"hssq{j}"])
        S.op("act", lambda e, j=j: e.activation(out=rr[:, j, :], in_=ssq[:, j, :], func=AF.Sqrt, bias=k.eps_t, scale=1.0 / 128),
             reads=[f"hssq{j}", "eps_t"], writes=[f"hrr{j}"])
        S.op("dve", lambda e, j=j: e.reciprocal(out=rs[:, j, :], in_=rr[:, j, :]), reads=[f"hrr{j}"], writes=[f"hrs{j}"])
        S.op("dve", lambda e, j=j, b=b, h3=h3: e.tensor_tensor(out=hn[b], in0=h3, in1=rs[:, j, :].unsqueeze(2).to_broadcast([128, 4, 128]),
                                                            op=ALU.mult), reads=[f"{hkey}{j}", hkey, f"hrs{j}"], writes=[f"hn{b}"])
        pb = 4 + j % 2
        pst = ps[pb][:, :].bitcast(BF16)[:, 0:512].rearrange("p (a b) -> p a b", a=4)
        for h in range(4):
            S.op("pe", lambda e, h=h, b=b, pst=pst: e.transpose(pst[:, h, :], hn[b][:, h, :], k.ident_b), reads=[f"hn{b}", "ident_b"],
                 writes=[f"ps{pb}"])
        for h in range(4):
            S.op("dve", lambda e, h=h, j=j, pst=pst: e.scalar_tensor_tensor(
                out=yaT[:, h, j * 128:(j + 1) * 128], in0=pst[:, h, :], scalar=nw[:, h:h + 1], in1=oT[:, h, j * 128:(j + 1) * 128],
                op0=ALU.mult, op1=ALU.mult), reads=[f"ps{pb}", "oT"] + nwk, writes=[f"yaT{h}"])
    for h in range(4):
        S.dma(lambda e, h=h: e.dma_start(out=scr["fmY"][yidx, h * 128:(h + 1) * 128, :], in_=yaT[:, h, :]), reads=[f"yaT{h}"],
              writes=[uid(k, "d")])


MD = F32
GDN_STOP = None
GDN_LIM = 99


def phase_gdn(k, l):
    S, A, inp, scr, ps = k.S, k.A, k.inp, k.scr, k.ps
    osum = A.alloc([NT, 512], F32)
    m_os = A.mark()
    qT = [A.alloc([T], BF16) for _ in range(4)]
    kT = [A.alloc([T], BF16) for _ in range(4)]
    kvtm = A.alloc([NT, 8, 128], BF16)
    sc_tm = A.alloc([2, 7, 72], F32)
    egl_bc = A.alloc([2, 144], F32)
    m0 = A.mark()
    cwraw = A.alloc([3, 128], F32, parts=12)
    cw = A.alloc([3, 12], F32)
    S.dma(lambda e: e.dma_start(out=cwraw, in_=inp["c_conv_w"][l].rearrange("k (c p) -> c k p", p=128)), writes=["cwraw"])
    for kk in range(3):
        S.op("pe", lambda e, kk=kk: e.transpose(ps[7][:, 0:12], cwraw[0:12, kk, :], k.ident_f[0:12, 0:12]),
             reads=["cwraw", "ident_f"], writes=["ps7"])
        S.op("dve", lambda e, kk=kk: e.tensor_copy(out=cw[:, kk, :], in_=ps[7][:, 0:12]), reads=["ps7"], writes=["cw"])
    u = [A.alloc([T], BF16) for _ in range(2)]
    cv = [A.alloc([T], F32) for _ in range(2)]
    sv = cv
    sq = A.alloc([T], F32)
    rn = [A.alloc([512], F32) for _ in range(2)]
    vT = [A.alloc([T], BF16) for _ in range(4)]
    segs_l = [(1, 256), (257, T)]
    segs_r = [(0, 255), (256, T - 1)]
    pc = 0
    for ch in range(12):
        b = ch % 2
        kind, h = ch // 4, ch % 4
        S.dma(lambda e, ch=ch, b=b: e.dma_start(out=u[b], in_=scr["fmC"][ch * 128:(ch + 1) * 128, :]), writes=[f"u{b}"])
        S.op("act", lambda e, ch=ch, b=b: e.activation(out=cv[b], in_=u[b], func=AF.Identity, scale=cw[:, 1, ch:ch + 1]),
             reads=[f"u{b}", "cw"], writes=[f"cv{b}", f"sv{b}"])
        for (a0, a1) in segs_l:
            S.op("dve", lambda e, ch=ch, b=b, a0=a0, a1=a1: e.scalar_tensor_tensor(
                out=cv[b][:, a0:a1], in0=u[b][:, a0 - 1:a1 - 1], scalar=cw[:, 0, ch:ch + 1], in1=cv[b][:, a0:a1],
                op0=ALU.mult, op1=ALU.add), reads=[f"u{b}", "cw", f"cv{b}"], writes=[f"cv{b}"])
        for (a0, a1) in segs_r:
            S.op("dve", lambda e, ch=ch, b=b, a0=a0, a1=a1: e.scalar_tensor_tensor(
                out=cv[b][:, a0:a1], in0=u[b][:, a0 + 1:a1 + 1], scalar=cw[:, 2, ch:ch + 1], in1=cv[b][:, a0:a1],
                op0=ALU.mult, op1=ALU.add), reads=[f"u{b}", "cw", f"cv{b}"], writes=[f"cv{b}"])
        if kind == 2:
            S.op("act", lambda e, b=b, h=h: e.activation(out=vT[h], in_=cv[b], func=AF.Silu), reads=[f"cv{b}"], writes=[f"vT{h}"])
            continue
        S.op("act", lambda e, b=b: e.activation(out=sv[b], in_=cv[b], func=AF.Silu), reads=[f"cv{b}"], writes=[f"cv{b}", f"sv{b}"])
        S.op("act", lambda e, b=b: e.activation(out=sq, in_=sv[b], func=AF.Square), reads=[f"sv{b}"], writes=["sq"])
        for (t0, nt) in TBLK:
            pb = pc % 4
            rb = pc % 2
            pc += 1
            S.op("pe", lambda e, pb=pb, t0=t0, nt=nt: e.matmul(ps[pb][:, 0:nt], k.ones_f, sq[:, t0:t0 + nt], start=True, stop=True),
                 reads=["ones_f", "sq"], writes=[f"ps{pb}"])
            S.op("act", lambda e, pb=pb, rb=rb, nt=nt: e.activation(out=rn[rb][:, 0:nt], in_=ps[pb][:, 0:nt], func=AF.Sqrt, bias=k.eps_t, scale=1.0),
                 reads=[f"ps{pb}", "eps_t"], writes=[f"rn{rb}"])
            S.op("dve", lambda e, rb=rb, nt=nt: e.reciprocal(out=rn[rb][:, 0:nt], in_=rn[rb][:, 0:nt]), reads=[f"rn{rb}"], writes=[f"rn{rb}"])
            if kind == 0:
                S.op("dve", lambda e, b=b, rb=rb, t0=t0, nt=nt, h=h: e.scalar_tensor_tensor(
                    out=qT[h][:, t0:t0 + nt], in0=sv[b][:, t0:t0 + nt], scalar=float(128 ** -0.5), in1=rn[rb][:, 0:nt],
                    op0=ALU.mult, op1=ALU.mult), reads=[f"sv{b}", f"rn{rb}"], writes=[f"qT{h}"])
            else:
                S.op("dve", lambda e, b=b, rb=rb, t0=t0, nt=nt, h=h: e.tensor_tensor(
                    out=kT[h][:, t0:t0 + nt], in0=sv[b][:, t0:t0 + nt], in1=rn[rb][:, 0:nt], op=ALU.mult),
                    reads=[f"sv{b}", f"rn{rb}"], writes=[f"kT{h}"])
    for j in range(NT):
        pb = 4 + j % 2
        pst = ps[pb][:, :].bitcast(BF16).rearrange("p (a b) -> p a b", a=8)
        for h in range(4):
            S.op("pe", lambda e, h=h, j=j, pst=pst: e.transpose(pst[:, h, :], kT[h][:, j * 128:(j + 1) * 128], k.ident_b),
                 reads=[f"kT{h}", "ident_b"], writes=[f"ps{pb}"])
            S.op("pe", lambda e, h=h, j=j, pst=pst: e.transpose(pst[:, 4 + h, :], vT[h][:, j * 128:(j + 1) * 128], k.ident_b),
                 reads=[f"vT{h}", "ident_b"], writes=[f"ps{pb}"])
        S.op("act", lambda e, j=j, pst=pst: e.copy(out=kvtm[:, j], in_=pst), reads=[f"ps{pb}"], writes=["kvtm"])
    S.barrier(); S.emit(); A.release(m0)
    if GDN_STOP == "A":
        return
    cm = A.alloc([128], F32)
    S.op("pool", lambda e: e.memset(cm[0:72, :], 1.0), writes=["cmask"])
    S.op("pool", lambda e: e.memset(cm[0:72, :].rearrange("p (c l) -> p c l", l=64)[:, :, 0:1], 0.0), writes=["cmask"])
    wa = A.alloc([128], F32); wbt = A.alloc([128], F32); Gp = A.alloc([128], F32); Gd = A.alloc([128], F32)
    tmpg = A.alloc([128], F32); eG = A.alloc([128], F32)
    ghb = A.alloc([128], BF16); ghi = A.alloc([128], F32); glo = A.alloc([128], F32); nghi = A.alloc([128], F32); nglo = A.alloc([128], F32)
    dtb = A.alloc([2], F32); alog = A.alloc([2], F32); nega = A.alloc([2], F32)
    egc = A.alloc([2], F32); Z = A.alloc([72, 2], F32)
    for d in range(2):
        for h in range(4):
            S.dma(lambda e, d=d, h=h: e.dma_start(out=dtb[h * 18:(h + 1) * 18, d:d + 1],
                                                  in_=inp["c_dt_bias"][l:l + 1, d * 4 + h:d * 4 + h + 1].partition_broadcast(18)),
                  writes=[f"dtb{d}{h}"])
            S.dma(lambda e, d=d, h=h: e.dma_start(out=alog[h * 18:(h + 1) * 18, d:d + 1],
                                                  in_=inp["c_a_log"][l:l + 1, d * 4 + h:d * 4 + h + 1].partition_broadcast(18)),
                  writes=[f"alog{d}{h}"])
    S.op("act", lambda e: e.activation(out=nega[0:72, :], in_=alog[0:72, :], func=AF.Exp),
         reads=[f"alog{d}{h}" for d in range(2) for h in range(4)], writes=["nega"])
    S.op("dve", lambda e: e.tensor_scalar(out=nega[0:72, :], in0=nega[0:72, :], scalar1=-1.0, scalar2=None, op0=ALU.mult),
         reads=["nega"], writes=["nega"])
    dtbk = [f"dtb{d}{h}" for d in range(2) for h in range(4)]
    v3 = lambda ap: ap[0:72, :].rearrange("p (c l) -> p c l", l=64)
    for d in range(2):
        S.dma(lambda e, d=d: e.dma_start(out=wa[0:72, :], in_=scr["gC"][d * 4:d * 4 + 4, :].rearrange("h (j x) -> (h j) x", x=128)), writes=["wa"])
        S.dma(lambda e, d=d: e.dma_start(out=wbt[0:72, :], in_=scr["gC"][8 + d * 4:8 + d * 4 + 4, :].rearrange("h (j x) -> (h j) x", x=128)),
              writes=["wbt"])
        S.op("act", lambda e, d=d: e.activation(out=wa[0:72, :], in_=wa[0:72, :], func=AF.Exp, bias=dtb[0:72, d:d + 1], scale=1.0),
             reads=["wa"] + dtbk, writes=["wa"])
        S.op("act", lambda e: e.activation(out=wa[0:72, :], in_=wa[0:72, :], func=AF.Ln, bias=1.0), reads=["wa"], writes=["wa"])
        S.op("dve", lambda e, d=d: e.tensor_scalar(out=wa[0:72, :], in0=wa[0:72, :], scalar1=nega[0:72, d:d + 1], scalar2=None, op0=ALU.mult),
             reads=["wa", "nega"], writes=["wa"])
        S.op("dve", lambda e: e.tensor_tensor_scan(out=Gp[0:72, :], data0=cm[0:72, :], data1=wa[0:72, :], initial=0.0,
                                                   op0=ALU.mult, op1=ALU.add), reads=["cmask", "wa"], writes=["Gp"])
        Gtot = v3(Gp)[:, :, 63:64]
        if d == 0:
            S.op("dve", lambda e: e.tensor_copy(out=Gd[0:72, :], in_=Gp[0:72, :]), reads=["Gp"], writes=["Gd"])
        else:
            S.op("dve", lambda e, Gtot=Gtot: e.tensor_tensor(out=v3(tmpg), in0=Gtot.to_broadcast([72, 2, 64]), in1=v3(Gp), op=ALU.subtract),
                 reads=["Gp"], writes=["tmpg"])
            S.op("dve", lambda e: e.tensor_tensor(out=Gd[0:72, :], in0=tmpg[0:72, :], in1=wa[0:72, :], op=ALU.add),
                 reads=["tmpg", "wa"], writes=["Gd"])
        S.op("act", lambda e: e.activation(out=wbt[0:72, :], in_=wbt[0:72, :], func=AF.Sigmoid), reads=["wbt"], writes=["wbt"])
        S.op("act", lambda e: e.activation(out=eG[0:72, :], in_=Gd[0:72, :], func=AF.Exp), reads=["Gd"], writes=["eG"])
        S.op("dve", lambda e, Gtot=Gtot: e.tensor_tensor(out=v3(tmpg), in0=Gtot.to_broadcast([72, 2, 64]), in1=v3(Gd), op=ALU.subtract),
             reads=["Gp", "Gd"], writes=["tmpg"])
        S.op("act", lambda e: e.activation(out=tmpg[0:72, :], in_=tmpg[0:72, :], func=AF.Exp), reads=["tmpg"], writes=["tmpg"])
        S.op("act", lambda e, Gtot=Gtot: e.activation(out=egc[0:72, :].unsqueeze(2), in_=Gtot, func=AF.Exp), reads=["Gp"], writes=["egc"])
        S.op("dve", lambda e: e.tensor_copy(out=ghb[0:72, :], in_=wa[0:72, :]), reads=["wa"], writes=["ghb"])
        S.op("dve", lambda e: e.tensor_copy(out=ghi[0:72, :], in_=ghb[0:72, :]), reads=["ghb"], writes=["ghi"])
        S.op("dve", lambda e: e.tensor_tensor(out=glo[0:72, :], in0=wa[0:72, :], in1=ghi[0:72, :], op=ALU.subtract), reads=["wa", "ghi"], writes=["glo"])
        S.op("dve", lambda e: e.tensor_copy(out=ghb[0:72, :], in_=glo[0:72, :]), reads=["glo", "ghi"], writes=["ghb"])
        S.op("dve", lambda e: e.tensor_copy(out=glo[0:72, :], in_=ghb[0:72, :]), reads=["ghb"], writes=["glo"])
        S.op("dve", lambda e: e.tensor_scalar(out=nghi[0:72, :], in0=ghi[0:72, :], scalar1=-1.0, scalar2=None, op0=ALU.mult), reads=["ghi"], writes=["nghi"])
        S.op("dve", lambda e: e.tensor_scalar(out=nglo[0:72, :], in0=glo[0:72, :], scalar1=-1.0, scalar2=None, op0=ALU.mult), reads=["glo"], writes=["nglo"])
        pst = ps[d][:, 0:7 * 72].rearrange("p (w x) -> p w x", w=7)
        for w_, src, skey in ((0, wbt, "wbt"), (1, eG, "eG"), (2, tmpg, "tmpg"), (3, ghi, "ghi"), (4, glo, "glo"), (5, nghi, "nghi"), (6, nglo, "nglo")):
            S.op("pe", lambda e, w_=w_, src=src, pst=pst: e.transpose(pst[:, w_, :], src[0:72, :], k.ident_f[0:72, 0:72]),
                 reads=[skey, "ident_f"], writes=[f"ps{d}"])
        S.op("dve", lambda e, d=d, pst=pst: e.tensor_copy(out=sc_tm[:, d], in_=pst), reads=[f"ps{d}"], writes=["sc_tm"])
        S.op("dve", lambda e: e.tensor_tensor(out=Z[0:72], in0=egc[0:72, :].unsqueeze(1).to_broadcast([72, 72, 2]),
                                              in1=k.ident_f[0:72, 0:72].unsqueeze(2).to_broadcast([72, 72, 2]), op=ALU.mult),
             reads=["egc", "ident_f"], writes=["Z"])
        S.op("pe", lambda e, d=d: e.matmul(ps[2 + d][:, 0:144], k.ones_f[0:72, :], Z[0:72].rearrange("p x c -> p (x c)"),
                                           start=True, stop=True), reads=["Z", "ones_f"], writes=[f"ps{2 + d}"])
        S.op("act", lambda e, d=d: e.copy(out=egl_bc[:, d], in_=ps[2 + d][:, 0:144]), reads=[f"ps{2 + d}"], writes=["egl_bc"])
    S.barrier(); S.emit(); A.release(m0)
    if GDN_STOP == "A2":
        return
    Sf = A.alloc([8, 128], F32)
    Sb = A.alloc([8, 128], BF16)
    S.op("pool", lambda e: e.memset(osum, 0.0), writes=["osum"])
    S.op("pool", lambda e: e.memset(Sf, 0.0), writes=[f"Sf{i}" for i in range(8)])
    S.op("pool", lambda e: e.memset(Sb, 0.0), writes=[f"Sb{i}" for i in range(8)])
    NB = 4
    GT = [[A.alloc([128], BF16) for _ in range(4)] for _ in range(NB)]
    xd = [A.alloc([128], F32) for _ in range(NB)]
    DT = [A.alloc([128], F32) for _ in range(NB)]
    qkT = [A.alloc([128], BF16) for _ in range(NB)]
    tf = xd
    Pm = [[A.alloc([128], MD) for _ in range(6)] for _ in range(NB)]
    Qm = [[A.alloc([128], MD) for _ in range(5)] for _ in range(NB)]
    Yf = [A.alloc([256], F32) for _ in range(NB)]
    Yb = [A.alloc([256], MD) for _ in range(NB)] if MD == BF16 else Yf
    uw = [A.alloc([128], F32) for _ in range(NB)]
    wb_ = [A.alloc([128], BF16) for _ in range(NB)]
    wTz = [A.alloc([2, 128], BF16) for _ in range(NB)]
    vn = [A.alloc([128], BF16) for _ in range(NB)]
    kgl = [A.alloc([128], BF16) for _ in range(NB)]
    qkv = [A.alloc([128], F32) for _ in range(NB)]
    ot = qkv
    qz = [A.alloc([2, 128], BF16) for _ in range(NB)]
    for r in range(NB):
        S.op("pool", lambda e, r=r: e.memset(wTz[r], 0.0), writes=[f"wTz{r}"])
        S.op("pool", lambda e, r=r: e.memset(qz[r], 0.0), writes=[f"qz{r}"])
    identm = k.ident_b if MD == BF16 else k.ident_f
    it = 0
    ecount = [0]

    def evac(dst, dkey, src, skey):
        eng = "act" if ecount[0] % 2 == 0 else "dve"
        ecount[0] += 1
        if eng == "act":
            S.op("act", lambda e: e.copy(out=dst, in_=src), reads=[skey], writes=[dkey])
        else:
            S.op("dve", lambda e: e.tensor_copy(out=dst, in_=src), reads=[skey], writes=[dkey])

    def reg(bank, i, n=128):
        return bank[:, i * 128:i * 128 + n]

    def pipe(d, h, j, pis):
        r = h
        si = d * 4 + h
        X, Y = ps[h], ps[4 + h]
        kX, kY = f"ps{h}", f"ps{4 + h}"
        kTt = kT[h][:, j * 128:(j + 1) * 128]
        hj = h * 18 + j
        bcol = sc_tm[:, d, 0, hj:hj + 1]
        eGcol = sc_tm[:, d, 1, hj:hj + 1]
        eglcol = sc_tm[:, d, 2, hj:hj + 1]
        S.op("pool", lambda e: e.tensor_copy(out=qz[r][:, 0, 0:64], in_=qT[h][:, j * 128:j * 128 + 64]), reads=[f"qT{h}"], writes=[f"qz{r}"])
        S.op("pool", lambda e: e.tensor_copy(out=qz[r][:, 1, 64:128], in_=qT[h][:, j * 128 + 64:(j + 1) * 128]), reads=[f"qT{h}"], writes=[f"qz{r}"])
        for q_ in range(4):
            gc_ = sc_tm[:, d, 3 + q_, hj:hj + 1]
            if q_ % 2 == 0:
                S.op("act", lambda e, q_=q_, gc_=gc_: e.activation(out=GT[r][q_], in_=k.masks[:, d, :], func=AF.Copy, scale=gc_),
                     reads=["masks", "sc_tm"], writes=[f"GT{r}_{q_}"])
            else:
                S.op("dve", lambda e, q_=q_, gc_=gc_: e.tensor_scalar(out=GT[r][q_], in0=k.masks[:, d, :], scalar1=gc_, scalar2=None, op0=ALU.mult),
                     reads=["masks", "sc_tm"], writes=[f"GT{r}_{q_}"])
        yield
        S.op("pe", lambda e: e.matmul(reg(X, 0), kTt, kTt, start=True, stop=True), reads=[f"kT{h}"], writes=[kX])
        for pi in range(2):
            S.op("pe", lambda e, pi=pi: e.matmul(reg(X, 1), kTt, qz[r][:, pi, :], start=(pi == 0), stop=(pi == 1)),
                 reads=[f"kT{h}", f"qz{r}"], writes=[kX])
        for q_ in range(2):
            S.op("pe", lambda e, q_=q_: e.matmul(reg(X, 2), k.ones_b, GT[r][q_], start=(q_ == 0), stop=False),
                 reads=[f"GT{r}_{q_}", "ones_b"], writes=[kX])
        for q_ in range(2, 4):
            S.op("pe", lambda e, q_=q_: e.matmul(reg(X, 2), GT[r][q_], k.ones_b, start=False, stop=(q_ == 3)),
                 reads=[f"GT{r}_{q_}", "ones_b"], writes=[kX])
        yield
        S.op("dve", lambda e: e.tensor_tensor(out=xd[r], in0=reg(X, 2), in1=k.masks[:, 4 + d, :], op=ALU.add),
             reads=[kX, "masks"], writes=[f"xd{r}", f"tf{r}"])
        S.op("act", lambda e: e.activation(out=DT[r], in_=xd[r], func=AF.Exp), reads=[f"xd{r}"], writes=[f"DT{r}"])
        yield
        S.op("dve", lambda e: e.tensor_tensor(out=qkT[r], in0=reg(X, 1), in1=DT[r], op=ALU.mult), reads=[kX, f"DT{r}"], writes=[f"qkT{r}"])
        S.op("dve", lambda e: e.scalar_tensor_tensor(out=tf[r], in0=reg(X, 0), scalar=bcol, in1=DT[r], op0=ALU.mult, op1=ALU.mult),
             reads=[kX, "sc_tm", f"DT{r}"], writes=[f"tf{r}", f"xd{r}"])
        S.op("pool", lambda e: e.tensor_tensor(out=Pm[r][0], in0=tf[r], in1=k.masks[:, 2 + d, :], op=ALU.mult),
             reads=[f"tf{r}", "masks"], writes=[f"P{r}_0"])
        yield
        q1t = reg(X, 3).bitcast(BF16)[:, 0:128] if MD == BF16 else reg(X, 3)
        S.op("pe", lambda e: e.transpose(q1t, Pm[r][0], identm), reads=[f"P{r}_0", "ident_b", "ident_f"], writes=[kX])
        yield
        evac(Qm[r][0], f"Q{r}_0", q1t, kX)
        yield
        for i in range(1, 6):
            ra = i % 2
            S.op("pe", lambda e, i=i, ra=ra: e.matmul(reg(X, ra), Qm[r][i - 1], Pm[r][i - 1], start=True, stop=True),
                 reads=[f"Q{r}_{i - 1}", f"P{r}_{i - 1}"], writes=[kX])
            if i < 5:
                S.op("pe", lambda e, i=i, ra=ra: e.matmul(reg(Y, ra), Pm[r][i - 1], Qm[r][i - 1], start=True, stop=True),
                     reads=[f"Q{r}_{i - 1}", f"P{r}_{i - 1}"], writes=[kY])
            yield
            evac(Pm[r][i], f"P{r}_{i}", reg(X, ra), kX)
            if i < 5:
                evac(Qm[r][i], f"Q{r}_{i}", reg(Y, ra), kY)
            yield
        S.op("pool", lambda e: e.tensor_copy(out=Yf[r][:, 0:128], in_=kvtm[:, j, 4 + h, :]), reads=["kvtm"], writes=[f"Yf{r}a"])
        S.op("act", lambda e: e.activation(out=Yf[r][:, 128:256], in_=kvtm[:, j, h, :], func=AF.Copy, scale=eGcol),
             reads=["kvtm", "sc_tm"], writes=[f"Yf{r}b"])
        if MD == BF16:
            S.op("pool", lambda e: e.tensor_copy(out=Yb[r], in_=Yf[r]), reads=[f"Yf{r}a", f"Yf{r}b"], writes=[f"Yb{r}"])
        S.op("act", lambda e: e.activation(out=kgl[r], in_=kvtm[:, j, h, :], func=AF.Copy, scale=eglcol), reads=["kvtm", "sc_tm"], writes=[f"kgl{r}"])
        yield
        for n_, i in enumerate(range(5, -1, -1)):
            yr = n_ % 2
            S.op("pe", lambda e, i=i, yr=yr: e.matmul(Y[:, yr * 256:(yr + 1) * 256], Pm[r][i], Yb[r], start=True, stop=True),
                 reads=[f"P{r}_{i}", f"Yb{r}", f"Yf{r}a", f"Yf{r}b"], writes=[kY])
            yield
            S.op("dve", lambda e, yr=yr: e.tensor_tensor(out=Yf[r], in0=Y[:, yr * 256:(yr + 1) * 256], in1=Yf[r], op=ALU.add),
                 reads=[kY, f"Yf{r}a", f"Yf{r}b"], writes=[f"Yf{r}a", f"Yf{r}b"])
            if i > 0 and MD == BF16:
                S.op("act", lambda e: e.copy(out=Yb[r], in_=Yf[r]), reads=[f"Yf{r}a", f"Yf{r}b"], writes=[f"Yb{r}"])
            yield
        S.op("dve", lambda e: e.tensor_scalar(out=uw[r], in0=Yf[r][:, 0:128], scalar1=bcol, scalar2=None, op0=ALU.mult),
             reads=[f"Yf{r}a", f"Yf{r}b", "sc_tm"], writes=[f"uw{r}"])
        S.op("act", lambda e: e.activation(out=wb_[r], in_=Yf[r][:, 128:256], func=AF.Copy, scale=bcol),
             reads=[f"Yf{r}a", f"Yf{r}b", "sc_tm"], writes=[f"wb{r}"])
        yield
        wtt = reg(X, 3).bitcast(BF16)[:, 0:128]
        S.op("pe", lambda e: e.transpose(wtt, wb_[r], k.ident_b), reads=[f"wb{r}", "ident_b"], writes=[kX])
        yield
        S.op("act", lambda e: e.copy(out=wTz[r][:, 0, 0:64], in_=wtt[:, 0:64]), reads=[kX], writes=[f"wTz{r}"])
        S.op("dve", lambda e: e.tensor_copy(out=wTz[r][:, 1, 64:128], in_=wtt[:, 64:128]), reads=[kX], writes=[f"wTz{r}"])
        yield
        for n_, pi in enumerate(pis):
            R = slice(pi * 64, (pi + 1) * 64)
            S.op("pe", lambda e, pi=pi: e.matmul(reg(X, pi), wTz[r][:, pi, :], Sb[:, si, :], start=True, stop=True),
                 reads=[f"wTz{r}", f"Sb{si}"], writes=[kX])
            S.op("pe", lambda e, pi=pi, n_=n_: e.matmul(reg(Y, 0), qz[r][:, pi, :], Sb[:, si, :], start=(n_ == 0), stop=(n_ == 1)),
                 reads=[f"qz{r}", f"Sb{si}"], writes=[kY])
            yield
            S.op("dve", lambda e, pi=pi, R=R: e.tensor_tensor(out=vn[r][R, :], in0=uw[r][R, :], in1=reg(X, pi)[R, :], op=ALU.subtract),
                 reads=[f"uw{r}", kX], writes=[f"vn{r}"])
            yield
            S.op("pe", lambda e, R=R: e.matmul(reg(X, 2), kgl[r][R, :], vn[r][R, :], start=True, stop=True),
                 reads=[f"kgl{r}", f"vn{r}"], writes=[kX])
            yield
            gcol = egl_bc[:, d, hj * 2 + pi:hj * 2 + pi + 1]
            S.op("dve", lambda e, gcol=gcol: e.scalar_tensor_tensor(out=Sf[:, si, :], in0=Sf[:, si, :], scalar=gcol, in1=reg(X, 2),
                                                                  op0=ALU.mult, op1=ALU.add),
                 reads=[f"Sf{si}", "egl_bc", kX], writes=[f"Sf{si}"])
            S.op("act", lambda e: e.copy(out=Sb[:, si, :], in_=Sf[:, si, :]), reads=[f"Sf{si}"], writes=[f"Sb{si}"])
            yield
        S.op("pe", lambda e: e.matmul(reg(X, 3), qkT[r], vn[r], start=True, stop=True), reads=[f"qkT{r}", f"vn{r}"], writes=[kX])
        yield
        S.op("act", lambda e: e.copy(out=qkv[r], in_=reg(X, 3)), reads=[kX], writes=[f"qkv{r}", f"ot{r}"])
        S.op("dve", lambda e: e.scalar_tensor_tensor(out=ot[r], in0=reg(Y, 0), scalar=eGcol, in1=qkv[r], op0=ALU.mult, op1=ALU.add),
             reads=[kY, "sc_tm", f"qkv{r}"], writes=[f"ot{r}", f"qkv{r}"])
        S.op("pool", lambda e: e.tensor_tensor(out=osum[:, j, h * 128:(h + 1) * 128], in0=osum[:, j, h * 128:(h + 1) * 128], in1=ot[r], op=ALU.add),
             reads=[f"ot{r}", "osum", f"osum{j}"], writes=[f"osum{j}"])

    for step in range(NT):
        for d in range(2):
            j = (FWD_TILES if d == 0 else BWD_TILES)[step]
            pis = (0, 1) if d == 0 else (1, 0)
            alive = [pipe(d, h, j, pis) for h in range(4)]
            while alive:
                nxt = []
                for g_ in alive:
                    try:
                        next(g_)
                        nxt.append(g_)
                    except StopIteration:
                        pass
                alive = nxt
    S.barrier(); S.emit(); A.release(m_os)
    head_norm_out(k, osum, "osum", scr["fmCz"], inp["c_norm_w"][l:l + 1, :].rearrange("o p -> p o"), 2, True)


def residual_update(k, j, psrc, pkeys, gbc_r, gkey, ht, tmp, b):
    S, scr = k.S, k.scr
    S.dma(lambda e: e.dma_start(out=ht[b], in_=scr["hres"][j * 128:(j + 1) * 128, :]), reads=[f"hres{j}"], writes=[f"rht{b}"])
    for half in range(2):
        cs = slice(half * 512, (half + 1) * 512)
        S.op("dve", lambda e, half=half, cs=cs: e.tensor_tensor(out=tmp[half], in0=psrc[half][:, :], in1=gbc_r[:, cs], op=ALU.mult),
             reads=[pkeys[half], gkey], writes=[f"rtmp{half}"])
        S.op("dve", lambda e, half=half, cs=cs: e.tensor_tensor(out=ht[b][:, cs], in0=ht[b][:, cs], in1=tmp[half], op=ALU.add),
             reads=[f"rtmp{half}", f"rht{b}"], writes=[f"rht{b}"])
    S.dma(lambda e: e.dma_start(out=scr["hres"][j * 128:(j + 1) * 128, :], in_=ht[b]), reads=[f"rht{b}"],
          writes=[f"hres{j}"], q="pool")


def phase_merge(k, l):
    S, A, inp, scr, ps = k.S, k.A, k.inp, k.scr, k.ps
    wbr = A.alloc([3, 4, D], BF16)
    wo = A.alloc([8, D], BF16)
    stage = A.alloc([4, D], F32)
    for i in range(3):
        S.dma(lambda e, i=i: e.dma_start(out=stage, in_=inp["w_branch"][l, i].rearrange("(k p) n -> p k n", p=128)),
              writes=["stage"])
        S.op("dve" if i % 2 == 0 else "pool", lambda e, i=i: e.tensor_copy(out=wbr[:, i], in_=stage), reads=["stage"], writes=[f"wbr{i}"])
    for hf in range(2):
        S.dma(lambda e, hf=hf: e.dma_start(out=stage, in_=inp["w_out"][l].rearrange("(k p) n -> p k n", p=128)[:, hf * 4:(hf + 1) * 4, :]),
              writes=["stage"])
        S.op("dve" if hf == 0 else "pool", lambda e, hf=hf: e.tensor_copy(out=wo[:, hf * 4:(hf + 1) * 4, :], in_=stage),
             reads=["stage"], writes=[f"wo{hf}"])
    g1bc = [A.alloc([D], F32) for _ in range(2)]
    for r in range(2):
        load_bc(k, g1bc[r], scr["modrow"][l, r:r + 1, 2 * D:3 * D], f"g1bc{r}")
    yin = [[A.alloc([4, 512], BF16) for _ in range(3)] for _ in range(2)]
    gin = [[A.alloc([8, 512], BF16) for _ in range(3)] for _ in range(2)]
    yT = [A.alloc([8, 512], BF16) for _ in range(2)]
    tA = [A.alloc([512], F32) for _ in range(2)]
    tB = [A.alloc([512], F32) for _ in range(2)]
    tC = [A.alloc([512], F32) for _ in range(2)]
    ht = [A.alloc([D], F32) for _ in range(2)]
    tmp = [A.alloc([512], F32) for _ in range(2)]
    tcount = 0
    for bi, (t0, nt) in enumerate(TBLK):
        bb = bi % 2
        r = 1 if t0 < TCTX else 0
        for i in range(3):
            S.dma(lambda e, i=i, bb=bb, t0=t0, nt=nt: e.dma_start(
                out=yin[bb][i][:, :, 0:nt], in_=scr["fmY"][i].rearrange("(k p) t -> p k t", p=128)[:, :, t0:t0 + nt]),
                writes=[f"yin{bb}{i}"])
            S.dma(lambda e, i=i, bb=bb, t0=t0, nt=nt: e.dma_start(
                out=gin[bb][i][:, :, 0:nt],
                in_=scr["fmG"][i * D:(i + 1) * D, :].rearrange("(k p) t -> p k t", p=128)[:, :, t0:t0 + nt]),
                writes=[f"gin{bb}{i}"])
        for dc in range(8):
            pbase = 3 * (dc % 2)
            tb = dc % 2
            for i in range(3):
                for kc in range(4):
                    S.op("pe", lambda e, i=i, kc=kc, dc=dc, bb=bb, nt=nt, pbase=pbase: e.matmul(
                        ps[pbase + i][:, 0:nt], wbr[:, i, kc, dc * 128:(dc + 1) * 128], yin[bb][i][:, kc, 0:nt],
                        start=(kc == 0), stop=(kc == 3)),
                        reads=[f"wbr{i}", f"yin{bb}{i}"], writes=[f"ps{pbase + i}"])
            for i, tt in enumerate((tA, tB, tC)):
                S.op("dve", lambda e, i=i, tt=tt, dc=dc, bb=bb, nt=nt, pbase=pbase, tb=tb: e.tensor_tensor(
                    out=tt[tb][:, 0:nt], in0=ps[pbase + i][:, 0:nt], in1=gin[bb][i][:, dc, 0:nt], op=ALU.mult),
                    reads=[f"ps{pbase + i}", f"gin{bb}{i}"], writes=[f"t{i}{tb}"])
            S.op("dve", lambda e, tb=tb, nt=nt: e.tensor_tensor(out=tA[tb][:, 0:nt], in0=tA[tb][:, 0:nt], in1=tB[tb][:, 0:nt], op=ALU.add),
                 reads=[f"t0{tb}", f"t1{tb}"], writes=[f"t0{tb}"])
            S.op("dve", lambda e, tb=tb, nt=nt, dc=dc, bb=bb: e.tensor_tensor(out=yT[bb][:, dc, 0:nt], in0=tA[tb][:, 0:nt], in1=tC[tb][:, 0:nt], op=ALU.add),
                 reads=[f"t0{tb}", f"t2{tb}"], writes=[f"yT{bb}"])
        for ti in range(nt // 128):
            j = t0 // 128 + ti
            for half in range(2):
                for kc in range(8):
                    S.op("pe", lambda e, kc=kc, ti=ti, half=half, bb=bb: e.matmul(
                        ps[6 + half][:, :], yT[bb][:, kc, ti * 128:(ti + 1) * 128], wo[:, kc, half * 512:(half + 1) * 512],
                        start=(kc == 0), stop=(kc == 7)),
                        reads=[f"yT{bb}", "wo0", "wo1"], writes=[f"ps{6 + half}"])
            residual_update(k, j, [ps[6], ps[7]], ["ps6", "ps7"], g1bc[r], f"g1bc{r}", ht, tmp, tcount % 2)
            tcount += 1


def phase_ffn(k, l):
    S, A, inp, scr, ps = k.S, k.A, k.inp, k.scr, k.ps
    xnT = A.alloc([KC, T], BF16)
    norm_to_xnT(k, l, 3 * D, 4 * D, xnT)
    cwraw = A.alloc([3, 128], F32, parts=44)
    cw = A.alloc([3, 44], F32)
    S.dma(lambda e: e.dma_start(out=cwraw, in_=inp["ffn_conv_w"][l].rearrange("k (c p) -> c k p", p=128)), writes=["cwraw"])
    for kk in range(3):
        S.op("pe", lambda e, kk=kk: e.transpose(ps[7][:, 0:44], cwraw[0:44, kk, :], k.ident_f[0:44, 0:44]),
             reads=["cwraw", "ident_f"], writes=["ps7"])
        S.op("dve", lambda e, kk=kk: e.tensor_copy(out=cw[:, kk, :], in_=ps[7][:, 0:44]), reads=["ps7"], writes=["cw"])
    wst2 = [A.alloc([KC, 256], F32) for _ in range(2)]
    wbf2 = [A.alloc([KC, 256], BF16) for _ in range(2)]
    cbuf = [[A.alloc([T], F32) for _ in range(2)] for _ in range(2)]
    sg = A.alloc([T], F32)
    hmt = [A.alloc([T], BF16) for _ in range(2)]
    blocks = [(0, 256, 0, 256)]
    for i in range(5):
        s0 = 256 + i * 410
        blocks.append((s0, min(s0 + 410, T), 256, T))
    pcount = 0
    for jp in range(22):
        wb = jp % 2
        for w2, c0 in ((0, jp * 128), (1, DFF + jp * 128)):
            S.dma(lambda e, wb=wb, w2=w2, c0=c0: e.dma_start(
                out=wst2[wb][:, :, w2 * 128:(w2 + 1) * 128],
                in_=inp["w_up"][l].rearrange("(k p) n -> p k n", p=128)[:, :, c0:c0 + 128]), writes=[f"fwst{wb}{w2}"])
        S.op("pool", lambda e, wb=wb: e.tensor_copy(out=wbf2[wb], in_=wst2[wb]), reads=[f"fwst{wb}0", f"fwst{wb}1"], writes=[f"fwbf{wb}"])
        for (s, e_, q0, q1) in blocks:
            hl = 1 if s > q0 else 0
            hr = 1 if e_ < q1 else 0
            n = e_ - s
            ncol = n + hl + hr
            for w2 in range(2):
                chunk = jp + 22 * w2
                pb = pcount % 6
                pcount += 1
                cdst = cbuf[wb][w2]
                ckey = f"cbuf{wb}{w2}"
                for kc in range(KC):
                    S.op("pe", lambda e, kc=kc, wb=wb, w2=w2, pb=pb, s=s, hl=hl, ncol=ncol: e.matmul(
                        ps[pb][:, 0:ncol], wbf2[wb][:, kc, w2 * 128:(w2 + 1) * 128], xnT[:, kc, s - hl:s - hl + ncol],
                        start=(kc == 0), stop=(kc == KC - 1)),
                        reads=[f"fwbf{wb}"] + xn_keys(s - hl, ncol), writes=[f"ps{pb}"])
                S.op("act", lambda e, pb=pb, cdst=cdst, s=s, e_=e_, hl=hl, n=n, chunk=chunk: e.activation(
                    out=cdst[:, s:e_], in_=ps[pb][:, hl:hl + n], func=AF.Identity, scale=cw[:, 1, chunk:chunk + 1]),
                    reads=[f"ps{pb}", "cw"], writes=[ckey])
                ts = s + (1 - hl)
                S.op("dve", lambda e, pb=pb, cdst=cdst, s=s, e_=e_, hl=hl, ts=ts, chunk=chunk: e.scalar_tensor_tensor(
                    out=cdst[:, ts:e_], in0=ps[pb][:, ts - s + hl - 1:e_ - s + hl - 1], scalar=cw[:, 0, chunk:chunk + 1],
                    in1=cdst[:, ts:e_], op0=ALU.mult, op1=ALU.add),
                    reads=[f"ps{pb}", "cw", ckey], writes=[ckey])
                te = e_ - (1 - hr)
                S.op("dve", lambda e, pb=pb, cdst=cdst, s=s, hl=hl, te=te, chunk=chunk: e.scalar_tensor_tensor(
                    out=cdst[:, s:te], in0=ps[pb][:, hl + 1:hl + 1 + (te - s)], scalar=cw[:, 2, chunk:chunk + 1],
                    in1=cdst[:, s:te], op0=ALU.mult, op1=ALU.add),
                    reads=[f"ps{pb}", "cw", ckey], writes=[ckey])
        S.op("act", lambda e, wb=wb: e.activation(out=sg, in_=cbuf[wb][1], func=AF.Silu), reads=[f"cbuf{wb}1"], writes=["sg"])
        S.op("pool", lambda e, wb=wb: e.tensor_tensor(out=hmt[wb], in0=cbuf[wb][0], in1=sg, op=ALU.mult),
             reads=[f"cbuf{wb}0", "sg"], writes=[f"hmt{wb}"])
        S.dma(lambda e, wb=wb, jp=jp: e.dma_start(out=scr["hmid"][jp * 128:(jp + 1) * 128, :], in_=hmt[wb]),
              reads=[f"hmt{wb}"], writes=[uid(k, "d")], q="pool")
    S.barrier(); S.emit(); A.reset()
    wd = A.alloc([22, D], BF16)
    stage = A.alloc([4, D], F32)
    for pi in range(6):
        j0 = pi * 4
        nj = min(4, 22 - j0)
        S.dma(lambda e, j0=j0, nj=nj: e.dma_start(
            out=stage[:, 0:nj, :], in_=inp["w_down"][l].rearrange("(j p) n -> p j n", p=128)[:, j0:j0 + nj, :]), writes=["stage"])
        S.op("dve" if pi % 2 == 0 else "pool", lambda e, j0=j0, nj=nj: e.tensor_copy(out=wd[:, j0:j0 + nj, :], in_=stage[:, 0:nj, :]),
             reads=["stage"], writes=[f"wd{pi}"])
    wdkeys = [f"wd{pi}" for pi in range(6)]
    g2bc = [A.alloc([D], F32) for _ in range(2)]
    for r in range(2):
        load_bc(k, g2bc[r], scr["modrow"][l, r:r + 1, 5 * D:6 * D], f"g2bc{r}")
    hin = [A.alloc([22, 512], BF16) for _ in range(2)]
    ht = [A.alloc([D], F32) for _ in range(2)]
    tmp = [A.alloc([512], F32) for _ in range(2)]
    tcount = 0
    for bi, (t0, nt) in enumerate(TBLK):
        bb = bi % 2
        r = 1 if t0 < TCTX else 0
        S.dma(lambda e, bb=bb, t0=t0, nt=nt: e.dma_start(
            out=hin[bb][:, :, 0:nt], in_=scr["hmid"].rearrange("(j p) t -> p j t", p=128)[:, :, t0:t0 + nt]), writes=[f"hin{bb}"])
        for ti in range(nt // 128):
            j = t0 // 128 + ti
            pq = (tcount % 2) * 2
            for half in range(2):
                for jj in range(22):
                    S.op("pe", lambda e, jj=jj, ti=ti, half=half, bb=bb, pq=pq: e.matmul(
                        ps[pq + half][:, :], hin[bb][:, jj, ti * 128:(ti + 1) * 128], wd[:, jj, half * 512:(half + 1) * 512],
                        start=(jj == 0), stop=(jj == 21)),
                        reads=[f"hin{bb}"] + wdkeys, writes=[f"ps{pq + half}"])
            residual_update(k, j, [ps[pq], ps[pq + 1]], [f"ps{pq}", f"ps{pq + 1}"], g2bc[r], f"g2bc{r}", ht, tmp, tcount % 2)
            tcount += 1


def host_consts():
    ident = np.eye(128, dtype=np.float32)
    n_freq = 16
    inv = (10000.0 ** (-np.arange(n_freq, dtype=np.float32) / n_freq)).astype(np.float32)
    tok = np.arange(2048)
    row = (tok // 64).astype(np.float32)
    col = (tok % 64).astype(np.float32)
    ang = np.stack([row[:, None] * inv, col[:, None] * inv], axis=1).astype(np.float32)
    cos = np.cos(ang).astype(np.float32).reshape(16, 128, 32).transpose(1, 0, 2).reshape(128, 16 * 32)
    sin = np.sin(ang).astype(np.float32).reshape(16, 128, 32).transpose(1, 0, 2).reshape(128, 16 * 32)
    s = np.arange(128)[:, None]
    t = np.arange(128)[None, :]
    same = (s // 64) == (t // 64)
    NEG = -1000.0
    m = np.zeros((128, 6, 128), np.float32)
    m[:, 0] = (same & (s <= t))
    m[:, 1] = (same & (s >= t))
    m[:, 2] = -1.0 * (same & (s < t))
    m[:, 3] = -1.0 * (same & (s > t))
    m[:, 4] = np.where(same & (s <= t), 0.0, NEG)
    m[:, 5] = np.where(same & (s >= t), 0.0, NEG)
    return dict(k_ident=ident, k_cos=np.ascontiguousarray(cos), k_sin=np.ascontiguousarray(sin),
                k_masks=np.ascontiguousarray(m.reshape(128, 6 * 128)))


def make_in_maps(inputs):
    f = lambda a: np.ascontiguousarray(np.asarray(a, dtype=np.float32))
    consts = host_consts()
    shared = dict(
        c_ctx=f(inputs["c_ctx"]).reshape(1, D),
        norm1_w=f(inputs["norm1_w"]), norm2_w=f(inputs["norm2_w"]),
        ada_w=f(inputs["ada_w"]), ada_b=f(inputs["ada_b"]), w_in=f(inputs["w_in"]),
        a_gate_b=f(inputs["a_gate_b"]).reshape(DEPTH, 16), a_norm_w=f(inputs["a_norm_w"]),
        b_qnorm_w=f(inputs["b_qnorm_w"]), b_knorm_w=f(inputs["b_knorm_w"]),
        c_conv_w=f(inputs["c_conv_w"]), c_a_log=f(inputs["c_a_log"]).reshape(DEPTH, 8),
        c_dt_bias=f(inputs["c_dt_bias"]).reshape(DEPTH, 8), c_norm_w=f(inputs["c_norm_w"]),
        w_branch=f(inputs["w_branch"]), w_out=f(inputs["w_out"]), w_up=f(inputs["w_up"]),
        ffn_conv_w=f(inputs["ffn_conv_w"]), w_down=f(inputs["w_down"]), **consts)
    x = f(inputs["x"]); c = f(inputs["c"]); ctx = f(inputs["ctx"])
    maps = []
    for b in range(8):
        m = dict(shared)
        m["x"] = x[b]; m["ctx"] = ctx[b]; m["c"] = c[b].reshape(1, D)
        maps.append(m)
    return maps


def kernel(**inputs):
    nc, k = build_program()
    in_maps = make_in_maps(inputs)
    res = run_bass_kernel_spmd(nc, in_maps, core_ids=list(range(8)))
    return np.stack([np.asarray(r["out"], dtype=np.float32) for r in res.results], axis=0)
```
